# Optimizing a Trainium2 kernel written in Bass

```python
import math
import jax, jax.numpy as jnp
from jax import lax
import numpy as np

D_MODEL = 1024
BATCH = 2
SEQ = 8192
DEPTH = 1

SSM_GROUP = 16
SSM_GROUPS = D_MODEL // 32
SSM_WIDTH = SSM_GROUP * SSM_GROUPS
SSM_STATE = 64
DT_MIN = 1e-3
DT_MAX = 1e-1
FOX_HEAD_DIM = 64
FOX_HEADS = D_MODEL // 128
FOX_WIDTH = FOX_HEADS * FOX_HEAD_DIM
Q_BLOCK = 128
MEM_LEN = 256
MEM_HEADS = 4
MEM_HEAD_DIM = 128
MEM_WIDTH = MEM_HEADS * MEM_HEAD_DIM
FFN_HIDDEN = -(-8 * D_MODEL // (3 * 256)) * 256
N_BRANCHES = 2
RMS_EPS = 1e-6
SPLIT_Q = SSM_WIDTH
SPLIT_K = SPLIT_Q + FOX_WIDTH
SPLIT_V = SPLIT_K + FOX_WIDTH
SPLIT_F = SPLIT_V + FOX_WIDTH
SPLIT_G = SPLIT_F + FOX_HEADS
IN_WIDTH = SPLIT_G + N_BRANCHES * D_MODEL

kernel_name = "hybrid_s5_fox_gated_block"


def rms_norm(x, gain):
    xf = x.astype(jnp.float32)
    y = xf * lax.rsqrt(jnp.mean(xf * xf, axis=-1, keepdims=True) + RMS_EPS)
    return (y * gain.astype(jnp.float32)).astype(x.dtype)


def _linear_recurrence_op(left, right):
    a1, b1 = left
    a2, b2 = right
    return a1 * a2, a2 * b1 + b2


def s5_ssm(u, lam_re, lam_im, log_dt, b_re, b_im, c_re, c_im, d_skip):
    bsz, seq, _ = u.shape
    f32 = jnp.float32
    uf = u.astype(f32).reshape(bsz, seq, SSM_GROUPS, SSM_GROUP)
    lam = lax.complex(lam_re.astype(f32), lam_im.astype(f32))
    dt = jnp.exp(log_dt.astype(f32))[:, None]
    lam_bar = jnp.exp(lam * dt)
    b = lax.complex(b_re.astype(f32), b_im.astype(f32))
    b_bar = ((lam_bar - 1.0) / lam)[..., None] * b
    c = lax.complex(c_re.astype(f32), c_im.astype(f32))
    bu = jnp.einsum('gpn,blgn->blgp', b_bar, uf.astype(jnp.complex64))
    a = jnp.broadcast_to(lam_bar, bu.shape)
    _, states = lax.associative_scan(_linear_recurrence_op, (a, bu), axis=1)
    y = jnp.einsum('gnp,blgp->blgn', c, states).real
    y = y + d_skip.astype(f32).reshape(SSM_GROUPS, SSM_GROUP) * uf
    return y.reshape(bsz, seq, SSM_WIDTH).astype(u.dtype)


def forgetting_attention(q, k, v, f_logit):
    bsz, seq, n_heads, head_dim = q.shape
    log_f = jax.nn.log_sigmoid(f_logit.astype(jnp.float32))
    cum = jnp.cumsum(log_f, axis=1).transpose(0, 2, 1)
    kh = k.transpose(0, 2, 1, 3)
    vh = v.transpose(0, 2, 1, 3)
    n_blocks = seq // Q_BLOCK
    q_blocks = q.transpose(0, 2, 1, 3).reshape(bsz, n_heads, n_blocks, Q_BLOCK, head_dim).transpose(2, 0, 1, 3, 4)
    c_blocks = cum.reshape(bsz, n_heads, n_blocks, Q_BLOCK).transpose(2, 0, 1, 3)
    starts = jnp.arange(n_blocks, dtype=jnp.int32) * Q_BLOCK
    key_pos = jnp.arange(seq, dtype=jnp.int32)
    scale = head_dim ** -0.5

    def one_block(args):
        qb, cb, start = args
        s = jnp.einsum('bhqd,bhkd->bhqk', qb, kh).astype(jnp.float32) * scale
        s = s + cb[..., :, None] - cum[..., None, :]
        q_pos = start + jnp.arange(Q_BLOCK, dtype=jnp.int32)
        causal = key_pos[None, :] <= q_pos[:, None]
        s = jnp.where(causal, s, -jnp.inf)
        p = jax.nn.softmax(s, axis=-1)
        return jnp.einsum('bhqk,bhkd->bhqd', p.astype(vh.dtype), vh)

    out = lax.map(one_block, (q_blocks, c_blocks, starts))
    return out.transpose(1, 0, 3, 2, 4).reshape(bsz, seq, n_heads * head_dim)


def memory_cross_attention(n, m, w_q, w_kv, w_o):
    bsz, seq, _ = n.shape
    mem_len = m.shape[1]
    q = (n @ w_q).reshape(bsz, seq, MEM_HEADS, MEM_HEAD_DIM)
    k, v = jnp.split(m @ w_kv, 2, axis=-1)
    k = k.reshape(bsz, mem_len, MEM_HEADS, MEM_HEAD_DIM)
    v = v.reshape(bsz, mem_len, MEM_HEADS, MEM_HEAD_DIM)
    s = jnp.einsum('blhd,bmhd->bhlm', q, k).astype(jnp.float32) * (MEM_HEAD_DIM ** -0.5)
    p = jax.nn.softmax(s, axis=-1)
    o = jnp.einsum('bhlm,bmhd->blhd', p.astype(v.dtype), v).reshape(bsz, seq, MEM_WIDTH)
    return o @ w_o


def setup_inputs(seed: int = 0) -> dict:
    key = jax.random.key(seed)
    ks = jax.random.split(key, 32)
    f32 = jnp.float32
    L = DEPTH

    def nrm(k, shape, fan_in):
        return jax.random.normal(k, shape, f32) * (fan_in ** -0.5)

    def gain(k, shape):
        return 1.0 + 0.01 * jax.random.normal(k, shape, f32)

    n_idx = jnp.arange(SSM_STATE, dtype=f32)
    lam_re = -0.5 + 0.01 * jax.random.normal(ks[4], (L, SSM_GROUPS, SSM_STATE), f32)
    lam_im = math.pi * n_idx + 0.01 * jax.random.normal(ks[5], (L, SSM_GROUPS, SSM_STATE), f32)
    log_dt = jax.random.uniform(ks[6], (L, SSM_GROUPS), f32, math.log(DT_MIN), math.log(DT_MAX))
    return {
        "x": jax.random.normal(ks[0], (BATCH, SEQ, D_MODEL), f32),
        "mem": jax.random.normal(ks[1], (BATCH, MEM_LEN, D_MODEL), f32),
        "norm_mix": gain(ks[2], (L, D_MODEL)),
        "w_in": nrm(ks[3], (L, D_MODEL, IN_WIDTH), D_MODEL),
        "b_forget": jax.random.uniform(ks[7], (L, FOX_HEADS), f32, 1.0, 5.0),
        "lam_re": lam_re,
        "lam_im": lam_im,
        "log_dt": log_dt,
        "b_re": nrm(ks[8], (L, SSM_GROUPS, SSM_STATE, SSM_GROUP), 2 * SSM_GROUP),
        "b_im": nrm(ks[9], (L, SSM_GROUPS, SSM_STATE, SSM_GROUP), 2 * SSM_GROUP),
        "c_re": nrm(ks[10], (L, SSM_GROUPS, SSM_GROUP, SSM_STATE), SSM_STATE),
        "c_im": nrm(ks[11], (L, SSM_GROUPS, SSM_GROUP, SSM_STATE), SSM_STATE),
        "d_skip": jax.random.normal(ks[12], (L, SSM_WIDTH), f32),
        "w_glu": nrm(ks[13], (L, SSM_WIDTH, 2 * D_MODEL), SSM_WIDTH),
        "w_fox_o": nrm(ks[14], (L, FOX_WIDTH, D_MODEL), FOX_WIDTH),
        "w_mix_out": nrm(ks[15], (L, D_MODEL, D_MODEL), D_MODEL),
        "norm_mem_q": gain(ks[16], (L, D_MODEL)),
        "norm_mem_kv": gain(ks[17], (L, D_MODEL)),
        "w_mem_q": nrm(ks[18], (L, D_MODEL, MEM_WIDTH), D_MODEL),
        "w_mem_kv": nrm(ks[19], (L, D_MODEL, 2 * MEM_WIDTH), D_MODEL),
        "w_mem_o": nrm(ks[20], (L, MEM_WIDTH, D_MODEL), MEM_WIDTH),
        "norm_ffn": gain(ks[21], (L, D_MODEL)),
        "w_ffn_in": nrm(ks[22], (L, D_MODEL, 2 * FFN_HIDDEN), D_MODEL),
        "w_ffn_out": nrm(ks[23], (L, FFN_HIDDEN, D_MODEL), FFN_HIDDEN),
        "norm_final": gain(ks[24], (D_MODEL,)),
    }


def reference(x, mem, norm_mix, w_in, b_forget, lam_re, lam_im, log_dt, b_re, b_im, c_re, c_im,
              d_skip, w_glu, w_fox_o, w_mix_out, norm_mem_q, norm_mem_kv, w_mem_q, w_mem_kv,
              w_mem_o, norm_ffn, w_ffn_in, w_ffn_out, norm_final):
    bsz, seq, _ = x.shape
    h = x
    for l in range(DEPTH):
        u = rms_norm(h, norm_mix[l])
        proj = u @ w_in[l]
        u_ssm, q, k, v, f_logit, gate_logits = jnp.split(
            proj, [SPLIT_Q, SPLIT_K, SPLIT_V, SPLIT_F, SPLIT_G], axis=-1)
        y_ssm = jax.nn.gelu(s5_ssm(u_ssm, lam_re[l], lam_im[l], log_dt[l], b_re[l], b_im[l],
                                   c_re[l], c_im[l], d_skip[l]))
        glu_a, glu_b = jnp.split(y_ssm @ w_glu[l], 2, axis=-1)
        out_a = glu_a * jax.nn.sigmoid(glu_b)
        att = forgetting_attention(
            q.reshape(bsz, seq, FOX_HEADS, FOX_HEAD_DIM),
            k.reshape(bsz, seq, FOX_HEADS, FOX_HEAD_DIM),
            v.reshape(bsz, seq, FOX_HEADS, FOX_HEAD_DIM),
            f_logit + b_forget[l])
        out_b = att @ w_fox_o[l]
        gate_a, gate_b = jnp.split(jax.nn.sigmoid(gate_logits), 2, axis=-1)
        h = h + (gate_a * out_a + gate_b * out_b) @ w_mix_out[l]
        h = h + memory_cross_attention(rms_norm(h, norm_mem_q[l]), rms_norm(mem, norm_mem_kv[l]),
                                       w_mem_q[l], w_mem_kv[l], w_mem_o[l])
        f_a, f_b = jnp.split(rms_norm(h, norm_ffn[l]) @ w_ffn_in[l], 2, axis=-1)
        h = h + (jax.nn.silu(f_a) * f_b) @ w_ffn_out[l]
    return rms_norm(h, norm_final)
```

```python
import contextlib
import os
import math
import numpy as np
import concourse.bass as bass
import concourse.mybir as mybir
from concourse.bass_utils import run_bass_kernel_spmd

F32 = mybir.dt.float32
BF16 = mybir.dt.bfloat16
I32 = mybir.dt.int32
AF = mybir.ActivationFunctionType
ALU = mybir.AluOpType

D = 1024
NB = 64
NT = NB * 128
NOWN = 2048
EPS = 1e-6
NEG = -30000.0
TWO_PI = 2.0 * math.pi


class _Stop(Exception):
    pass


class Res:
    __slots__ = ("w", "r", "dsem", "dcnt", "name")

    def __init__(self, name=""):
        self.w = None
        self.r = {}
        self.dsem = None
        self.dcnt = 0
        self.name = name


class Ker:
    def __init__(self, nc, es, needed=None):
        self.nc = nc
        self.es = es
        self.needed = needed
        self.rec = set()
        self.pcnt = {}
        self.pmap = {}
        self.eng = {"pe": nc.tensor, "act": nc.scalar, "dve": nc.vector, "pool": nc.gpsimd, "sp": nc.sync}
        self.sem = {}
        self.cnt = {}
        for e in ("pe", "act", "dve", "pool"):
            self.sem[e] = es.enter_context(nc.semaphore("s_" + e))
            self.cnt[e] = 0
            self.pcnt[e] = 0
        self.waited = {e: {} for e in self.eng}
        self.free_d = []
        self.phase_res = []
        self.ndsem = 0

    def new_dsem(self):
        if self.free_d:
            return self.free_d.pop()
        s = self.es.enter_context(self.nc.semaphore("d%d" % self.ndsem))
        key = "d%d" % self.ndsem
        self.ndsem += 1
        self.sem[key] = s
        self.cnt[key] = 0
        return key

    def _need(self, e, tok, needs):
        if tok is None:
            return
        k, v = tok
        if k == "pe" and e == "pe":
            return
        if needs.get(k, 0) < v:
            needs[k] = v

    def _waits(self, e, reads, writes, skip_key=None):
        needs = {}
        for r in reads:
            self._need(e, r.w, needs)
            for k, v in r.r.items():
                if k != e:
                    self._need(e, (k, v), needs)
        for r in writes:
            self._need(e, r.w, needs)
            for k, v in r.r.items():
                self._need(e, (k, v), needs)
        wd = self.waited[e]
        for k, v in needs.items():
            if k == skip_key:
                continue
            if wd.get(k, 0) < v:
                self._emit_wait(e, k, v)
                wd[k] = v

    def _emit_wait(self, e, k, v):
        if k in self.pcnt:
            self.rec.add((k, v))
            pv = v if self.needed is None else self.pmap[(k, v)]
        else:
            pv = v
        self.eng[e].wait_ge(self.sem[k], pv)

    def op(self, e, fn, reads=(), writes=(), signal=True):
        self._waits(e, reads, writes)
        ins = fn(self.eng[e])
        if signal:
            self.cnt[e] += 1
            if self.needed is None or (e, self.cnt[e]) in self.needed:
                self.pcnt[e] += 1
                self.pmap[(e, self.cnt[e])] = self.pcnt[e]
                ins.then_inc(self.sem[e], 1)
            tok = (e, self.cnt[e])
        else:
            tok = (e, self.cnt[e] + 1)
        for r in writes:
            r.w = tok
            r.r = {}
        for r in reads:
            if r.r.get(tok[0], 0) < tok[1]:
                r.r[tok[0]] = tok[1]
        return tok

    def dma(self, q, out, in_, reads=(), writes=(), dres=None):
        if dres.dsem is None:
            dres.dsem = self.new_dsem()
            self.phase_res.append(dres)
        k = dres.dsem
        self._waits(q, reads, writes, skip_key=k)
        self.eng[q].dma_start(out=out, in_=in_).then_inc(self.sem[k], 16)
        self.cnt[k] += 16
        tok = (k, self.cnt[k])
        for r in writes:
            r.w = tok
            r.r = {}
        for r in reads:
            if r.r.get(tok[0], 0) < tok[1]:
                r.r[tok[0]] = tok[1]
        return tok

    def barrier(self):
        for e in self.eng:
            wd = self.waited[e]
            for k, v in self.cnt.items():
                if v > 0 and wd.get(k, 0) < v:
                    self._emit_wait(e, k, v)
                    wd[k] = v
        for r in self.phase_res:
            self.free_d.append(r.dsem)
            r.dsem = None
        self.phase_res = []


def build(stage=9, debug=False):
    _, rec = _build(stage, debug, None)
    nc, _ = _build(stage, debug, rec)
    return nc


def _build(stage, debug, needed):
    nc = bass.Bass("TRN2", target_bir_lowering=False)

    def din(name, shape, dt=F32):
        return nc.dram_tensor(name, list(shape), dt, kind="ExternalInput").ap()

    xp = din("xp", [NT, D])
    padrow = din("padrow", [1, NT])
    mem = din("mem", [256, D])
    w_in = din("w_in", [D, 4104])
    g_mix = din("g_mix", [1, D]); g_memq = din("g_memq", [1, D]); g_memkv = din("g_memkv", [1, D])
    g_ffn = din("g_ffn", [1, D]); g_fin = din("g_fin", [1, D])
    b_forget = din("b_forget", [1, 8])
    lamr_c = din("lamr_c", [96, 768]); lami_c = din("lami_c", [96, 768]); ldt_c = din("ldt_c", [96, 768])
    Br_c = din("Br_c", [96, 768]); Bi_c = din("Bi_c", [96, 768])
    lamr_r = din("lamr_r", [128, 16]); lami_r = din("lami_r", [128, 16]); ldt_r = din("ldt_r", [128, 16])
    Cr_r = din("Cr_r", [128, 512]); Ci_r = din("Ci_r", [128, 512])
    Dm = din("Dm", [96, 192])
    Brow_r = din("Brow_r", [128, 512]); Brow_i = din("Brow_i", [128, 512])
    w_glu = din("w_glu", [512, 2048]); w_fox_o = din("w_fox_o", [512, D]); w_mix = din("w_mix", [D, D])
    w_mem_q = din("w_mem_q", [D, 512]); w_mem_kv = din("w_mem_kv", [D, D]); w_mem_o = din("w_mem_o", [512, D])
    w_ffn_in = din("w_ffn_in", [D, 5632]); w_ffn_out = din("w_ffn_out", [2816, D])
    c_idx = din("c_idx", [128, 512]); c_tau = din("c_tau", [128, 128]); c_reset = din("c_reset", [128, 512])
    c_ident = din("c_ident", [128, 128]); c_triu = din("c_triu", [128, 128]); c_e127 = din("c_e127", [128, 128])
    c_tmask = din("c_tmask", [128, 128]); c_sel = din("c_sel", [128, 128])

    out = nc.dram_tensor("out", [NOWN, D], F32, kind="ExternalOutput").ap()

    def dscr(name, shape, dt=BF16):
        return nc.dram_tensor(name, list(shape), dt, kind="ExternalOutput" if debug else "Internal").ap()

    kT = dscr("kT", [8, 71, NT])
    qT = dscr("qT", [8, 71, NOWN])
    vA = dscr("vA", [8, NT, 128])
    usT = dscr("usT", [512, 16, 512])
    uTo = dscr("uTo", [D, NOWN])

    with contextlib.ExitStack() as es:
        K = Ker(nc, es, needed)
        dbg = {}
        try:

            uid = [0]

            def sb(st, name, shape, dt=F32):
                uid[0] += 1
                return st.enter_context(nc.sbuf_tensor("%s_%d" % (name, uid[0]), list(shape), dt))

            def ps(st, name, shape, dt=F32):
                uid[0] += 1
                return st.enter_context(nc.psum_tensor("%s_%d" % (name, uid[0]), list(shape), dt))

            identf = sb(es, "identf", [128, 128]); identb = sb(es, "identb", [128, 128], BF16)
            triu = sb(es, "triu", [128, 128]); e127 = sb(es, "e127", [128, 128])
            tmaskb = sb(es, "tmaskb", [128, 128], BF16)
            self_ = sb(es, "self_", [128, 128])
            onesb = sb(es, "onesb", [128, 128], BF16)
            halfpi = sb(es, "halfpi", [128, 1])
            epsc = sb(es, "epsc", [128, 1])
            yfm = sb(es, "yfm", [128, 4, NOWN], BF16)
            r_const = Res("const")
            r_yfm = Res("yfm"); r_att = Res("att")
            r_scr = {n: Res(n) for n in ("kT", "qT", "vA", "usT", "uTo")}

            ld = Res("ldc")
            for t_, src in ((identf, c_ident), (triu, c_triu), (e127, c_e127), (self_, c_sel)):
                K.dma("sp", t_[:], src[:, :], writes=[r_const], dres=ld)
            tmaskf = sb(es, "tmaskf", [128, 128])
            K.dma("sp", tmaskf[:], c_tmask[:, :], writes=[r_const], dres=ld)
            K.op("dve", lambda e: e.tensor_copy(out=identb[:], in_=identf[:]), reads=[r_const], writes=[Res()])
            K.op("dve", lambda e: e.tensor_copy(out=tmaskb[:], in_=tmaskf[:]), reads=[r_const], writes=[Res()])
            K.op("dve", lambda e: e.memset(onesb[:], 1.0), writes=[Res()])
            K.op("dve", lambda e: e.memset(halfpi[:], math.pi / 2), writes=[Res()])
            K.op("dve", lambda e: e.memset(epsc[:], EPS), writes=[Res()])
            K.barrier()

            NWST = 4
            wst = [sb(es, "wst%d" % i, [128, 1024]) for i in range(NWST)]
            r_wst = [Res("wst%d" % i) for i in range(NWST)]
            wst_cnt = [0]

            def cast_load(dst_fn, r_dst, src, rows0, ncols, c0, cast_eng="rr"):
                cc = 0
                while cc < ncols:
                    n = min(1024, ncols - cc)
                    si = wst_cnt[0] % NWST
                    wst_cnt[0] += 1
                    K.dma("sp", wst[si][:, 0:n], src[rows0:rows0 + 128, c0 + cc:c0 + cc + n], writes=[r_wst[si]], dres=r_wst[si])
                    dst = dst_fn(cc, n)
                    ce = ("pool", "act", "dve")[wst_cnt[0] % 3] if cast_eng == "rr" else cast_eng
                    if ce == "act":
                        K.op("act", lambda e, dst=dst, si=si, n=n: e.activation(out=dst, in_=wst[si][:, 0:n], func=AF.Copy), reads=[r_wst[si]], writes=[r_dst])
                    else:
                        K.op(ce, lambda e, dst=dst, si=si, n=n: e.tensor_copy(out=dst, in_=wst[si][:, 0:n]), reads=[r_wst[si]], writes=[r_dst])
                    cc += n

            evac_flip = [0]

            def evac(out_ap, in_ap, reads, writes, scale=None, eng=None):
                if eng is None:
                    eng = "act" if evac_flip[0] % 2 == 0 else "dve"
                    evac_flip[0] += 1
                if eng == "act":
                    if scale is None:
                        return K.op("act", lambda e: e.activation(out=out_ap, in_=in_ap, func=AF.Copy), reads=reads, writes=writes)
                    return K.op("act", lambda e: e.activation(out=out_ap, in_=in_ap, func=AF.Copy, scale=scale), reads=reads, writes=writes)
                if scale is None:
                    return K.op("dve", lambda e: e.tensor_copy(out=out_ap, in_=in_ap), reads=reads, writes=writes)
                return K.op("dve", lambda e: e.tensor_scalar(out=out_ap, in0=in_ap, scalar1=scale, scalar2=None, op0=ALU.mult), reads=reads, writes=writes)

            def rmsnorm_tok(st_tag, x_ap, r_x, gain_t, r_gain, out_bf, r_out, junk, r_junk, ss, r_ss, rstd, r_rstd):
                K.op("act", lambda e: e.activation(out=junk, in_=x_ap, func=AF.Square, accum_out=ss), reads=[r_x], writes=[r_junk, r_ss])
                K.op("dve", lambda e: e.tensor_scalar(out=rstd, in0=ss, scalar1=1.0 / D, scalar2=EPS, op0=ALU.mult, op1=ALU.add), reads=[r_ss], writes=[r_rstd])
                K.op("act", lambda e: e.activation(out=rstd, in_=rstd, func=AF.Sqrt), reads=[r_rstd], writes=[r_rstd])
                K.op("dve", lambda e: e.reciprocal(out=rstd, in_=rstd), reads=[r_rstd], writes=[r_rstd])
                K.op("dve", lambda e: e.scalar_tensor_tensor(out=out_bf, in0=x_ap, scalar=rstd, in1=gain_t, op0=ALU.mult, op1=ALU.mult),
                     reads=[r_x, r_rstd, r_gain], writes=[r_out])

            F_all = sb(es, "F_all", [128, 8, NB])
            wb = {}
            r_F = Res("F")
            with contextlib.ExitStack() as p1:
                W1 = sb(p1, "W1", [128, 8, 2056], BF16)
                r_W1 = Res("W1")
                for k in range(8):
                    cast_load(lambda cc, n, k=k: W1[:, k, cc:cc + n], r_W1, w_in, k * 128, 2056, 0)
                gmix = sb(p1, "gmix", [128, D]); r_g = Res("g")
                K.dma("sp", gmix[:], g_mix[0:1, :].partition_broadcast(128), writes=[r_g], dres=r_g)
                bfg = sb(p1, "bfg", [128, 8])
                K.dma("sp", bfg[:], b_forget[0:1, :].partition_broadcast(128), writes=[r_g], dres=r_g)
                onesrow = sb(p1, "onesrow", [3, NOWN], BF16); r_or = Res("or")
                K.op("dve", lambda e: e.memset(onesrow[:], 1.0), writes=[r_or])
                zrow = sb(p1, "zrow", [1, NOWN], BF16)
                K.op("dve", lambda e: e.memset(zrow[:], 0.0), writes=[r_or])
                padb = sb(p1, "padb", [1, 512], BF16); r_pb = Res("pb")
                padf = sb(p1, "padf", [1, 512]); r_pf_ = Res("pf_")
                K.dma("sp", padf[:], padrow[0:1, 0:512], writes=[r_pf_], dres=r_pf_)
                K.op("dve", lambda e: e.tensor_copy(out=padb[:], in_=padf[:]), reads=[r_pf_], writes=[r_pb])
                st_aug = Res("st_aug")
                for h in range(8):
                    for c4 in range(4):
                        K.dma("sp", kT[h, 67:70, c4 * NOWN:(c4 + 1) * NOWN], onesrow[:], reads=[r_or], dres=st_aug)
                    K.dma("sp", kT[h, 70:71, 0:512], padb[:], reads=[r_pb], dres=st_aug)
                    for c0_, c1_ in ((512, 2560), (2560, 4608), (4608, 6656), (6656, 8192)):
                        K.dma("sp", kT[h, 70:71, c0_:c1_], zrow[:, 0:c1_ - c0_], reads=[r_or], dres=st_aug)
                    K.dma("sp", qT[h, 64:67, :], onesrow[:], reads=[r_or], dres=st_aug)
                    K.dma("sp", qT[h, 70:71, :], onesrow[0:1, :], reads=[r_or], dres=st_aug)

                NXB = 6
                xt = [sb(p1, "xt%d" % i, [128, D]) for i in range(NXB)]
                r_xt = [Res("xt%d" % i) for i in range(NXB)]
                junk = sb(p1, "junk", [128, D], BF16); r_junk = Res("junk")
                ss = [sb(p1, "ss%d" % i, [128, 1]) for i in range(4)]; r_ss = [Res() for _ in range(4)]
                rstd = [sb(p1, "rstd%d" % i, [128, 1]) for i in range(4)]; r_rstd = [Res() for _ in range(4)]
                ub = [sb(p1, "ub%d" % i, [128, D], BF16) for i in range(4)]; r_ub = [Res() for _ in range(4)]
                uT = [sb(p1, "uT%d" % i, [128, 8, 512], BF16) for i in range(2)]; r_uT = [Res() for _ in range(2)]
                kst = [sb(p1, "kst%d" % i, [128, 512], BF16) for i in range(4)]; r_kst = [Res() for _ in range(4)]
                vst = [sb(p1, "vst%d" % i, [128, 8, 128], BF16) for i in range(3)]; r_vst = [Res() for _ in range(3)]
                qst = [sb(p1, "qst%d" % i, [128, 128], BF16) for i in range(2)]; r_qst = [Res() for _ in range(2)]
                usd = [sb(p1, "usd%d" % i, [128, 16, 128], BF16) for i in range(4)]; r_usd = [Res() for _ in range(4)]
                ptr = [ps(p1, "ptr%d" % i, [128, 1024], BF16) for i in range(2)]; r_ptr = [Res() for _ in range(2)]
                pk = [ps(p1, "pk%d" % i, [128, 512]) for i in range(3)]; r_pk = [Res() for _ in range(3)]
                pv = [ps(p1, "pv%d" % i, [128, 512]) for i in range(2)]; r_pv = [Res() for _ in range(2)]
                pf = ps(p1, "pf", [128, 8]); r_pf = Res()
                for i in range(3):
                    v_ = vst[i]
                    K.op("pool", lambda e, v_=v_: e.memset(v_[:], 0.0), writes=[r_vst[i]])
                    K.op("pool", lambda e, v_=v_: e.memset(v_[:, 0:8:2, 64:65], 1.0), writes=[r_vst[i]])
                    K.op("pool", lambda e, v_=v_: e.memset(v_[:, 1:8:2, 0:1], 1.0), writes=[r_vst[i]])

                def load_x(gb):
                    i = gb % NXB
                    K.dma("sp", xt[i][:], xp[gb * 128:(gb + 1) * 128, :], writes=[r_xt[i]], dres=r_xt[i])

                for gb in range(NXB):
                    load_x(gb)
                kcount = [0]
                vcount = [0]

                def stage_N(s):
                    for blk in range(4):
                        gb = 4 * s + blk
                        xi = gb % NXB
                        bi = gb % 4
                        rmsnorm_tok("p1", xt[xi][:], r_xt[xi], gmix[:], r_g, ub[bi][:], r_ub[bi], junk[:], r_junk,
                                    ss[bi][:], r_ss[bi], rstd[bi][:], r_rstd[bi])
                        if gb + NXB < NB:
                            load_x(gb + NXB)

                def stage_T(s):
                    ui = s % 2
                    for blk in range(4):
                        gb = 4 * s + blk
                        bi = gb % 4
                        pi_ = gb % 2
                        pt = ptr[pi_]
                        for k in range(8):
                            K.op("pe", lambda e, k=k, pt=pt, bi=bi: e.transpose(out=pt[:, k * 128:(k + 1) * 128], in_=ub[bi][:, k * 128:(k + 1) * 128], identity=identb[:]),
                                 reads=[r_ub[bi], r_const], writes=[r_ptr[pi_]], signal=(k == 7))
                        evac(uT[ui][:, :, blk * 128:(blk + 1) * 128], pt[:].rearrange("p (k t) -> p k t", k=8), [r_ptr[pi_]], [r_uT[ui]])

                def stage_M(s):
                    ui = s % 2
                    for blk in range(4):
                        gb = 4 * s + blk
                        pvi = vcount[0] % 2
                        for k in range(8):
                            K.op("pe", lambda e, k=k, pvi=pvi, blk=blk: e.matmul(pv[pvi][:], lhsT=uT[ui][:, k, blk * 128:(blk + 1) * 128], rhs=W1[:, k, 1536:2048], start=(k == 0), stop=(k == 7)),
                                 reads=[r_uT[ui], r_W1], writes=[r_pv[pvi]], signal=(k == 7))
                        vi = vcount[0] % 3
                        vcount[0] += 1
                        pvv = pv[pvi][:].rearrange("p (hp e d) -> p hp e d", hp=4, e=2)
                        vsv = vst[vi][:].rearrange("p (hp e) c -> p hp e c", e=2)
                        evac(vsv[:, :, 0, 0:64], pvv[:, :, 0, :], [r_pv[pvi]], [r_vst[vi]], eng="act")
                        evac(vsv[:, :, 1, 64:128], pvv[:, :, 1, :], [r_pv[pvi]], [r_vst[vi]], eng="dve")
                        K.dma("sp", vA[:, gb * 128:(gb + 1) * 128, :].rearrange("h t c -> t h c"), vst[vi][:], reads=[r_vst[vi]], dres=r_vst[vi])
                        for k in range(8):
                            K.op("pe", lambda e, k=k, blk=blk: e.matmul(pf[:], lhsT=uT[ui][:, k, blk * 128:(blk + 1) * 128], rhs=W1[:, k, 2048:2056], start=(k == 0), stop=(k == 7)),
                                 reads=[r_uT[ui], r_W1], writes=[r_pf], signal=(k == 7))
                        K.op("dve", lambda e, gb=gb: e.tensor_tensor(out=F_all[:, :, gb], in0=pf[:], in1=bfg[:], op=ALU.add), reads=[r_pf, r_g], writes=[r_F])
                    for grp, c0, dst in (("k", 1024, "kT"), ("u", 0, "usT")):
                        for t in range(4):
                            pi = kcount[0] % 3
                            si = kcount[0] % 4
                            kcount[0] += 1
                            for k in range(8):
                                K.op("pe", lambda e, k=k, pi=pi, c0=c0, t=t: e.matmul(pk[pi][:], lhsT=W1[:, k, c0 + t * 128:c0 + (t + 1) * 128], rhs=uT[ui][:, k, :], start=(k == 0), stop=(k == 7)),
                                     reads=[r_uT[ui], r_W1], writes=[r_pk[pi]], signal=(k == 7))
                            if grp == "k":
                                evac(kst[si][:], pk[pi][:], [r_pk[pi]], [r_kst[si]])
                                for e_ in range(2):
                                    K.dma("sp", kT[2 * t + e_, 0:64, s * 512:(s + 1) * 512], kst[si][64 * e_:64 * e_ + 64, :], reads=[r_kst[si]], dres=r_kst[si])
                            else:
                                s4 = s % 4
                                evac(usd[t][:, :, s4 * 32:(s4 + 1) * 32], pk[pi][:].rearrange("p (c s) -> p s c", s=16), [r_pk[pi]], [r_usd[t]])
                                if s4 == 3:
                                    g4_ = s // 4
                                    K.dma("sp", usT[t * 128:(t + 1) * 128, :, g4_ * 128:(g4_ + 1) * 128], usd[t][:], reads=[r_usd[t]], dres=r_usd[t])
                    for t in range(4):
                        pi = kcount[0] % 3
                        qi = kcount[0] % 2
                        kcount[0] += 1
                        for k in range(8):
                            K.op("pe", lambda e, k=k, pi=pi, t=t: e.matmul(pk[pi][:, 0:128], lhsT=W1[:, k, 512 + t * 128:512 + (t + 1) * 128], rhs=uT[ui][:, k, 384:512], start=(k == 0), stop=(k == 7)),
                                 reads=[r_uT[ui], r_W1], writes=[r_pk[pi]], signal=(k == 7))
                        evac(qst[qi][:], pk[pi][:, 0:128], [r_pk[pi]], [r_qst[qi]], scale=0.125)
                        for e_ in range(2):
                            K.dma("sp", qT[2 * t + e_, 0:64, s * 128:(s + 1) * 128], qst[qi][64 * e_:64 * e_ + 64, :], reads=[r_qst[qi]], dres=r_qst[qi])
                    K.dma("sp", uTo[:, s * 128:(s + 1) * 128].rearrange("(k p) t -> p k t", p=128), uT[ui][:, :, 384:512], reads=[r_uT[ui]], dres=r_uT[ui])

                stage_N(0)
                stage_T(0)
                stage_N(1)
                for s in range(16):
                    if s + 1 < 16:
                        stage_T(s + 1)
                    if s + 2 < 16:
                        stage_N(s + 2)
                    stage_M(s)
                K.barrier()

            if stage <= 1:
                raise _Stop()
            with contextlib.ExitStack() as p2:
                for _once in (0,):
                    L = sb(p2, "L", [128, 512]); r_L = Res()
                    Fv = F_all[:].rearrange("p h b -> p (h b)")
                    K.op("act", lambda e: e.activation(out=L[:], in_=Fv, func=AF.Exp, scale=-1.0), reads=[r_F], writes=[r_L])
                    one1 = sb(p2, "one1", [128, 1]); r_one1 = Res()
                    K.op("dve", lambda e: e.memset(one1[:], 1.0), writes=[r_one1])
                    K.op("act", lambda e: e.activation(out=L[:], in_=L[:], func=AF.Ln, bias=one1[:], scale=1.0), reads=[r_L, r_one1], writes=[r_L])
                    pc = ps(p2, "pc", [128, 512]); r_pc = Res()
                    pc2 = ps(p2, "pc2", [128, 512]); r_pc2 = Res()
                    K.op("pe", lambda e: e.matmul(pc[:], lhsT=triu[:], rhs=L[:], start=True, stop=True), reads=[r_L, r_const], writes=[r_pc])
                    incl = sb(p2, "incl", [128, 512]); r_incl = Res()
                    evac(incl[:], pc[:], [r_pc], [r_incl], eng="dve")
                    K.op("pe", lambda e: e.matmul(pc2[:], lhsT=e127[:], rhs=incl[:], start=True, stop=True), reads=[r_incl, r_const], writes=[r_pc2])
                    tot = sb(p2, "tot", [128, 512]); r_tot = Res()
                    evac(tot[:], pc2[:], [r_pc2], [r_tot], eng="dve")
                    rmul = sb(p2, "rmul", [128, 8, NB]); r_rmul = Res()
                    K.op("dve", lambda e: e.memset(rmul[:], 1.0), writes=[r_rmul])
                    K.op("dve", lambda e: e.memset(rmul[:, :, 0:1], 0.0), writes=[r_rmul])
                    offs = sb(p2, "offs", [128, 512]); r_offs = Res()
                    K.op("dve", lambda e: e.tensor_tensor_scan(out=offs[:], data0=rmul[:].rearrange("p h b -> p (h b)"), data1=tot[:], initial=0.0, op0=ALU.mult, op1=ALU.add),
                         reads=[r_rmul, r_tot], writes=[r_offs])
                    cl = sb(p2, "cl", [128, 512]); r_cl = Res()
                    K.op("dve", lambda e: e.tensor_tensor(out=cl[:], in0=offs[:], in1=tot[:], op=ALU.subtract), reads=[r_offs, r_tot], writes=[r_cl])
                    K.op("dve", lambda e: e.tensor_tensor(out=cl[:], in0=cl[:], in1=incl[:], op=ALU.add), reads=[r_cl, r_incl], writes=[r_cl])
                    if os.environ.get("DBG_SUB") == "1":
                        break
                    spl = sb(p2, "spl", [128, NB, 8, 3], BF16); r_spl = Res()
                    res1 = sb(p2, "res1", [128, 512]); r_res1 = Res()
                    clv = cl[:].rearrange("p (h b) -> p b h", h=8)
                    r1v = res1[:].rearrange("p (h b) -> p b h", h=8)
                    K.op("dve", lambda e: e.tensor_copy(out=spl[:, :, :, 0], in_=clv), reads=[r_cl], writes=[r_spl])
                    K.op("dve", lambda e: e.tensor_tensor(out=r1v, in0=clv, in1=spl[:, :, :, 0], op=ALU.subtract), reads=[r_cl, r_spl], writes=[r_res1])
                    K.op("dve", lambda e: e.tensor_copy(out=spl[:, :, :, 1], in_=r1v), reads=[r_res1], writes=[r_spl])
                    K.op("dve", lambda e: e.tensor_tensor(out=r1v, in0=r1v, in1=spl[:, :, :, 1], op=ALU.subtract), reads=[r_res1, r_spl], writes=[r_res1])
                    K.op("dve", lambda e: e.tensor_copy(out=spl[:, :, :, 2], in_=r1v), reads=[r_res1], writes=[r_spl])
                    if os.environ.get("DBG_SUB") == "2":
                        break
                    augT = sb(p2, "augT", [24, NT], BF16); r_augT = Res()
                    qaug = sb(p2, "qaug", [24, NOWN], BF16); r_qaug = Res()
                    pa = [ps(p2, "pa%d" % i, [128, 512]) for i in range(2)]; r_pa = [Res() for _ in range(2)]
                    for g4 in range(16):
                        pi = g4 % 2
                        for bb in range(4):
                            blk = 4 * g4 + bb
                            K.op("pe", lambda e, blk=blk, bb=bb, pi=pi: e.matmul(pa[pi][0:24, bb * 128:(bb + 1) * 128], lhsT=spl[:, blk, :, :].rearrange("p h s -> p (h s)"), rhs=identb[:], start=True, stop=True),
                                 reads=[r_spl, r_const], writes=[r_pa[pi]], signal=(bb == 3))
                        if os.environ.get("DBG_VAR") != "B":
                            evac(augT[:, g4 * 512:(g4 + 1) * 512], pa[pi][0:24, :], [r_pa[pi]], [r_augT], eng={"C": "act", "D": "dve"}.get(os.environ.get("DBG_VAR"), None))
                        if os.environ.get("DBG_VAR") == "A":
                            continue
                        K.op("dve", lambda e, g4=g4, pi=pi: e.tensor_scalar(out=qaug[:, g4 * 128:(g4 + 1) * 128], in0=pa[pi][0:24, 384:512], scalar1=-1.0, scalar2=None, op0=ALU.mult),
                             reads=[r_pa[pi]], writes=[r_qaug])
                    if os.environ.get("DBG_SUB") == "3":
                        break
                    for h in range(8):
                        K.dma("sp", kT[h, 64:67, :], augT[3 * h:3 * h + 3, :], reads=[r_augT], dres=r_augT)
                        K.dma("sp", qT[h, 67:70, :], qaug[3 * h:3 * h + 3, :], reads=[r_qaug], dres=r_qaug)
                    K.barrier()

            if stage <= 2:
                raise _Stop()
            with contextlib.ExitStack() as p3:
                T16 = 16
                r_pc_ = Res("parc"); r_pcc = Res("parcc")
                def lam_bar(st, P, n, lr, li, ldt, tag, pw, st_tmp=None, r_in=None):
                    lbr = sb(st, tag + "lbr", [P, n]); lbi = sb(st, tag + "lbi", [P, n])
                    al = sb(st, tag + "al", [P, n]); tf = sb(st, tag + "tf", [P, n])
                    st2 = st if st_tmp is None else st_tmp
                    dt_ = sb(st2, tag + "dt", [P, n]); th = sb(st2, tag + "th", [P, n])
                    ti = sb(st2, tag + "ti", [P, n], I32); fa = sb(st2, tag + "fa", [P, n])
                    mg = sb(st2, tag + "mg", [P, n]); sn = sb(st2, tag + "sn", [P, n]); cs = sb(st2, tag + "cs", [P, n])
                    rr = Res(tag)
                    r_in = r_pc_ if r_in is None else r_in
                    K.op("act", lambda e: e.activation(out=dt_[:], in_=ldt[:], func=AF.Exp), reads=[r_in], writes=[rr])
                    K.op("dve", lambda e: e.tensor_tensor(out=al[:], in0=lr[:], in1=dt_[:], op=ALU.mult), reads=[rr, r_in], writes=[rr])
                    K.op("dve", lambda e: e.tensor_tensor(out=th[:], in0=li[:], in1=dt_[:], op=ALU.mult), reads=[rr, r_in], writes=[rr])
                    K.op("act", lambda e: e.activation(out=mg[:], in_=al[:], func=AF.Exp, scale=float(pw)), reads=[rr], writes=[rr])
                    K.op("dve", lambda e: e.tensor_scalar(out=tf[:], in0=th[:], scalar1=float(pw) / TWO_PI, scalar2=None, op0=ALU.mult), reads=[rr], writes=[rr])
                    K.op("dve", lambda e: e.tensor_copy(out=ti[:], in_=tf[:]), reads=[rr], writes=[rr])
                    K.op("dve", lambda e: e.tensor_copy(out=fa[:], in_=ti[:]), reads=[rr], writes=[rr])
                    K.op("dve", lambda e: e.tensor_tensor(out=tf[:], in0=tf[:], in1=fa[:], op=ALU.subtract), reads=[rr], writes=[rr])
                    K.op("act", lambda e: e.activation(out=sn[:], in_=tf[:], func=AF.Sin, scale=TWO_PI), reads=[rr], writes=[rr])
                    K.op("dve", lambda e: e.tensor_scalar(out=fa[:], in0=tf[:], scalar1=-1.0, scalar2=None, op0=ALU.mult), reads=[rr], writes=[rr])
                    K.op("dve", lambda e: e.tensor_tensor(out=fa[:], in0=fa[:], in1=tf[:], op=ALU.max), reads=[rr], writes=[rr])
                    K.op("act", lambda e: e.activation(out=cs[:], in_=fa[:], func=AF.Sin, scale=-TWO_PI, bias=halfpi[0:P, :]), reads=[rr], writes=[rr])
                    K.op("dve", lambda e: e.tensor_tensor(out=lbr[:], in0=mg[:], in1=cs[:], op=ALU.mult), reads=[rr], writes=[rr])
                    K.op("dve", lambda e: e.tensor_tensor(out=lbi[:], in0=mg[:], in1=sn[:], op=ALU.mult), reads=[rr], writes=[rr])
                    return lbr, lbi, al, tf, rr

                Wa = sb(p3, "Wa", [96, T16, 2, 768], BF16)
                lrr = sb(p3, "lrr", [128, 16]); lir = sb(p3, "lir", [128, 16]); dtr = sb(p3, "dtr", [128, 16])
                Crb = sb(p3, "Crb", [128, 512], BF16); Cib = sb(p3, "Cib", [128, 512], BF16)
                Dmb = sb(p3, "Dmb", [96, 192], BF16)
                for t_, s_ in ((lrr, lamr_r), (lir, lami_r), (dtr, ldt_r)):
                    K.dma("sp", t_[:], s_[:, :], writes=[r_pc_], dres=r_pc_)
                Vtab = sb(p3, "Vtab", [128, 16, T16, 2, 32], BF16)
                Kd = sb(p3, "Kd", [96, 6, T16, 32], BF16)
                K.op("pool", lambda e: e.memset(Kd[:].rearrange("p s d c -> p (s d c)"), 0.0), writes=[Res()])
                cidx = sb(p3, "cidx", [128, 512])
                K.dma("sp", cidx[:], c_idx[:, :], writes=[r_pc_], dres=r_pc_)
                r_tab = Res("tab")
                lbr_r, lbi_r, alr, f1r, r_lr1 = lam_bar(p3, 128, 16, lrr, lir, dtr, "r1", 1)
                _, _, _, f16r, r_lr16 = lam_bar(p3, 128, 16, lrr, lir, dtr, "r16", 16)
                rho16 = sb(p3, "rho16", [128, 16])
                K.op("act", lambda e: e.activation(out=rho16[:], in_=alr[:], func=AF.Exp, scale=16.0), reads=[r_lr1], writes=[r_tab])
                tmp = {e_: (sb(p3, "ta_" + e_, [128, 512]), sb(p3, "tb_" + e_, [128, 512]), Res()) for e_ in ("dve", "pool")}
                ptc = contextlib.ExitStack()
                Bbr = sb(ptc, "Bbr", [128, 512], BF16); Bbi = sb(ptc, "Bbi", [128, 512], BF16)
                Lr = sb(ptc, "Lr", [128, 16, T16 + 1]); Li = sb(ptc, "Li", [128, 16, T16 + 1])
                Brf = sb(ptc, "Brf", [128, 512]); Bif = sb(ptc, "Bif", [128, 512]); Cif = sb(ptc, "Cif", [128, 512])
                Crf = sb(ptc, "Crf", [128, 512]); Dmf = sb(ptc, "Dmf", [96, 192]); r_cd = Res("cd")
                cfr = sb(ptc, "cfr", [128, 16]); cfi = sb(ptc, "cfi", [128, 16]); s1 = sb(ptc, "s1", [128, 16]); s2 = sb(ptc, "s2", [128, 16])
                dn = sb(ptc, "dn", [128, 16]); nr2 = sb(ptc, "nr2", [128, 16]); r_cf = Res("cf")
                K.dma("sp", Cif[:], Ci_r[:, :], writes=[r_cd], dres=r_cd)
                K.dma("sp", Brf[:], Brow_r[:, :], writes=[r_cd], dres=r_cd)
                K.dma("sp", Bif[:], Brow_i[:, :], writes=[r_cd], dres=r_cd)
                K.dma("sp", Crf[:], Cr_r[:, :], writes=[r_cd], dres=r_cd)
                K.dma("sp", Dmf[:], Dm[:, :], writes=[r_cd], dres=r_cd)
                r_cdb = Res("cdb")
                K.op("dve", lambda e: e.tensor_copy(out=Crb[:], in_=Crf[:]), reads=[r_cd], writes=[r_cdb])
                K.op("dve", lambda e: e.tensor_copy(out=Dmb[:], in_=Dmf[:]), reads=[r_cd], writes=[r_cdb])
                ptb = contextlib.ExitStack()
                lrc = sb(ptb, "lrc", [96, 768]); lic = sb(ptb, "lic", [96, 768]); dtc = sb(ptb, "dtc", [96, 768])
                Brc = sb(ptb, "Brc", [96, 768]); Bic = sb(ptb, "Bic", [96, 768])
                for t_, s_ in ((lrc, lamr_c), (lic, lami_c), (dtc, ldt_c), (Brc, Br_c), (Bic, Bi_c)):
                    K.dma("sp", t_[:], s_[:, :], writes=[r_pcc], dres=r_pcc)
                K.op("dve", lambda e: e.tensor_scalar(out=Cib[:], in0=Cif[:], scalar1=-1.0, scalar2=None, op0=ALU.mult), reads=[r_cd], writes=[r_tab])

                with contextlib.ExitStack() as ptb2:
                    lbr, lbi, alc, _, r_lc = lam_bar(ptb, 96, 768, lrc, lic, dtc, "c", 1, st_tmp=ptb2, r_in=r_pcc)
                    K.barrier()
                Pr = sb(ptb, "Pr", [96, 768]); Pi = sb(ptb, "Pi", [96, 768]); t1 = sb(ptb, "t1", [96, 768]); t2 = sb(ptb, "t2", [96, 768])
                den = sb(ptb, "den", [96, 768]); nr = sb(ptb, "nr", [96, 768])
                r_P = Res("P")
                V = "dve"
                K.op(V, lambda e: e.tensor_scalar(out=nr[:], in0=lbr[:], scalar1=-1.0, scalar2=None, op0=ALU.add), reads=[r_lc], writes=[r_P])
                K.op(V, lambda e: e.tensor_tensor(out=den[:], in0=lrc[:], in1=lrc[:], op=ALU.mult), reads=[r_pcc], writes=[r_P])
                K.op(V, lambda e: e.tensor_tensor(out=t1[:], in0=lic[:], in1=lic[:], op=ALU.mult), reads=[r_pcc, r_P], writes=[r_P])
                K.op(V, lambda e: e.tensor_tensor(out=den[:], in0=den[:], in1=t1[:], op=ALU.add), reads=[r_P], writes=[r_P])
                K.op(V, lambda e: e.reciprocal(out=den[:], in_=den[:]), reads=[r_P], writes=[r_P])
                K.op(V, lambda e: e.tensor_tensor(out=t1[:], in0=nr[:], in1=lrc[:], op=ALU.mult), reads=[r_P, r_pcc], writes=[r_P])
                K.op(V, lambda e: e.tensor_tensor(out=t2[:], in0=lbi[:], in1=lic[:], op=ALU.mult), reads=[r_P, r_pcc, r_lc], writes=[r_P])
                K.op(V, lambda e: e.tensor_tensor(out=t1[:], in0=t1[:], in1=t2[:], op=ALU.add), reads=[r_P], writes=[r_P])
                K.op(V, lambda e: e.tensor_tensor(out=Pr[:], in0=t1[:], in1=den[:], op=ALU.mult), reads=[r_P], writes=[r_P])
                K.op(V, lambda e: e.tensor_tensor(out=t1[:], in0=lbi[:], in1=lrc[:], op=ALU.mult), reads=[r_P, r_pcc, r_lc], writes=[r_P])
                K.op(V, lambda e: e.tensor_tensor(out=t2[:], in0=nr[:], in1=lic[:], op=ALU.mult), reads=[r_P, r_pcc], writes=[r_P])
                K.op(V, lambda e: e.tensor_tensor(out=t1[:], in0=t1[:], in1=t2[:], op=ALU.subtract), reads=[r_P], writes=[r_P])
                K.op(V, lambda e: e.tensor_tensor(out=Pi[:], in0=t1[:], in1=den[:], op=ALU.mult), reads=[r_P], writes=[r_P])
                Pn = sb(ptb, "Pn", [96, 768])
                for kk in range(T16):
                    tau = T16 - 1 - kk
                    K.op(V, lambda e: e.tensor_tensor(out=t1[:], in0=Pr[:], in1=Brc[:], op=ALU.mult), reads=[r_P, r_pcc], writes=[r_P])
                    K.op(V, lambda e: e.tensor_tensor(out=t2[:], in0=Pi[:], in1=Bic[:], op=ALU.mult), reads=[r_P, r_pcc], writes=[r_P])
                    K.op(V, lambda e, tau=tau: e.tensor_tensor(out=Wa[:, tau, 0, :], in0=t1[:], in1=t2[:], op=ALU.subtract), reads=[r_P], writes=[r_tab])
                    K.op(V, lambda e: e.tensor_tensor(out=t1[:], in0=Pr[:], in1=Bic[:], op=ALU.mult), reads=[r_P, r_pcc], writes=[r_P])
                    K.op(V, lambda e: e.tensor_tensor(out=t2[:], in0=Pi[:], in1=Brc[:], op=ALU.mult), reads=[r_P, r_pcc], writes=[r_P])
                    K.op(V, lambda e, tau=tau: e.tensor_tensor(out=Wa[:, tau, 1, :], in0=t1[:], in1=t2[:], op=ALU.add), reads=[r_P], writes=[r_tab])
                    if kk < T16 - 1:
                        K.op(V, lambda e: e.tensor_tensor(out=t1[:], in0=Pr[:], in1=lbr[:], op=ALU.mult), reads=[r_P, r_lc], writes=[r_P])
                        K.op(V, lambda e: e.tensor_tensor(out=t2[:], in0=Pi[:], in1=lbi[:], op=ALU.mult), reads=[r_P, r_lc], writes=[r_P])
                        K.op(V, lambda e: e.tensor_tensor(out=Pn[:], in0=t1[:], in1=t2[:], op=ALU.subtract), reads=[r_P], writes=[r_P])
                        K.op(V, lambda e: e.tensor_tensor(out=t1[:], in0=Pr[:], in1=lbi[:], op=ALU.mult), reads=[r_P, r_lc], writes=[r_P])
                        K.op(V, lambda e: e.tensor_tensor(out=t2[:], in0=Pi[:], in1=lbr[:], op=ALU.mult), reads=[r_P, r_lc], writes=[r_P])
                        K.op(V, lambda e: e.tensor_tensor(out=Pi[:], in0=t1[:], in1=t2[:], op=ALU.add), reads=[r_P], writes=[r_P])
                        K.op(V, lambda e: e.tensor_copy(out=Pr[:], in_=Pn[:]), reads=[r_P], writes=[r_P])
                K.barrier()
                ptb.close()

                def cmul(eng, outr, outi, ar, ai, br, bi, conj_b, reads, writes, shp):
                    ta, tb, r_tt = tmp[eng]
                    o1 = ALU.subtract if not conj_b else ALU.add
                    o2 = ALU.add if not conj_b else ALU.subtract
                    tav, tbv = shp(ta), shp(tb)
                    K.op(eng, lambda e: e.tensor_tensor(out=tav, in0=ar, in1=br, op=ALU.mult), reads=reads, writes=[r_tt])
                    K.op(eng, lambda e: e.tensor_tensor(out=tbv, in0=ai, in1=bi, op=ALU.mult), reads=reads + [r_tt], writes=[r_tt])
                    K.op(eng, lambda e: e.tensor_tensor(out=outr, in0=tav, in1=tbv, op=o1), reads=[r_tt], writes=writes)
                    K.op(eng, lambda e: e.tensor_tensor(out=tav, in0=ai, in1=br, op=ALU.mult), reads=reads + [r_tt], writes=[r_tt])
                    K.op(eng, lambda e: e.tensor_tensor(out=tbv, in0=ar, in1=bi, op=ALU.mult), reads=reads + [r_tt], writes=[r_tt])
                    K.op(eng, lambda e: e.tensor_tensor(out=outi, in0=tav, in1=tbv, op=o2), reads=[r_tt], writes=writes)

                flat = lambda n: (lambda t: t[:, 0:n])
                pq32 = lambda t: t[:].rearrange("p (q c) -> p q c", c=32)

                V = "dve"
                K.op(V, lambda e: e.tensor_scalar(out=nr2[:], in0=lbr_r[:], scalar1=-1.0, scalar2=None, op0=ALU.add), reads=[r_lr1], writes=[r_cf])
                K.op(V, lambda e: e.tensor_tensor(out=dn[:], in0=lrr[:], in1=lrr[:], op=ALU.mult), reads=[r_pc_], writes=[r_cf])
                K.op(V, lambda e: e.tensor_tensor(out=s1[:], in0=lir[:], in1=lir[:], op=ALU.mult), reads=[r_pc_, r_cf], writes=[r_cf])
                K.op(V, lambda e: e.tensor_tensor(out=dn[:], in0=dn[:], in1=s1[:], op=ALU.add), reads=[r_cf], writes=[r_cf])
                K.op(V, lambda e: e.reciprocal(out=dn[:], in_=dn[:]), reads=[r_cf], writes=[r_cf])
                K.op(V, lambda e: e.tensor_tensor(out=s1[:], in0=nr2[:], in1=lrr[:], op=ALU.mult), reads=[r_cf, r_pc_], writes=[r_cf])
                K.op(V, lambda e: e.tensor_tensor(out=s2[:], in0=lbi_r[:], in1=lir[:], op=ALU.mult), reads=[r_cf, r_pc_, r_lr1], writes=[r_cf])
                K.op(V, lambda e: e.tensor_tensor(out=s1[:], in0=s1[:], in1=s2[:], op=ALU.add), reads=[r_cf], writes=[r_cf])
                K.op(V, lambda e: e.tensor_tensor(out=cfr[:], in0=s1[:], in1=dn[:], op=ALU.mult), reads=[r_cf], writes=[r_cf])
                K.op(V, lambda e: e.tensor_tensor(out=s1[:], in0=lbi_r[:], in1=lrr[:], op=ALU.mult), reads=[r_cf, r_pc_, r_lr1], writes=[r_cf])
                K.op(V, lambda e: e.tensor_tensor(out=s2[:], in0=nr2[:], in1=lir[:], op=ALU.mult), reads=[r_cf, r_pc_], writes=[r_cf])
                K.op(V, lambda e: e.tensor_tensor(out=s1[:], in0=s1[:], in1=s2[:], op=ALU.subtract), reads=[r_cf], writes=[r_cf])
                K.op(V, lambda e: e.tensor_tensor(out=cfi[:], in0=s1[:], in1=dn[:], op=ALU.mult), reads=[r_cf], writes=[r_cf])
                r_Bb = Res("Bb")
                bc = lambda t: t[:].unsqueeze(2).to_broadcast([128, 16, 32])
                cmul("dve", pq32(Bbr), pq32(Bbi), pq32(Brf), pq32(Bif), bc(cfr), bc(cfi), False, [r_cd, r_cf], [r_Bb], pq32)
                r_L = Res("L")
                K.op("pool", lambda e: e.memset(Lr[:, :, 0:1], 1.0), writes=[r_L])
                K.op("pool", lambda e: e.memset(Li[:, :, 0:1], 0.0), writes=[r_L])
                tp_ = tmp["pool"]
                for k in range(T16):
                    ta, tb, r_tt = tp_
                    K.op("pool", lambda e, k=k: e.tensor_tensor(out=ta[:, 0:16], in0=Lr[:, :, k], in1=lbr_r[:], op=ALU.mult), reads=[r_L, r_lr1], writes=[r_tt])
                    K.op("pool", lambda e, k=k: e.tensor_tensor(out=tb[:, 0:16], in0=Li[:, :, k], in1=lbi_r[:], op=ALU.mult), reads=[r_L, r_lr1, r_tt], writes=[r_tt])
                    K.op("pool", lambda e, k=k: e.tensor_tensor(out=Lr[:, :, k + 1], in0=ta[:, 0:16], in1=tb[:, 0:16], op=ALU.subtract), reads=[r_tt], writes=[r_L])
                    K.op("pool", lambda e, k=k: e.tensor_tensor(out=ta[:, 0:16], in0=Lr[:, :, k], in1=lbi_r[:], op=ALU.mult), reads=[r_L, r_lr1, r_tt], writes=[r_tt])
                    K.op("pool", lambda e, k=k: e.tensor_tensor(out=tb[:, 0:16], in0=Li[:, :, k], in1=lbr_r[:], op=ALU.mult), reads=[r_L, r_lr1, r_tt], writes=[r_tt])
                    K.op("pool", lambda e, k=k: e.tensor_tensor(out=Li[:, :, k + 1], in0=ta[:, 0:16], in1=tb[:, 0:16], op=ALU.add), reads=[r_tt], writes=[r_L])
                r_V = Res("V")
                for tau in range(T16):
                    eng = "dve" if tau % 2 == 0 else "pool"
                    ta, tb, r_tt = tmp[eng]
                    lr_b = Lr[:, :, tau + 1].unsqueeze(2).to_broadcast([128, 16, 32])
                    li_b = Li[:, :, tau + 1].unsqueeze(2).to_broadcast([128, 16, 32])
                    K.op(eng, lambda e, ta=ta, lr_b=lr_b: e.tensor_tensor(out=pq32(ta), in0=pq32(Crf), in1=lr_b, op=ALU.mult), reads=[r_cd, r_L], writes=[r_tt])
                    K.op(eng, lambda e, tb=tb, li_b=li_b: e.tensor_tensor(out=pq32(tb), in0=pq32(Cif), in1=li_b, op=ALU.mult), reads=[r_cd, r_L, r_tt], writes=[r_tt])
                    K.op(eng, lambda e, ta=ta, tb=tb, tau=tau: e.tensor_tensor(out=Vtab[:, :, tau, 0, :], in0=pq32(ta), in1=pq32(tb), op=ALU.subtract), reads=[r_tt], writes=[r_V])
                    K.op(eng, lambda e, ta=ta, li_b=li_b: e.tensor_tensor(out=pq32(ta), in0=pq32(Crf), in1=li_b, op=ALU.mult), reads=[r_cd, r_L, r_tt], writes=[r_tt])
                    K.op(eng, lambda e, tb=tb, lr_b=lr_b: e.tensor_tensor(out=pq32(tb), in0=pq32(Cif), in1=lr_b, op=ALU.mult), reads=[r_cd, r_L, r_tt], writes=[r_tt])
                    K.op(eng, lambda e, ta=ta, tb=tb: e.tensor_tensor(out=pq32(ta), in0=pq32(ta), in1=pq32(tb), op=ALU.add), reads=[r_tt], writes=[r_tt])
                    K.op(eng, lambda e, ta=ta, tau=tau: e.tensor_scalar(out=Vtab[:, :, tau, 1, :], in0=pq32(ta), scalar1=-1.0, scalar2=None, op0=ALU.mult), reads=[r_tt], writes=[r_V])
                r_Kd = Res("Kd")
                with contextlib.ExitStack() as pk_:
                    pskd = [ps(pk_, "pskd%d" % i_, [96, 512]) for i_ in range(2)]; r_pskd = [Res() for _ in range(2)]
                    for sl in range(6):
                        pi_ = sl % 2
                        npair = min(3, 16 - 3 * sl)
                        for kk in range(npair):
                            q = 3 * sl + kk
                            for dl in range(T16):
                                rr_ = Crb[:, q * 32:(q + 1) * 32] if dl == 0 else Vtab[:, q, dl - 1, 0, :]
                                ri_ = Cib[:, q * 32:(q + 1) * 32] if dl == 0 else Vtab[:, q, dl - 1, 1, :]
                                lastmm = (kk == npair - 1 and dl == T16 - 1)
                                K.op("pe", lambda e, rr_=rr_, q=q, kk=kk, dl=dl, pi_=pi_: e.matmul(pskd[pi_][32 * kk:32 * kk + 32, dl * 32:(dl + 1) * 32], lhsT=Bbr[:, q * 32:(q + 1) * 32], rhs=rr_, start=True, stop=False),
                                     reads=[r_Bb, r_V, r_cdb, r_tab], writes=[r_pskd[pi_]], signal=False)
                                K.op("pe", lambda e, ri_=ri_, q=q, kk=kk, dl=dl, pi_=pi_: e.matmul(pskd[pi_][32 * kk:32 * kk + 32, dl * 32:(dl + 1) * 32], lhsT=Bbi[:, q * 32:(q + 1) * 32], rhs=ri_, start=False, stop=True),
                                     reads=[r_Bb, r_V, r_cdb, r_tab], writes=[r_pskd[pi_]], signal=lastmm)
                        np_ = 32 * npair
                        evac(Kd[0:np_, sl, :, :].rearrange("p d c -> p (d c)"), pskd[pi_][0:np_, :], [r_pskd[pi_]], [r_Kd], eng="act")
                    K.op("dve", lambda e: e.tensor_tensor(out=Kd[:, :, 0, :], in0=Kd[:, :, 0, :], in1=Dmf[:].rearrange("p (s c) -> p s c", c=32), op=ALU.add), reads=[r_Kd, r_cd], writes=[r_Kd])
                    K.barrier()
                ptc.close()

                usp = [sb(p3, "usp%d" % i, [96, 16, 512], BF16) for i in range(2)]; r_usp = [Res() for _ in range(2)]
                uso = [sb(p3, "uso%d" % i, [96, 16, 128], BF16) for i in range(2)]; r_uso = [Res() for _ in range(2)]
                pz = [ps(p3, "pz%d" % i, [128, 512]) for i in range(2)]; r_pz = [Res() for _ in range(2)]
                pY = [[ps(p3, "pY%d_%d" % (b_, j_), [32, 512]) for j_ in range(1)] for b_ in range(4)]
                r_pY = [Res() for _ in range(4)]
                Z = sb(p3, "Z", [128, 2, 512]); r_Z = Res()
                Zd = sb(p3, "Zd", [128, 2, 512]); r_Zd = Res()
                Wc = sb(p3, "Wc", [128, 2, 512]); r_Wc = Res()
                ang, angf, r_ang = tmp["dve"]; angi = sb(p3, "angi", [128, 512], I32)
                Ec = sb(p3, "Ec", [128, 2, 512]); r_Ec = Res()
                rhoT = sb(p3, "rhoT", [128, 512]); r_rhoT = Res()
                Sown = [sb(p3, "Sown%d" % i, [128, 2, 128], BF16) for i in range(2)]; r_Sown = [Res() for _ in range(2)]
                ysb1 = sb(p3, "ysb", [32, NOWN], BF16); ysb = [ysb1, ysb1]; r_ysb1 = Res(); r_ysb = [r_ysb1, r_ysb1]

                def load_pair(q):
                    i = q % 2
                    kb = 32 * (q % 3)
                    K.dma("sp", usp[i][kb:kb + 32, :, :], usT[q * 32:(q + 1) * 32, :, :], writes=[r_usp[i]], dres=r_usp[i])
                    K.op("act", lambda e: e.activation(out=uso[i][kb:kb + 32, :, :].rearrange("p s (m k) -> p s m k", k=8),
                                                       in_=usp[i][kb:kb + 32, :, :].rearrange("p s (m r k) -> p s m r k", r=4, k=8)[:, :, :, 3, :], func=AF.Copy),
                         reads=[r_usp[i]], writes=[r_uso[i]])

                def twiddle(dst, r_dst, idx_ap, f_col, n):
                    K.op("dve", lambda e: e.tensor_scalar(out=ang[:, 0:n], in0=idx_ap, scalar1=f_col, scalar2=None, op0=ALU.mult), reads=[r_pc_, r_lr1, r_lr16], writes=[r_ang])
                    K.op("dve", lambda e: e.tensor_copy(out=angi[:, 0:n], in_=ang[:, 0:n]), reads=[r_ang], writes=[r_ang])
                    K.op("dve", lambda e: e.tensor_copy(out=angf[:, 0:n], in_=angi[:, 0:n]), reads=[r_ang], writes=[r_ang])
                    K.op("dve", lambda e: e.tensor_tensor(out=ang[:, 0:n], in0=ang[:, 0:n], in1=angf[:, 0:n], op=ALU.subtract), reads=[r_ang], writes=[r_ang])
                    K.op("act", lambda e: e.activation(out=dst[:, 1, 0:n], in_=ang[:, 0:n], func=AF.Sin, scale=-TWO_PI), reads=[r_ang], writes=[r_dst])
                    K.op("dve", lambda e: e.tensor_scalar(out=angf[:, 0:n], in0=ang[:, 0:n], scalar1=-1.0, scalar2=None, op0=ALU.mult), reads=[r_ang], writes=[r_ang])
                    K.op("dve", lambda e: e.tensor_tensor(out=angf[:, 0:n], in0=angf[:, 0:n], in1=ang[:, 0:n], op=ALU.max), reads=[r_ang], writes=[r_ang])
                    K.op("act", lambda e: e.activation(out=dst[:, 0, 0:n], in_=angf[:, 0:n], func=AF.Sin, scale=-TWO_PI, bias=halfpi[:]), reads=[r_ang], writes=[r_dst])

                def stage_P(q):
                    i = q % 2
                    kb = 32 * (q % 3)
                    sl = q // 3
                    cs_ = slice(sl * 128, (sl + 1) * 128)
                    for ri in range(2):
                        for tau in range(T16):
                            K.op("pe", lambda e, ri=ri, tau=tau: e.matmul(pz[ri][:], lhsT=Wa[kb:kb + 32, tau, ri, cs_], rhs=usp[i][kb:kb + 32, tau, :], start=(tau == 0), stop=(tau == T16 - 1)),
                                 reads=[r_usp[i], r_tab], writes=[r_pz[ri]], signal=(tau == T16 - 1))
                        evac(Z[:, ri, :], pz[ri][:], [r_pz[ri]], [r_Z], eng="act")
                    twiddle(Ec, r_Ec, cidx[:], f16r[:, q:q + 1], 512)
                    cmul("dve", Zd[:, 0, :], Zd[:, 1, :], Z[:, 0, :], Z[:, 1, :], Ec[:, 0, :], Ec[:, 1, :], False, [r_Z, r_Ec], [r_Zd], flat(512))
                    K.op("dve", lambda e: e.tensor_copy(out=rhoT[:], in_=rho16[:, q:q + 1].to_broadcast([128, 512])), reads=[r_tab], writes=[r_rhoT])
                    for ri in range(2):
                        K.op("dve", lambda e, ri=ri: e.tensor_tensor_scan(out=Wc[:, ri, :], data0=rhoT[:], data1=Zd[:, ri, :], initial=0.0, op0=ALU.mult, op1=ALU.add),
                             reads=[r_rhoT, r_Zd], writes=[r_Wc])
                    wv = Wc[:].rearrange("p r (m c) -> p r m c", c=32)
                    ev = Ec[:].rearrange("p r (m c) -> p r m c", c=32)
                    so = Sown[i][:].rearrange("p r (m k) -> p r m k", k=8)
                    mk = lambda t: t[:, 0:128].rearrange("p (m k) -> p m k", k=8)
                    cmul("dve", so[:, 0], so[:, 1], wv[:, 0, :, 23:31], wv[:, 1, :, 23:31], ev[:, 0, :, 23:31], ev[:, 1, :, 23:31], True, [r_Wc, r_Ec], [r_Sown[i]], mk)

                def stage_F(q):
                    i = q % 2
                    kb = 32 * (q % 3)
                    sl = q // 3
                    for tau in range(T16):
                        j_ = tau // 4
                        oap = pY[j_][0][:, (tau % 4) * 128:(tau % 4 + 1) * 128]
                        K.op("pe", lambda e, tau=tau, oap=oap: e.matmul(oap, lhsT=Vtab[:, q, tau, 0, :], rhs=Sown[i][:, 0, :], start=True, stop=False),
                             reads=[r_V, r_Sown[i]], writes=[r_pY[j_]], signal=False)
                        K.op("pe", lambda e, tau=tau, oap=oap: e.matmul(oap, lhsT=Vtab[:, q, tau, 1, :], rhs=Sown[i][:, 1, :], start=False, stop=False),
                             reads=[r_V, r_Sown[i]], writes=[r_pY[j_]], signal=False)
                        for sg_ in range(tau + 1):
                            K.op("pe", lambda e, tau=tau, sg_=sg_, oap=oap: e.matmul(oap, lhsT=Kd[kb:kb + 32, sl, tau - sg_, :], rhs=uso[i][kb:kb + 32, sg_, :], start=False, stop=(sg_ == tau)),
                                 reads=[r_Kd, r_uso[i]], writes=[r_pY[j_]], signal=(sg_ == tau and tau % 4 == 3))
                        if tau % 4 == 3:
                            yv = ysb[i][:].rearrange("p (c t) -> p c t", t=T16)
                            evac(yv[:, :, 4 * j_:4 * j_ + 4], pY[j_][0][:].rearrange("p (t c) -> p c t", t=4), [r_pY[j_]], [r_ysb[i]], eng="act")
                    K.dma("sp", yfm[32 * (q % 4):32 * (q % 4) + 32, q // 4, :], ysb[i][:], reads=[r_ysb[i]], dres=r_ysb[i])

                conv = []
                if stage >= 9 and not debug:
                    for name, src, rows, cols, c0 in (("Wglu", w_glu, 512, 2048, 0), ("Wfo", w_fox_o, 512, D, 0), ("Wg", w_in, D, 2048, 2056),
                                                      ("Wmx", w_mix, D, D, 0), ("Wmq", w_mem_q, D, 512, 0), ("Wmo", w_mem_o, 512, D, 0),
                                                      ("Wfi", w_ffn_in, D, 5632, 0), ("Wfo2", w_ffn_out, 2816, D, 0)):
                        wb[name] = dscr("wb_" + name, [rows, cols])
                        for k in range(rows // 128):
                            cc = 0
                            while cc < cols:
                                n = min(1024, cols - cc)
                                conv.append((src[k * 128:(k + 1) * 128, c0 + cc:c0 + cc + n], wb[name][k * 128:(k + 1) * 128, cc:cc + n], n))
                                cc += n
                cvb = [sb(p3, "cvb%d" % i_, [128, 1024], BF16) for i_ in range(2)]; r_cvb = [Res() for _ in range(2)]
                cstate = {"i": 0, "pend": []}

                def conv_step():
                    i_ = cstate["i"]
                    if i_ < len(conv):
                        src_ap, dst_ap, n = conv[i_]
                        si = wst_cnt[0] % NWST
                        wst_cnt[0] += 1
                        bi = i_ % 2
                        K.dma("sp", wst[si][:, 0:n], src_ap, writes=[r_wst[si]], dres=r_wst[si])
                        K.op("pool", lambda e: e.tensor_copy(out=cvb[bi][:, 0:n], in_=wst[si][:, 0:n]), reads=[r_wst[si]], writes=[r_cvb[bi]])
                        cstate["pend"].append((dst_ap, bi, n))
                        cstate["i"] += 1
                    while cstate["pend"] and (len(cstate["pend"]) > 1 or cstate["i"] >= len(conv)):
                        dst_ap, bi, n = cstate["pend"].pop(0)
                        K.dma("sp", dst_ap, cvb[bi][:, 0:n], reads=[r_cvb[bi]], dres=r_cvb[bi])

                load_pair(0)
                load_pair(1)
                stage_P(0)
                for q in range(16):
                    if q + 1 < 16:
                        stage_P(q + 1)
                    for _ in range(4):
                        conv_step()
                    stage_F(q)
                    if q + 2 < 16:
                        load_pair(q + 2)
                    for _ in range(4):
                        conv_step()
                while conv and (cstate["i"] < len(conv) or cstate["pend"]):
                    conv_step()
                K.barrier()
                for t in range(4):
                    K.op("act", lambda e, t=t: e.activation(out=yfm[:, t, :], in_=yfm[:, t, :], func=AF.Gelu_apprx_tanh), reads=[r_yfm], writes=[r_yfm])
                K.barrier()

            if debug:
                dbg["yfm"] = nc.dram_tensor("dbg_yfm", [128, 4, NOWN], BF16, kind="ExternalOutput").ap()
                K.dma("sp", dbg["yfm"][:, :, :], yfm[:], dres=Res())
            if stage <= 3:
                raise _Stop()
            attT = sb(es, "attT", [128, 4, NOWN], BF16)
            with contextlib.ExitStack() as p4:
                kth = [sb(p4, "kth%d" % i, [71, NT], BF16) for i in range(2)]; r_kth = [Res() for _ in range(2)]
                vah = [sb(p4, "vah%d" % i, [128, NB, 128], BF16) for i in range(2)]; r_vah = [Res() for _ in range(2)]
                qth = [sb(p4, "qth%d" % i, [71, NOWN], BF16) for i in range(2)]; r_qth = [Res() for _ in range(2)]
                pT = [sb(p4, "pT%d" % i, [128, 512], BF16) for i in range(4)]; r_pT = [Res() for _ in range(4)]
                osb = sb(p4, "osb", [128, 512]); r_osb = Res()
                rcp = sb(p4, "rcp", [128, 512]); r_rcp = Res()
                pss = [ps(p4, "pss%d" % i, [128, 512]) for i in range(4)]; r_pss = [Res() for _ in range(4)]
                pso = [ps(p4, "pso%d" % i, [128, 512]) for i in range(2)]; r_pso = [Res() for _ in range(2)]
                psb = ps(p4, "psb", [128, 512]); r_psb = Res()

                def load_head(h):
                    i = h % 2
                    K.dma("sp", kth[i][:], kT[h, :, :], writes=[r_kth[i]], dres=r_kth[i])
                    K.dma("sp", qth[i][:], qT[h, :, :], writes=[r_qth[i]], dres=r_qth[i])
                    K.dma("sp", vah[i][:], vA[h, :, :].rearrange("(b t) c -> t b c", t=128), writes=[r_vah[i]], dres=r_vah[i])

                NSB = 4
                LA = 3
                load_head(0)
                tiles = []
                for h in range(8):
                    for M in range(4):
                        written = [False] * 4
                        nkb = 16 * M + 16
                        for kb in range(nkb):
                            rel = kb - 16 * M - 3
                            i0 = 0 if rel <= 0 else (rel + 3) // 4
                            diag = (rel >= 0 and rel % 4 == 0)
                            st_flag = not written[i0]
                            assert all(written[x] == written[i0] for x in range(i0, 4))
                            for x in range(i0, 4):
                                written[x] = True
                            tiles.append(dict(h=h, M=M, kb=kb, c0=128 * i0, diag=diag, st=st_flag, last=(kb == nkb - 1), first=(kb == 0), g=h * 4 + M))

                def emit_qk(t, T):
                    h, M, kb, c0, diag = T["h"], T["M"], T["kb"], T["c0"], T["diag"]
                    i = h % 2
                    si = t % NSB
                    if not diag:
                        K.op("pe", lambda e: e.matmul(pss[si][:, c0:512], lhsT=kth[i][:, kb * 128:(kb + 1) * 128], rhs=qth[i][:, M * 512 + c0:(M + 1) * 512], start=True, stop=True),
                             reads=[r_kth[i], r_qth[i]], writes=[r_pss[si]], signal=True)
                    else:
                        if c0 + 128 < 512:
                            K.op("pe", lambda e: e.matmul(pss[si][:, c0 + 128:512], lhsT=kth[i][:, kb * 128:(kb + 1) * 128], rhs=qth[i][:, M * 512 + c0 + 128:(M + 1) * 512], start=True, stop=True),
                                 reads=[r_kth[i], r_qth[i]], writes=[r_pss[si]], signal=False)
                        K.op("pe", lambda e: e.matmul(pss[si][:, c0:c0 + 128], lhsT=kth[i][:, kb * 128:(kb + 1) * 128], rhs=qth[i][:, M * 512 + c0:M * 512 + c0 + 128], start=True, stop=False),
                             reads=[r_kth[i], r_qth[i]], writes=[r_pss[si]], signal=False)
                        K.op("pe", lambda e: e.matmul(pss[si][:, c0:c0 + 128], lhsT=identb[:], rhs=tmaskb[:], start=False, stop=True),
                             reads=[r_const], writes=[r_pss[si]], signal=True)
                    K.op("act", lambda e: e.activation(out=pT[si][:, c0:512], in_=pss[si][:, c0:512], func=AF.Exp), reads=[r_pss[si]], writes=[r_pT[si]])

                def emit_pv(t, T):
                    h, M, kb, c0 = T["h"], T["M"], T["kb"], T["c0"]
                    i = h % 2
                    si = t % NSB
                    oi = T["g"] % 2
                    odd = h % 2
                    mcols = 128 if odd else 65
                    st_flag, last = T["st"], T["last"]
                    if T["first"] and M == 0 and h + 1 < 8:
                        load_head(h + 1)
                    K.op("pe", lambda e: e.matmul(pso[oi][0:mcols, c0:512], lhsT=vah[i][:, kb, 0:mcols], rhs=pT[si][:, c0:512], start=st_flag, stop=last),
                         reads=[r_vah[i], r_pT[si]], writes=[r_pso[oi]], signal=last)
                    if last:
                        K.op("act", lambda e: e.activation(out=osb[0:mcols, :], in_=pso[oi][0:mcols, :], func=AF.Copy), reads=[r_pso[oi]], writes=[r_osb])
                        if odd:
                            K.op("pe", lambda e: e.matmul(psb[:], lhsT=self_[0:1, :], rhs=osb[0:1, :], start=True, stop=True), reads=[r_osb, r_const], writes=[r_psb])
                        else:
                            K.op("pe", lambda e: e.matmul(psb[0:64, :], lhsT=self_[64:65, 0:64], rhs=osb[64:65, :], start=True, stop=True), reads=[r_osb, r_const], writes=[r_psb])
                        lo = 64 if odd else 0
                        K.op("dve", lambda e: e.reciprocal(out=rcp[lo:lo + 64, :], in_=psb[lo:lo + 64, :]), reads=[r_psb], writes=[r_rcp])
                        K.op("dve", lambda e: e.tensor_tensor(out=attT[lo:lo + 64, h // 2, M * 512:(M + 1) * 512], in0=osb[lo:lo + 64, :], in1=rcp[lo:lo + 64, :], op=ALU.mult),
                             reads=[r_osb, r_rcp], writes=[r_att])

                nt = len(tiles)
                for t in range(nt + LA):
                    if t < nt:
                        emit_qk(t, tiles[t])
                    if t - LA >= 0:
                        emit_pv(t - LA, tiles[t - LA])
                K.barrier()

            if debug:
                dbg["att"] = nc.dram_tensor("dbg_att", [128, 4, NOWN], BF16, kind="ExternalOutput").ap()
                K.dma("sp", dbg["att"][:, :, :], attT[:], dres=Res())
            if stage <= 4:
                raise _Stop()
            kmT = sb(es, "kmT", [128, 4, 256], BF16)
            vmem = sb(es, "vmem", [128, 2, 512], BF16)
            r_km = Res("km")
            with contextlib.ExitStack() as pm:
                gkv = sb(pm, "gkv", [128, D]); r_gkv = Res()
                K.dma("sp", gkv[:], g_memkv[0:1, :].partition_broadcast(128), writes=[r_gkv], dres=r_gkv)
                Wkv = sb(pm, "Wkv", [128, 8, D], BF16); r_Wkv = Res()
                for k in range(8):
                    cast_load(lambda cc, n, k=k: Wkv[:, k, cc:cc + n], r_Wkv, w_mem_kv, k * 128, D, 0)
                mt = sb(pm, "mt", [128, 2, D]); r_mt = Res()
                K.dma("sp", mt[:], mem.rearrange("(b t) d -> t b d", t=128), writes=[r_mt], dres=r_mt)
                junk = sb(pm, "junkm", [128, D], BF16); r_junk = Res()
                ssm_ = sb(pm, "ssm_", [128, 1]); r_ssm = Res(); rsm = sb(pm, "rsm", [128, 1]); r_rsm = Res()
                mb = sb(pm, "mb", [128, D], BF16); r_mb = Res()
                mT = sb(pm, "mT", [128, 8, 256], BF16); r_mT = Res()
                ptm = ps(pm, "ptm", [128, 1024], BF16); r_ptm = Res()
                pkm = [ps(pm, "pkm%d" % i, [128, 512]) for i in range(2)]; r_pkm = [Res() for _ in range(2)]
                for b2 in range(2):
                    rmsnorm_tok("pm", mt[:, b2, :], r_mt, gkv[:], r_gkv, mb[:], r_mb, junk[:], r_junk, ssm_[:], r_ssm, rsm[:], r_rsm)
                    for k in range(8):
                        K.op("pe", lambda e, k=k: e.transpose(out=ptm[:, k * 128:(k + 1) * 128], in_=mb[:, k * 128:(k + 1) * 128], identity=identb[:]), reads=[r_mb, r_const], writes=[r_ptm], signal=(k == 7))
                    evac(mT[:, :, b2 * 128:(b2 + 1) * 128], ptm[:].rearrange("p (k t) -> p k t", k=8), [r_ptm], [r_mT])
                for hh in range(4):
                    pi = hh % 2
                    for k in range(8):
                        K.op("pe", lambda e, k=k, hh=hh, pi=pi: e.matmul(pkm[pi][:, 0:256], lhsT=Wkv[:, k, hh * 128:(hh + 1) * 128], rhs=mT[:, k, :], start=(k == 0), stop=(k == 7)),
                             reads=[r_Wkv, r_mT], writes=[r_pkm[pi]], signal=(k == 7))
                    evac(kmT[:, hh, :], pkm[pi][:, 0:256], [r_pkm[pi]], [r_km])
                for b2 in range(2):
                    pi = b2 % 2
                    for k in range(8):
                        K.op("pe", lambda e, k=k, b2=b2, pi=pi: e.matmul(pkm[pi][:], lhsT=mT[:, k, b2 * 128:(b2 + 1) * 128], rhs=Wkv[:, k, 512:1024], start=(k == 0), stop=(k == 7)),
                             reads=[r_Wkv, r_mT], writes=[r_pkm[pi]], signal=(k == 7))
                    evac(vmem[:, b2, :], pkm[pi][:], [r_pkm[pi]], [r_km])
                K.barrier()

            wcache = {}

            def load_w(st, name, src, rows, cols, c0=0):
                nk = rows // 128
                t = sb(st, name, [128, nk, cols], BF16)
                r = Res(name)
                if name in wb:
                    K.dma("sp", t[:], wb[name].rearrange("(k p) c -> p k c", p=128), writes=[r], dres=r)
                    return t, r
                if name in wcache:
                    K.dma("sp", t[:], wcache[name].rearrange("(k p) c -> p k c", p=128), writes=[r], dres=r)
                    return t, r
                for k in range(nk):
                    cast_load(lambda cc, n, k=k: t[:, k, cc:cc + n], r, src, k * 128, cols, c0)
                if not debug:
                    wcache[name] = dscr("wc_" + name, [rows, cols])
                    K.dma("act", wcache[name].rearrange("(k p) c -> p k c", p=128), t[:], reads=[r], dres=Res())
                return t, r

            NH = 1024
            for half in range(2):
                tok0 = half * NH
                with contextlib.ExitStack() as ph:
                    mixed = sb(ph, "mixed", [128, 8, NH], BF16); r_mixed = Res()
                    with contextlib.ExitStack() as pa_:
                        uTh = sb(pa_, "uTh", [128, 8, NH], BF16); r_uTh = Res()
                        for k in range(8):
                            K.dma("sp", uTh[:, k, :], uTo[k * 128:(k + 1) * 128, tok0:tok0 + NH], writes=[r_uTh] if k == 0 else [], dres=r_uTh)
                        r_uTh.w = (r_uTh.dsem, K.cnt[r_uTh.dsem])
                        Wglu, r_Wglu = load_w(pa_, "Wglu", w_glu, 512, 2048)
                        Wfo, r_Wfo = load_w(pa_, "Wfo", w_fox_o, 512, D)
                        Wg, r_Wg = load_w(pa_, "Wg", w_in, D, 2048, c0=2056)
                        pA = [ps(pa_, "pA%d" % i, [128, 512]) for i in range(2)]; r_pA = [Res() for _ in range(2)]
                        pB = [ps(pa_, "pB%d" % i, [128, 512]) for i in range(2)]; r_pB = [Res() for _ in range(2)]
                        pG = [ps(pa_, "pG%d" % i, [128, 512]) for i in range(2)]; r_pG = [Res() for _ in range(2)]
                        sg = [sb(pa_, "sg%d" % i, [128, 512]) for i in range(2)]; r_sg = [Res() for _ in range(2)]
                        oa = [sb(pa_, "oa%d" % i, [128, 512]) for i in range(2)]; r_oa = [Res() for _ in range(2)]
                        ob = [sb(pa_, "ob%d" % i, [128, 512]) for i in range(2)]; r_ob = [Res() for _ in range(2)]
                        cnt = 0
                        for ft in range(8):
                            for ch in range(2):
                                x = cnt % 2
                                cnt += 1
                                tsl = slice(tok0 + ch * 512, tok0 + (ch + 1) * 512)
                                lsl = slice(ch * 512, (ch + 1) * 512)
                                for k in range(4):
                                    K.op("pe", lambda e, k=k, x=x, ft=ft, tsl=tsl: e.matmul(pA[x][:], lhsT=Wglu[:, k, ft * 128:(ft + 1) * 128], rhs=yfm[:, k, tsl], start=(k == 0), stop=(k == 3)),
                                         reads=[r_Wglu, r_yfm], writes=[r_pA[x]], signal=(k == 3))
                                for k in range(4):
                                    K.op("pe", lambda e, k=k, x=x, ft=ft, tsl=tsl: e.matmul(pB[x][:], lhsT=Wglu[:, k, D + ft * 128:D + (ft + 1) * 128], rhs=yfm[:, k, tsl], start=(k == 0), stop=(k == 3)),
                                         reads=[r_Wglu, r_yfm], writes=[r_pB[x]], signal=(k == 3))
                                K.op("act", lambda e, x=x: e.activation(out=sg[x][:], in_=pB[x][:], func=AF.Sigmoid), reads=[r_pB[x]], writes=[r_sg[x]])
                                K.op("dve", lambda e, x=x: e.tensor_tensor(out=oa[x][:], in0=pA[x][:], in1=sg[x][:], op=ALU.mult), reads=[r_pA[x], r_sg[x]], writes=[r_oa[x]])
                                for k in range(8):
                                    K.op("pe", lambda e, k=k, x=x, ft=ft, lsl=lsl: e.matmul(pG[x][:], lhsT=Wg[:, k, ft * 128:(ft + 1) * 128], rhs=uTh[:, k, lsl], start=(k == 0), stop=(k == 7)),
                                         reads=[r_Wg, r_uTh], writes=[r_pG[x]], signal=(k == 7))
                                K.op("act", lambda e, x=x: e.activation(out=sg[x][:], in_=pG[x][:], func=AF.Sigmoid), reads=[r_pG[x], r_oa[x]], writes=[r_sg[x]])
                                K.op("pool", lambda e, x=x: e.tensor_tensor(out=oa[x][:], in0=oa[x][:], in1=sg[x][:], op=ALU.mult), reads=[r_oa[x], r_sg[x]], writes=[r_oa[x]])
                                for k in range(4):
                                    K.op("pe", lambda e, k=k, x=x, ft=ft, tsl=tsl: e.matmul(pA[x][:], lhsT=Wfo[:, k, ft * 128:(ft + 1) * 128], rhs=attT[:, k, tsl], start=(k == 0), stop=(k == 3)),
                                         reads=[r_Wfo, r_att], writes=[r_pA[x]], signal=(k == 3))
                                for k in range(8):
                                    K.op("pe", lambda e, k=k, x=x, ft=ft, lsl=lsl: e.matmul(pG[x][:], lhsT=Wg[:, k, D + ft * 128:D + (ft + 1) * 128], rhs=uTh[:, k, lsl], start=(k == 0), stop=(k == 7)),
                                         reads=[r_Wg, r_uTh], writes=[r_pG[x]], signal=(k == 7))
                                K.op("act", lambda e, x=x: e.activation(out=sg[x][:], in_=pG[x][:], func=AF.Sigmoid), reads=[r_pG[x], r_oa[x]], writes=[r_sg[x]])
                                K.op("dve", lambda e, x=x: e.tensor_tensor(out=ob[x][:], in0=pA[x][:], in1=sg[x][:], op=ALU.mult), reads=[r_pA[x], r_sg[x]], writes=[r_ob[x]])
                                K.op("pool", lambda e, x=x, ft=ft, lsl=lsl: e.tensor_tensor(out=mixed[:, ft, lsl], in0=oa[x][:], in1=ob[x][:], op=ALU.add), reads=[r_oa[x], r_ob[x]], writes=[r_mixed])
                        K.barrier()

                    hres = sb(ph, "hres", [128, 8, D]); r_h = [Res() for _ in range(8)]
                    for bb in range(8):
                        gb = 4 * (8 * half + bb) + 3
                        K.dma("sp", hres[:, bb, :], xp[gb * 128:(gb + 1) * 128, :], writes=[r_h[bb]], dres=r_h[bb])

                    def norm_T(st, tag, gain_src, nT, r_nT):
                        gt = sb(st, tag + "g", [128, D]); r_gt = Res()
                        K.dma("sp", gt[:], gain_src[0:1, :].partition_broadcast(128), writes=[r_gt], dres=r_gt)
                        junk = sb(st, tag + "junk", [128, D], BF16); r_junk = Res()
                        ss8 = sb(st, tag + "ss8", [128, 8]); r_ss8 = Res()
                        rs8 = sb(st, tag + "rs8", [128, 8]); r_rs8 = Res()
                        nb = [sb(st, tag + "nb%d" % i, [128, D], BF16) for i in range(2)]; r_nb = [Res() for _ in range(2)]
                        ptn = [ps(st, tag + "pt%d" % i, [128, 1024], BF16) for i in range(2)]; r_ptn = [Res() for _ in range(2)]
                        for bb in range(8):
                            K.op("act", lambda e, bb=bb: e.activation(out=junk[:], in_=hres[:, bb, :], func=AF.Square, accum_out=ss8[:, bb:bb + 1]), reads=[r_h[bb]], writes=[r_junk, r_ss8])
                        K.op("dve", lambda e: e.tensor_scalar(out=rs8[:], in0=ss8[:], scalar1=1.0 / D, scalar2=EPS, op0=ALU.mult, op1=ALU.add), reads=[r_ss8], writes=[r_rs8])
                        K.op("act", lambda e: e.activation(out=rs8[:], in_=rs8[:], func=AF.Sqrt), reads=[r_rs8], writes=[r_rs8])
                        K.op("dve", lambda e: e.reciprocal(out=rs8[:], in_=rs8[:]), reads=[r_rs8], writes=[r_rs8])
                        for bb in range(8):
                            x = bb % 2
                            K.op("dve", lambda e, bb=bb, x=x: e.scalar_tensor_tensor(out=nb[x][:], in0=hres[:, bb, :], scalar=rs8[:, bb:bb + 1], in1=gt[:], op0=ALU.mult, op1=ALU.mult),
                                 reads=[r_h[bb], r_rs8, r_gt], writes=[r_nb[x]])
                            for k in range(8):
                                K.op("pe", lambda e, k=k, x=x: e.transpose(out=ptn[x][:, k * 128:(k + 1) * 128], in_=nb[x][:, k * 128:(k + 1) * 128], identity=identb[:]),
                                     reads=[r_nb[x], r_const], writes=[r_ptn[x]], signal=(k == 7))
                            evac(nT[:, :, bb * 128:(bb + 1) * 128], ptn[x][:].rearrange("p (k t) -> p k t", k=8), [r_ptn[x]], [r_nT])

                    def proj_add(st, tag, actT, r_actT, nk, W, r_W):
                        pr = [ps(st, tag + "pr%d" % i, [128, 512]) for i in range(2)]; r_pr = [Res() for _ in range(2)]
                        c = 0
                        for bb in range(8):
                            for cc in range(2):
                                x = c % 2
                                c += 1
                                for k in range(nk):
                                    K.op("pe", lambda e, k=k, x=x, bb=bb, cc=cc: e.matmul(pr[x][:], lhsT=actT[:, k, bb * 128:(bb + 1) * 128], rhs=W[:, k, cc * 512:(cc + 1) * 512], start=(k == 0), stop=(k == nk - 1)),
                                         reads=[r_actT, r_W], writes=[r_pr[x]], signal=(k == nk - 1))
                                K.op("dve", lambda e, x=x, bb=bb, cc=cc: e.tensor_tensor(out=hres[:, bb, cc * 512:(cc + 1) * 512], in0=pr[x][:], in1=hres[:, bb, cc * 512:(cc + 1) * 512], op=ALU.add),
                                     reads=[r_pr[x], r_h[bb]], writes=[r_h[bb]])

                    with contextlib.ExitStack() as pb_:
                        Wmx, r_Wmx = load_w(pb_, "Wmx", w_mix, D, D)
                        proj_add(pb_, "mx", mixed, r_mixed, 8, Wmx, r_Wmx)
                        K.barrier()
                    with contextlib.ExitStack() as pc_:
                        nT = sb(pc_, "nT", [128, 8, NH], BF16); r_nT = Res()
                        Wmq, r_Wmq = load_w(pc_, "Wmq", w_mem_q, D, 512)
                        Wmo, r_Wmo = load_w(pc_, "Wmo", w_mem_o, 512, D)
                        with contextlib.ExitStack() as pn_:
                            norm_T(pn_, "n5", g_memq, nT, r_nT)
                            K.barrier()
                        qm = sb(pc_, "qm", [128, 4, NH], BF16); r_qm = Res()
                        om = sb(pc_, "om", [128, 4, NH], BF16); r_om = Res()
                        with contextlib.ExitStack() as pq_:
                            pq = [ps(pq_, "pq%d" % i, [128, 512]) for i in range(2)]; r_pq = [Res() for _ in range(2)]
                            pS = [ps(pq_, "pS%d" % i, [128, 512]) for i in range(2)]; r_pS = [Res() for _ in range(2)]
                            pO = [ps(pq_, "pO%d" % i, [128, 512]) for i in range(2)]; r_pO = [Res() for _ in range(2)]
                            pD = [ps(pq_, "pD%d" % i, [128, 512]) for i in range(2)]; r_pD = [Res() for _ in range(2)]
                            pe_ = [[sb(pq_, "pe%d_%d" % (i, j_), [128, 512], BF16) for j_ in range(2)] for i in range(2)]
                            r_pe = [[Res() for _ in range(2)] for _ in range(2)]
                            rc = [sb(pq_, "rc%d" % i, [128, 512]) for i in range(2)]; r_rc = [Res() for _ in range(2)]
                            r_qmi = [Res() for _ in range(8)]
                            its = [(hh, ch) for hh in range(4) for ch in range(2)]

                            def st_Q(it):
                                hh, ch = its[it]
                                x = it % 2
                                lsl = slice(ch * 512, (ch + 1) * 512)
                                for k in range(8):
                                    K.op("pe", lambda e, k=k: e.matmul(pq[x][:], lhsT=Wmq[:, k, hh * 128:(hh + 1) * 128], rhs=nT[:, k, lsl], start=(k == 0), stop=(k == 7)),
                                         reads=[r_Wmq, r_nT], writes=[r_pq[x]], signal=(k == 7))
                                evac(qm[:, hh, lsl], pq[x][:], [r_pq[x]], [r_qmi[it]], scale=1.0 / math.sqrt(128.0))

                            def st_S(it):
                                hh, ch = its[it]
                                x = it % 2
                                lsl = slice(ch * 512, (ch + 1) * 512)
                                for mt_ in range(2):
                                    K.op("pe", lambda e, mt_=mt_: e.matmul(pS[mt_][:], lhsT=kmT[:, hh, mt_ * 128:(mt_ + 1) * 128], rhs=qm[:, hh, lsl], start=True, stop=True),
                                         reads=[r_km, r_qmi[it]], writes=[r_pS[mt_]])
                                    K.op("act", lambda e, mt_=mt_: e.activation(out=pe_[x][mt_][:], in_=pS[mt_][:], func=AF.Exp), reads=[r_pS[mt_]], writes=[r_pe[x][mt_]])

                            def st_O(it):
                                hh, ch = its[it]
                                x = it % 2
                                lsl = slice(ch * 512, (ch + 1) * 512)
                                for mt_ in range(2):
                                    K.op("pe", lambda e, mt_=mt_: e.matmul(pO[x][:], lhsT=vmem[:, mt_, hh * 128:(hh + 1) * 128], rhs=pe_[x][mt_][:], start=(mt_ == 0), stop=(mt_ == 1)),
                                         reads=[r_km, r_pe[x][mt_]], writes=[r_pO[x]], signal=(mt_ == 1))
                                for mt_ in range(2):
                                    K.op("pe", lambda e, mt_=mt_: e.matmul(pD[x][:], lhsT=onesb[:], rhs=pe_[x][mt_][:], start=(mt_ == 0), stop=(mt_ == 1)),
                                         reads=[r_pe[x][mt_]], writes=[r_pD[x]], signal=(mt_ == 1))
                                K.op("dve", lambda e: e.reciprocal(out=rc[x][:], in_=pD[x][:]), reads=[r_pD[x]], writes=[r_rc[x]])
                                K.op("dve", lambda e: e.tensor_tensor(out=om[:, hh, lsl], in0=pO[x][:], in1=rc[x][:], op=ALU.mult), reads=[r_pO[x], r_rc[x]], writes=[r_om])

                            st_Q(0)
                            st_Q(1)
                            for it in range(8):
                                st_S(it)
                                if it + 2 < 8:
                                    st_Q(it + 2)
                                st_O(it)
                            K.barrier()
                        with contextlib.ExitStack() as po_:
                            proj_add(po_, "mo", om, r_om, 4, Wmo, r_Wmo)
                            K.barrier()
                    with contextlib.ExitStack() as pf_:
                        hid = sb(pf_, "hid", [128, 22, NH], BF16); r_hid = Res()
                        with contextlib.ExitStack() as pg_:
                            n2T = sb(pg_, "n2T", [128, 8, NH], BF16); r_n2T = Res()
                            with contextlib.ExitStack() as pn_:
                                norm_T(pn_, "n6", g_ffn, n2T, r_n2T)
                                K.barrier()
                            wfa = [sb(pg_, "wfa%d" % i, [128, 8, 2, 512], BF16) for i in range(2)]; r_wfa = [Res() for _ in range(2)]
                            pfa = [ps(pg_, "pfa%d" % i, [128, 512]) for i in range(2)]; r_pfa = [Res() for _ in range(2)]
                            pfb = [ps(pg_, "pfb%d" % i, [128, 512]) for i in range(2)]; r_pfb = [Res() for _ in range(2)]
                            sl_ = [sb(pg_, "sl%d" % i, [128, 512]) for i in range(2)]; r_sl = [Res() for _ in range(2)]
                            NG = 6

                            def load_ff(g):
                                w = g % 2
                                nt_ = min(4, 22 - 4 * g)
                                key = "ffin%d" % g
                                if "Wfi" in wb:
                                    wv_ = wb["Wfi"].rearrange("(k p) c -> p k c", p=128)
                                    for ab in range(2):
                                        K.dma("sp", wfa[w][:, :, ab, 0:nt_ * 128], wv_[:, :, ab * 2816 + g * 512:ab * 2816 + g * 512 + nt_ * 128], writes=[r_wfa[w]], dres=r_wfa[w])
                                    return
                                if key in wcache:
                                    K.dma("sp", wfa[w][:], wcache[key][:, :, :, :], writes=[r_wfa[w]], dres=r_wfa[w])
                                    return
                                for k in range(8):
                                    for ab in range(2):
                                        cast_load(lambda cc, n, k=k, ab=ab, w=w: wfa[w][:, k, ab, cc:cc + n], r_wfa[w], w_ffn_in, k * 128, nt_ * 128, ab * 2816 + g * 512)
                                if not debug:
                                    wcache[key] = dscr("wc_" + key, [128, 8, 2, 512])
                                    K.dma("act", wcache[key][:, :, :, :], wfa[w][:], reads=[r_wfa[w]], dres=Res())

                            load_ff(0)
                            c = 0
                            for g in range(NG):
                                w = g % 2
                                if g + 1 < NG:
                                    load_ff(g + 1)
                                for hi in range(min(4, 22 - 4 * g)):
                                    ht = 4 * g + hi
                                    for ch in range(2):
                                        x = c % 2
                                        c += 1
                                        lsl = slice(ch * 512, (ch + 1) * 512)
                                        for k in range(8):
                                            K.op("pe", lambda e, k=k, x=x, w=w, lsl=lsl, hi=hi: e.matmul(pfa[x][:], lhsT=wfa[w][:, k, 0, hi * 128:(hi + 1) * 128], rhs=n2T[:, k, lsl], start=(k == 0), stop=(k == 7)),
                                                 reads=[r_wfa[w], r_n2T], writes=[r_pfa[x]], signal=(k == 7))
                                        for k in range(8):
                                            K.op("pe", lambda e, k=k, x=x, w=w, lsl=lsl, hi=hi: e.matmul(pfb[x][:], lhsT=wfa[w][:, k, 1, hi * 128:(hi + 1) * 128], rhs=n2T[:, k, lsl], start=(k == 0), stop=(k == 7)),
                                                 reads=[r_wfa[w], r_n2T], writes=[r_pfb[x]], signal=(k == 7))
                                        K.op("act", lambda e, x=x: e.activation(out=sl_[x][:], in_=pfa[x][:], func=AF.Silu), reads=[r_pfa[x]], writes=[r_sl[x]])
                                        K.op("dve", lambda e, x=x, ht=ht, lsl=lsl: e.tensor_tensor(out=hid[:, ht, lsl], in0=pfb[x][:], in1=sl_[x][:], op=ALU.mult), reads=[r_pfb[x], r_sl[x]], writes=[r_hid])
                            K.barrier()
                        with contextlib.ExitStack() as po_:
                            Wfo2, r_Wfo2 = load_w(po_, "Wfo2", w_ffn_out, 2816, D)
                            proj_add(po_, "fo", hid, r_hid, 22, Wfo2, r_Wfo2)
                            K.barrier()
                    with contextlib.ExitStack() as pz_:
                        gt = sb(pz_, "gfin", [128, D]); r_gt = Res()
                        K.dma("sp", gt[:], g_fin[0:1, :].partition_broadcast(128), writes=[r_gt], dres=r_gt)
                        junk = sb(pz_, "junkz", [128, D], BF16); r_junk = Res()
                        ssz = [sb(pz_, "ssz%d" % i, [128, 1]) for i in range(2)]; r_ssz = [Res() for _ in range(2)]
                        rsz = [sb(pz_, "rsz%d" % i, [128, 1]) for i in range(2)]; r_rsz = [Res() for _ in range(2)]
                        ot = [sb(pz_, "ot%d" % i, [128, D]) for i in range(2)]; r_ot = [Res() for _ in range(2)]
                        r_out = Res("out")
                        for bb in range(8):
                            x = bb % 2
                            rmsnorm_tok("fz", hres[:, bb, :], r_h[bb], gt[:], r_gt, ot[x][:], r_ot[x], junk[:], r_junk, ssz[x][:], r_ssz[x], rsz[x][:], r_rsz[x])
                            m = 8 * half + bb
                            K.dma("sp", out[m * 128:(m + 1) * 128, :], ot[x][:], reads=[r_ot[x]], dres=r_ot[x])
                        K.barrier()
        except _Stop:
            pass
        K.barrier()
    return nc, K.rec


_CACHE = {}


def _consts():
    c = {}
    c["c_idx"] = np.tile(np.arange(512, dtype=np.float32)[None, :], (128, 1))
    c["c_tau"] = np.tile(np.arange(1, 129, dtype=np.float32)[None, :], (128, 1))
    rs = np.ones((128, 512), np.float32)
    rs[:, 0::128] = 0.0
    c["c_reset"] = rs
    c["c_ident"] = np.eye(128, dtype=np.float32)
    s = np.arange(128)
    c["c_triu"] = (s[:, None] <= s[None, :]).astype(np.float32)
    e = np.zeros((128, 128), np.float32)
    e[127, :] = 1.0
    c["c_e127"] = e
    c["c_tmask"] = np.where(s[:, None] <= s[None, :], 0.0, NEG).astype(np.float32)
    sel = np.zeros((128, 128), np.float32)
    sel[0, :] = 1.0
    sel[64, 0:64] = 1.0
    c["c_sel"] = sel
    return c


def _prep(x, mem, norm_mix, w_in, b_forget, lam_re, lam_im, log_dt, b_re, b_im, c_re, c_im,
          d_skip, w_glu, w_fox_o, w_mix_out, norm_mem_q, norm_mem_kv, w_mem_q, w_mem_kv,
          w_mem_o, norm_ffn, w_ffn_in, w_ffn_out, norm_final):
    f = lambda a: np.ascontiguousarray(np.asarray(a, dtype=np.float32))
    x = f(x); mem = f(mem)
    lam_re = f(lam_re)[0]; lam_im = f(lam_im)[0]; log_dt = f(log_dt)[0]
    b_re = f(b_re)[0]; b_im = f(b_im)[0]; c_re = f(c_re)[0]; c_im = f(c_im)[0]; d_skip = f(d_skip)[0]
    lamr_c = np.full((96, 6, 128), -1.0, np.float32); lami_c = np.zeros((96, 6, 128), np.float32)
    ldt_c = np.zeros((96, 6, 128), np.float32)
    Br_c = np.zeros((96, 6, 128), np.float32); Bi_c = np.zeros((96, 6, 128), np.float32)
    Dm = np.zeros((96, 6, 32), np.float32)
    lamr_r = np.zeros((128, 16), np.float32); lami_r = np.zeros((128, 16), np.float32); ldt_r = np.zeros((128, 16), np.float32)
    Cr_r = np.zeros((128, 16, 32), np.float32); Ci_r = np.zeros((128, 16, 32), np.float32)
    Brow_r = np.zeros((128, 16, 32), np.float32); Brow_i = np.zeros((128, 16, 32), np.float32)
    for q in range(16):
        sl, kk = q // 3, q % 3
        for g2 in range(2):
            g = 2 * q + g2
            lamr_c[32 * kk:32 * kk + 32, sl, 64 * g2:64 * g2 + 64] = lam_re[g][None, :]
            lami_c[32 * kk:32 * kk + 32, sl, 64 * g2:64 * g2 + 64] = lam_im[g][None, :]
            ldt_c[32 * kk:32 * kk + 32, sl, 64 * g2:64 * g2 + 64] = log_dt[g]
            Br_c[32 * kk + 16 * g2:32 * kk + 16 * g2 + 16, sl, 64 * g2:64 * g2 + 64] = b_re[g].T
            Bi_c[32 * kk + 16 * g2:32 * kk + 16 * g2 + 16, sl, 64 * g2:64 * g2 + 64] = b_im[g].T
            for m in range(16):
                Dm[32 * kk + 16 * g2 + m, sl, 16 * g2 + m] = d_skip[16 * g + m]
            lamr_r[64 * g2:64 * g2 + 64, q] = lam_re[g]
            lami_r[64 * g2:64 * g2 + 64, q] = lam_im[g]
            ldt_r[64 * g2:64 * g2 + 64, q] = log_dt[g]
            Cr_r[64 * g2:64 * g2 + 64, q, 16 * g2:16 * g2 + 16] = c_re[g].T
            Ci_r[64 * g2:64 * g2 + 64, q, 16 * g2:16 * g2 + 16] = c_im[g].T
            Brow_r[64 * g2:64 * g2 + 64, q, 16 * g2:16 * g2 + 16] = b_re[g]
            Brow_i[64 * g2:64 * g2 + 64, q, 16 * g2:16 * g2 + 16] = b_im[g]
    shared = dict(
        w_in=f(w_in)[0], g_mix=f(norm_mix), g_memq=f(norm_mem_q), g_memkv=f(norm_mem_kv), g_ffn=f(norm_ffn),
        g_fin=f(norm_final).reshape(1, D), b_forget=f(b_forget),
        lamr_c=lamr_c.reshape(96, 768), lami_c=lami_c.reshape(96, 768), ldt_c=ldt_c.reshape(96, 768),
        Br_c=Br_c.reshape(96, 768), Bi_c=Bi_c.reshape(96, 768),
        lamr_r=lamr_r, lami_r=lami_r, ldt_r=ldt_r, Cr_r=Cr_r.reshape(128, 512), Ci_r=Ci_r.reshape(128, 512),
        Dm=Dm.reshape(96, 192), Brow_r=Brow_r.reshape(128, 512), Brow_i=Brow_i.reshape(128, 512),
        w_glu=f(w_glu)[0], w_fox_o=f(w_fox_o)[0], w_mix=f(w_mix_out)[0], w_mem_q=f(w_mem_q)[0], w_mem_kv=f(w_mem_kv)[0],
        w_mem_o=f(w_mem_o)[0], w_ffn_in=f(w_ffn_in)[0], w_ffn_out=f(w_ffn_out)[0],
    )
    shared.update(_consts())
    in_maps = []
    for c in range(8):
        b, j = c // 4, c % 4
        npad = (3 - j) * 128
        xpad = np.zeros((NT, D), np.float32)
        xpad[npad:] = x[b, :NT - npad]
        pr = np.zeros((1, NT), np.float32)
        pr[0, :npad] = NEG
        m = dict(shared)
        m["xp"] = xpad
        m["padrow"] = pr
        m["mem"] = mem[b]
        in_maps.append(m)
    return in_maps


def kernel(**inputs):
    in_maps = _prep(**inputs)
    if "nc" not in _CACHE:
        _CACHE["nc"] = build()
    res = run_bass_kernel_spmd(_CACHE["nc"], in_maps, core_ids=list(range(8)))
    outp = np.zeros((2, 8192, D), np.float32)
    for c in range(8):
        b, j = c // 4, c % 4
        o = np.asarray(res.results[c]["out"]).reshape(16, 128, D)
        for m in range(16):
            gblk = 4 * m + j
            outp[b, gblk * 128:(gblk + 1) * 128, :] = o[m]
    return outp
```

```python
import contextlib
import os
import math
import numpy as np
import concourse.bass as bass
import concourse.mybir as mybir
from concourse.bass_utils import run_bass_kernel_spmd

F32 = mybir.dt.float32
BF16 = mybir.dt.bfloat16
I32 = mybir.dt.int32
AF = mybir.ActivationFunctionType
ALU = mybir.AluOpType

D = 1024
NB = 64
NT = NB * 128
NOWN = 2048
EPS = 1e-6
NEG = -30000.0
TWO_PI = 2.0 * math.pi


class _Stop(Exception):
    pass


class Res:
    __slots__ = ("w", "r", "dsem", "dcnt", "name")

    def __init__(self, name=""):
        self.w = None
        self.r = {}
        self.dsem = None
        self.dcnt = 0
        self.name = name


class Ker:
    def __init__(self, nc, es, needed=None):
        self.nc = nc
        self.es = es
        self.needed = needed
        self.rec = set()
        self.pcnt = {}
        self.pmap = {}
        self.eng = {"pe": nc.tensor, "act": nc.scalar, "dve": nc.vector, "pool": nc.gpsimd, "sp": nc.sync}
        self.sem = {}
        self.cnt = {}
        for e in ("pe", "act", "dve", "pool"):
            self.sem[e] = es.enter_context(nc.semaphore("s_" + e))
            self.cnt[e] = 0
            self.pcnt[e] = 0
        self.waited = {e: {} for e in self.eng}
        self.free_d = []
        self.phase_res = []
        self.ndsem = 0

    def new_dsem(self):
        if self.free_d:
            return self.free_d.pop()
        s = self.es.enter_context(self.nc.semaphore("d%d" % self.ndsem))
        key = "d%d" % self.ndsem
        self.ndsem += 1
        self.sem[key] = s
        self.cnt[key] = 0
        return key

    def _need(self, e, tok, needs):
        if tok is None:
            return
        k, v = tok
        if k == "pe" and e == "pe":
            return
        if needs.get(k, 0) < v:
            needs[k] = v

    def _waits(self, e, reads, writes, skip_key=None):
        needs = {}
        for r in reads:
            self._need(e, r.w, needs)
            for k, v in r.r.items():
                if k != e:
                    self._need(e, (k, v), needs)
        for r in writes:
            self._need(e, r.w, needs)
            for k, v in r.r.items():
                self._need(e, (k, v), needs)
        wd = self.waited[e]
        for k, v in needs.items():
            if k == skip_key:
                continue
            if wd.get(k, 0) < v:
                self._emit_wait(e, k, v)
                wd[k] = v

    def _emit_wait(self, e, k, v):
        if k in self.pcnt:
            self.rec.add((k, v))
            pv = v if self.needed is None else self.pmap[(k, v)]
        else:
            pv = v
        self.eng[e].wait_ge(self.sem[k], pv)

    def op(self, e, fn, reads=(), writes=(), signal=True):
        self._waits(e, reads, writes)
        ins = fn(self.eng[e])
        if signal:
            self.cnt[e] += 1
            if self.needed is None or (e, self.cnt[e]) in self.needed:
                self.pcnt[e] += 1
                self.pmap[(e, self.cnt[e])] = self.pcnt[e]
                ins.then_inc(self.sem[e], 1)
            tok = (e, self.cnt[e])
        else:
            tok = (e, self.cnt[e] + 1)
        for r in writes:
            r.w = tok
            r.r = {}
        for r in reads:
            if r.r.get(tok[0], 0) < tok[1]:
                r.r[tok[0]] = tok[1]
        return tok

    def dma(self, q, out, in_, reads=(), writes=(), dres=None):
        if dres.dsem is None:
            dres.dsem = self.new_dsem()
            self.phase_res.append(dres)
        k = dres.dsem
        self._waits(q, reads, writes, skip_key=k)
        self.eng[q].dma_start(out=out, in_=in_).then_inc(self.sem[k], 16)
        self.cnt[k] += 16
        tok = (k, self.cnt[k])
        for r in writes:
            r.w = tok
            r.r = {}
        for r in reads:
            if r.r.get(tok[0], 0) < tok[1]:
                r.r[tok[0]] = tok[1]
        return tok

    def barrier(self):
        for e in self.eng:
            wd = self.waited[e]
            for k, v in self.cnt.items():
                if v > 0 and wd.get(k, 0) < v:
                    self._emit_wait(e, k, v)
                    wd[k] = v
        for r in self.phase_res:
            self.free_d.append(r.dsem)
            r.dsem = None
        self.phase_res = []


def build(stage=9, debug=False):
    _, rec = _build(stage, debug, None)
    nc, _ = _build(stage, debug, rec)
    return nc


def _build(stage, debug, needed):
    nc = bass.Bass("TRN2", target_bir_lowering=False)

    def din(name, shape, dt=F32):
        return nc.dram_tensor(name, list(shape), dt, kind="ExternalInput").ap()

    xp = din("xp", [NT, D])
    padrow = din("padrow", [1, NT])
    mem = din("mem", [256, D])
    w_in = din("w_in", [D, 4104])
    g_mix = din("g_mix", [1, D]); g_memq = din("g_memq", [1, D]); g_memkv = din("g_memkv", [1, D])
    g_ffn = din("g_ffn", [1, D]); g_fin = din("g_fin", [1, D])
    b_forget = din("b_forget", [1, 8])
    lamr_c = din("lamr_c", [96, 768]); lami_c = din("lami_c", [96, 768]); ldt_c = din("ldt_c", [96, 768])
    Br_c = din("Br_c", [96, 768]); Bi_c = din("Bi_c", [96, 768])
    lamr_r = din("lamr_r", [128, 16]); lami_r = din("lami_r", [128, 16]); ldt_r = din("ldt_r", [128, 16])
    Cr_r = din("Cr_r", [128, 512]); Ci_r = din("Ci_r", [128, 512])
    Dm = din("Dm", [96, 192])
    Brow_r = din("Brow_r", [128, 512]); Brow_i = din("Brow_i", [128, 512])
    w_glu = din("w_glu", [512, 2048]); w_fox_o = din("w_fox_o", [512, D]); w_mix = din("w_mix", [D, D])
    w_mem_q = din("w_mem_q", [D, 512]); w_mem_kv = din("w_mem_kv", [D, D]); w_mem_o = din("w_mem_o", [512, D])
    w_ffn_in = din("w_ffn_in", [D, 5632]); w_ffn_out = din("w_ffn_out", [2816, D])
    c_idx = din("c_idx", [128, 512]); c_tau = din("c_tau", [128, 128]); c_reset = din("c_reset", [128, 512])
    c_ident = din("c_ident", [128, 128]); c_triu = din("c_triu", [128, 128]); c_e127 = din("c_e127", [128, 128])
    c_tmask = din("c_tmask", [128, 128]); c_sel = din("c_sel", [128, 128])

    out = nc.dram_tensor("out", [NOWN, D], F32, kind="ExternalOutput").ap()

    def dscr(name, shape, dt=BF16):
        return nc.dram_tensor(name, list(shape), dt, kind="ExternalOutput" if debug else "Internal").ap()

    kT = dscr("kT", [8, 71, NT])
    qT = dscr("qT", [8, 71, NOWN])
    vA = dscr("vA", [8, NT, 128])
    usT = dscr("usT", [512, 16, 512])
    uTo = dscr("uTo", [D, NOWN])

    with contextlib.ExitStack() as es:
        K = Ker(nc, es, needed)
        dbg = {}
        try:

            uid = [0]

            def sb(st, name, shape, dt=F32):
                uid[0] += 1
                return st.enter_context(nc.sbuf_tensor("%s_%d" % (name, uid[0]), list(shape), dt))

            def ps(st, name, shape, dt=F32):
                uid[0] += 1
                return st.enter_context(nc.psum_tensor("%s_%d" % (name, uid[0]), list(shape), dt))

            identf = sb(es, "identf", [128, 128]); identb = sb(es, "identb", [128, 128], BF16)
            triu = sb(es, "triu", [128, 128]); e127 = sb(es, "e127", [128, 128])
            tmaskb = sb(es, "tmaskb", [128, 128], BF16)
            self_ = sb(es, "self_", [128, 128])
            onesb = sb(es, "onesb", [128, 128], BF16)
            halfpi = sb(es, "halfpi", [128, 1])
            epsc = sb(es, "epsc", [128, 1])
            yfm = sb(es, "yfm", [128, 4, NOWN], BF16)
            r_const = Res("const")
            r_yfm = Res("yfm"); r_att = Res("att")
            r_scr = {n: Res(n) for n in ("kT", "qT", "vA", "usT", "uTo")}

            ld = Res("ldc")
            for t_, src in ((identf, c_ident), (triu, c_triu), (e127, c_e127), (self_, c_sel)):
                K.dma("sp", t_[:], src[:, :], writes=[r_const], dres=ld)
            tmaskf = sb(es, "tmaskf", [128, 128])
            K.dma("sp", tmaskf[:], c_tmask[:, :], writes=[r_const], dres=ld)
            K.op("dve", lambda e: e.tensor_copy(out=identb[:], in_=identf[:]), reads=[r_const], writes=[Res()])
            K.op("dve", lambda e: e.tensor_copy(out=tmaskb[:], in_=tmaskf[:]), reads=[r_const], writes=[Res()])
            K.op("dve", lambda e: e.memset(onesb[:], 1.0), writes=[Res()])
            K.op("dve", lambda e: e.memset(halfpi[:], math.pi / 2), writes=[Res()])
            K.op("dve", lambda e: e.memset(epsc[:], EPS), writes=[Res()])
            K.barrier()

            NWST = 4
            wst = [sb(es, "wst%d" % i, [128, 1024]) for i in range(NWST)]
            r_wst = [Res("wst%d" % i) for i in range(NWST)]
            wst_cnt = [0]

            def cast_load(dst_fn, r_dst, src, rows0, ncols, c0, cast_eng="rr"):
                cc = 0
                while cc < ncols:
                    n = min(1024, ncols - cc)
                    si = wst_cnt[0] % NWST
                    wst_cnt[0] += 1
                    K.dma("sp", wst[si][:, 0:n], src[rows0:rows0 + 128, c0 + cc:c0 + cc + n], writes=[r_wst[si]], dres=r_wst[si])
                    dst = dst_fn(cc, n)
                    ce = ("pool", "act", "dve")[wst_cnt[0] % 3] if cast_eng == "rr" else cast_eng
                    if ce == "act":
                        K.op("act", lambda e, dst=dst, si=si, n=n: e.activation(out=dst, in_=wst[si][:, 0:n], func=AF.Copy), reads=[r_wst[si]], writes=[r_dst])
                    else:
                        K.op(ce, lambda e, dst=dst, si=si, n=n: e.tensor_copy(out=dst, in_=wst[si][:, 0:n]), reads=[r_wst[si]], writes=[r_dst])
                    cc += n

            evac_flip = [0]

            def evac(out_ap, in_ap, reads, writes, scale=None, eng=None):
                if eng is None:
                    eng = "act" if evac_flip[0] % 2 == 0 else "dve"
                    evac_flip[0] += 1
                if eng == "act":
                    if scale is None:
                        return K.op("act", lambda e: e.activation(out=out_ap, in_=in_ap, func=AF.Copy), reads=reads, writes=writes)
                    return K.op("act", lambda e: e.activation(out=out_ap, in_=in_ap, func=AF.Copy, scale=scale), reads=reads, writes=writes)
                if scale is None:
                    return K.op("dve", lambda e: e.tensor_copy(out=out_ap, in_=in_ap), reads=reads, writes=writes)
                return K.op("dve", lambda e: e.tensor_scalar(out=out_ap, in0=in_ap, scalar1=scale, scalar2=None, op0=ALU.mult), reads=reads, writes=writes)

            def rmsnorm_tok(st_tag, x_ap, r_x, gain_t, r_gain, out_bf, r_out, junk, r_junk, ss, r_ss, rstd, r_rstd):
                K.op("act", lambda e: e.activation(out=junk, in_=x_ap, func=AF.Square, accum_out=ss), reads=[r_x], writes=[r_junk, r_ss])
                K.op("dve", lambda e: e.tensor_scalar(out=rstd, in0=ss, scalar1=1.0 / D, scalar2=EPS, op0=ALU.mult, op1=ALU.add), reads=[r_ss], writes=[r_rstd])
                K.op("act", lambda e: e.activation(out=rstd, in_=rstd, func=AF.Sqrt), reads=[r_rstd], writes=[r_rstd])
                K.op("dve", lambda e: e.reciprocal(out=rstd, in_=rstd), reads=[r_rstd], writes=[r_rstd])
                K.op("dve", lambda e: e.scalar_tensor_tensor(out=out_bf, in0=x_ap, scalar=rstd, in1=gain_t, op0=ALU.mult, op1=ALU.mult),
                     reads=[r_x, r_rstd, r_gain], writes=[r_out])

            F_all = sb(es, "F_all", [128, 8, NB])
            wb = {}
            r_F = Res("F")
            with contextlib.ExitStack() as p1:
                W1 = sb(p1, "W1", [128, 8, 2056], BF16)
                r_W1 = Res("W1")
                for k in range(8):
                    cast_load(lambda cc, n, k=k: W1[:, k, cc:cc + n], r_W1, w_in, k * 128, 2056, 0)
                gmix = sb(p1, "gmix", [128, D]); r_g = Res("g")
                K.dma("sp", gmix[:], g_mix[0:1, :].partition_broadcast(128), writes=[r_g], dres=r_g)
                bfg = sb(p1, "bfg", [128, 8])
                K.dma("sp", bfg[:], b_forget[0:1, :].partition_broadcast(128), writes=[r_g], dres=r_g)
                onesrow = sb(p1, "onesrow", [3, NOWN], BF16); r_or = Res("or")
                K.op("dve", lambda e: e.memset(onesrow[:], 1.0), writes=[r_or])
                zrow = sb(p1, "zrow", [1, NOWN], BF16)
                K.op("dve", lambda e: e.memset(zrow[:], 0.0), writes=[r_or])
                padb = sb(p1, "padb", [1, 512], BF16); r_pb = Res("pb")
                padf = sb(p1, "padf", [1, 512]); r_pf_ = Res("pf_")
                K.dma("sp", padf[:], padrow[0:1, 0:512], writes=[r_pf_], dres=r_pf_)
                K.op("dve", lambda e: e.tensor_copy(out=padb[:], in_=padf[:]), reads=[r_pf_], writes=[r_pb])
                st_aug = Res("st_aug")
                for h in range(8):
                    for c4 in range(4):
                        K.dma("sp", kT[h, 67:70, c4 * NOWN:(c4 + 1) * NOWN], onesrow[:], reads=[r_or], dres=st_aug)
                    K.dma("sp", kT[h, 70:71, 0:512], padb[:], reads=[r_pb], dres=st_aug)
                    for c0_, c1_ in ((512, 2560), (2560, 4608), (4608, 6656), (6656, 8192)):
                        K.dma("sp", kT[h, 70:71, c0_:c1_], zrow[:, 0:c1_ - c0_], reads=[r_or], dres=st_aug)
                    K.dma("sp", qT[h, 64:67, :], onesrow[:], reads=[r_or], dres=st_aug)
                    K.dma("sp", qT[h, 70:71, :], onesrow[0:1, :], reads=[r_or], dres=st_aug)

                NXB = 6
                xt = [sb(p1, "xt%d" % i, [128, D]) for i in range(NXB)]
                r_xt = [Res("xt%d" % i) for i in range(NXB)]
                junk = sb(p1, "junk", [128, D], BF16); r_junk = Res("junk")
                ss = [sb(p1, "ss%d" % i, [128, 1]) for i in range(4)]; r_ss = [Res() for _ in range(4)]
                rstd = [sb(p1, "rstd%d" % i, [128, 1]) for i in range(4)]; r_rstd = [Res() for _ in range(4)]
                ub = [sb(p1, "ub%d" % i, [128, D], BF16) for i in range(4)]; r_ub = [Res() for _ in range(4)]
                uT = [sb(p1, "uT%d" % i, [128, 8, 512], BF16) for i in range(2)]; r_uT = [Res() for _ in range(2)]
                kst = [sb(p1, "kst%d" % i, [128, 512], BF16) for i in range(4)]; r_kst = [Res() for _ in range(4)]
                vst = [sb(p1, "vst%d" % i, [128, 8, 128], BF16) for i in range(3)]; r_vst = [Res() for _ in range(3)]
                qst = [sb(p1, "qst%d" % i, [128, 128], BF16) for i in range(2)]; r_qst = [Res() for _ in range(2)]
                usd = [sb(p1, "usd%d" % i, [128, 16, 128], BF16) for i in range(4)]; r_usd = [Res() for _ in range(4)]
                ptr = [ps(p1, "ptr%d" % i, [128, 1024], BF16) for i in range(2)]; r_ptr = [Res() for _ in range(2)]
                pk = [ps(p1, "pk%d" % i, [128, 512]) for i in range(3)]; r_pk = [Res() for _ in range(3)]
                pv = [ps(p1, "pv%d" % i, [128, 512]) for i in range(2)]; r_pv = [Res() for _ in range(2)]
                pf = ps(p1, "pf", [128, 8]); r_pf = Res()
                for i in range(3):
                    v_ = vst[i]
                    K.op("pool", lambda e, v_=v_: e.memset(v_[:], 0.0), writes=[r_vst[i]])
                    K.op("pool", lambda e, v_=v_: e.memset(v_[:, 0:8:2, 64:65], 1.0), writes=[r_vst[i]])
                    K.op("pool", lambda e, v_=v_: e.memset(v_[:, 1:8:2, 0:1], 1.0), writes=[r_vst[i]])

                def load_x(gb):
                    i = gb % NXB
                    K.dma("sp", xt[i][:], xp[gb * 128:(gb + 1) * 128, :], writes=[r_xt[i]], dres=r_xt[i])

                for gb in range(NXB):
                    load_x(gb)
                kcount = [0]
                vcount = [0]

                def stage_N(s):
                    for blk in range(4):
                        gb = 4 * s + blk
                        xi = gb % NXB
                        bi = gb % 4
                        rmsnorm_tok("p1", xt[xi][:], r_xt[xi], gmix[:], r_g, ub[bi][:], r_ub[bi], junk[:], r_junk,
                                    ss[bi][:], r_ss[bi], rstd[bi][:], r_rstd[bi])
                        if gb + NXB < NB:
                            load_x(gb + NXB)

                def stage_T(s):
                    ui = s % 2
                    for blk in range(4):
                        gb = 4 * s + blk
                        bi = gb % 4
                        pi_ = gb % 2
                        pt = ptr[pi_]
                        for k in range(8):
                            K.op("pe", lambda e, k=k, pt=pt, bi=bi: e.transpose(out=pt[:, k * 128:(k + 1) * 128], in_=ub[bi][:, k * 128:(k + 1) * 128], identity=identb[:]),
                                 reads=[r_ub[bi], r_const], writes=[r_ptr[pi_]], signal=(k == 7))
                        evac(uT[ui][:, :, blk * 128:(blk + 1) * 128], pt[:].rearrange("p (k t) -> p k t", k=8), [r_ptr[pi_]], [r_uT[ui]])

                def stage_M(s):
                    ui = s % 2
                    for blk in range(4):
                        gb = 4 * s + blk
                        pvi = vcount[0] % 2
                        for k in range(8):
                            K.op("pe", lambda e, k=k, pvi=pvi, blk=blk: e.matmul(pv[pvi][:], lhsT=uT[ui][:, k, blk * 128:(blk + 1) * 128], rhs=W1[:, k, 1536:2048], start=(k == 0), stop=(k == 7)),
                                 reads=[r_uT[ui], r_W1], writes=[r_pv[pvi]], signal=(k == 7))
                        vi = vcount[0] % 3
                        vcount[0] += 1
                        pvv = pv[pvi][:].rearrange("p (hp e d) -> p hp e d", hp=4, e=2)
                        vsv = vst[vi][:].rearrange("p (hp e) c -> p hp e c", e=2)
                        evac(vsv[:, :, 0, 0:64], pvv[:, :, 0, :], [r_pv[pvi]], [r_vst[vi]], eng="act")
                        evac(vsv[:, :, 1, 64:128], pvv[:, :, 1, :], [r_pv[pvi]], [r_vst[vi]], eng="dve")
                        K.dma("sp", vA[:, gb * 128:(gb + 1) * 128, :].rearrange("h t c -> t h c"), vst[vi][:], reads=[r_vst[vi]], dres=r_vst[vi])
                        for k in range(8):
                            K.op("pe", lambda e, k=k, blk=blk: e.matmul(pf[:], lhsT=uT[ui][:, k, blk * 128:(blk + 1) * 128], rhs=W1[:, k, 2048:2056], start=(k == 0), stop=(k == 7)),
                                 reads=[r_uT[ui], r_W1], writes=[r_pf], signal=(k == 7))
                        K.op("dve", lambda e, gb=gb: e.tensor_tensor(out=F_all[:, :, gb], in0=pf[:], in1=bfg[:], op=ALU.add), reads=[r_pf, r_g], writes=[r_F])
                    for grp, c0, dst in (("k", 1024, "kT"), ("u", 0, "usT")):
                        for t in range(4):
                            pi = kcount[0] % 3
                            si = kcount[0] % 4
                            kcount[0] += 1
                            for k in range(8):
                                K.op("pe", lambda e, k=k, pi=pi, c0=c0, t=t: e.matmul(pk[pi][:], lhsT=W1[:, k, c0 + t * 128:c0 + (t + 1) * 128], rhs=uT[ui][:, k, :], start=(k == 0), stop=(k == 7)),
                                     reads=[r_uT[ui], r_W1], writes=[r_pk[pi]], signal=(k == 7))
                            if grp == "k":
                                evac(kst[si][:], pk[pi][:], [r_pk[pi]], [r_kst[si]])
                                for e_ in range(2):
                                    K.dma("sp", kT[2 * t + e_, 0:64, s * 512:(s + 1) * 512], kst[si][64 * e_:64 * e_ + 64, :], reads=[r_kst[si]], dres=r_kst[si])
                            else:
                                s4 = s % 4
                                evac(usd[t][:, :, s4 * 32:(s4 + 1) * 32], pk[pi][:].rearrange("p (c s) -> p s c", s=16), [r_pk[pi]], [r_usd[t]])
                                if s4 == 3:
                                    g4_ = s // 4
                                    K.dma("sp", usT[t * 128:(t + 1) * 128, :, g4_ * 128:(g4_ + 1) * 128], usd[t][:], reads=[r_usd[t]], dres=r_usd[t])
                    for t in range(4):
                        pi = kcount[0] % 3
                        qi = kcount[0] % 2
                        kcount[0] += 1
                        for k in range(8):
                            K.op("pe", lambda e, k=k, pi=pi, t=t: e.matmul(pk[pi][:, 0:128], lhsT=W1[:, k, 512 + t * 128:512 + (t + 1) * 128], rhs=uT[ui][:, k, 384:512], start=(k == 0), stop=(k == 7)),
                                 reads=[r_uT[ui], r_W1], writes=[r_pk[pi]], signal=(k == 7))
                        evac(qst[qi][:], pk[pi][:, 0:128], [r_pk[pi]], [r_qst[qi]], scale=0.125)
                        for e_ in range(2):
                            K.dma("sp", qT[2 * t + e_, 0:64, s * 128:(s + 1) * 128], qst[qi][64 * e_:64 * e_ + 64, :], reads=[r_qst[qi]], dres=r_qst[qi])
                    K.dma("sp", uTo[:, s * 128:(s + 1) * 128].rearrange("(k p) t -> p k t", p=128), uT[ui][:, :, 384:512], reads=[r_uT[ui]], dres=r_uT[ui])

                stage_N(0)
                stage_T(0)
                stage_N(1)
                for s in range(16):
                    if s + 1 < 16:
                        stage_T(s + 1)
                    if s + 2 < 16:
                        stage_N(s + 2)
                    stage_M(s)
                K.barrier()

            if stage <= 1:
                raise _Stop()
            with contextlib.ExitStack() as p2:
                for _once in (0,):
                    L = sb(p2, "L", [128, 512]); r_L = Res()
                    Fv = F_all[:].rearrange("p h b -> p (h b)")
                    K.op("act", lambda e: e.activation(out=L[:], in_=Fv, func=AF.Exp, scale=-1.0), reads=[r_F], writes=[r_L])
                    one1 = sb(p2, "one1", [128, 1]); r_one1 = Res()
                    K.op("dve", lambda e: e.memset(one1[:], 1.0), writes=[r_one1])
                    K.op("act", lambda e: e.activation(out=L[:], in_=L[:], func=AF.Ln, bias=one1[:], scale=1.0), reads=[r_L, r_one1], writes=[r_L])
                    pc = ps(p2, "pc", [128, 512]); r_pc = Res()
                    pc2 = ps(p2, "pc2", [128, 512]); r_pc2 = Res()
                    K.op("pe", lambda e: e.matmul(pc[:], lhsT=triu[:], rhs=L[:], start=True, stop=True), reads=[r_L, r_const], writes=[r_pc])
                    incl = sb(p2, "incl", [128, 512]); r_incl = Res()
                    evac(incl[:], pc[:], [r_pc], [r_incl], eng="dve")
                    K.op("pe", lambda e: e.matmul(pc2[:], lhsT=e127[:], rhs=incl[:], start=True, stop=True), reads=[r_incl, r_const], writes=[r_pc2])
                    tot = sb(p2, "tot", [128, 512]); r_tot = Res()
                    evac(tot[:], pc2[:], [r_pc2], [r_tot], eng="dve")
                    rmul = sb(p2, "rmul", [128, 8, NB]); r_rmul = Res()
                    K.op("dve", lambda e: e.memset(rmul[:], 1.0), writes=[r_rmul])
                    K.op("dve", lambda e: e.memset(rmul[:, :, 0:1], 0.0), writes=[r_rmul])
                    offs = sb(p2, "offs", [128, 512]); r_offs = Res()
                    K.op("dve", lambda e: e.tensor_tensor_scan(out=offs[:], data0=rmul[:].rearrange("p h b -> p (h b)"), data1=tot[:], initial=0.0, op0=ALU.mult, op1=ALU.add),
                         reads=[r_rmul, r_tot], writes=[r_offs])
                    cl = sb(p2, "cl", [128, 512]); r_cl = Res()
                    K.op("dve", lambda e: e.tensor_tensor(out=cl[:], in0=offs[:], in1=tot[:], op=ALU.subtract), reads=[r_offs, r_tot], writes=[r_cl])
                    K.op("dve", lambda e: e.tensor_tensor(out=cl[:], in0=cl[:], in1=incl[:], op=ALU.add), reads=[r_cl, r_incl], writes=[r_cl])
                    if os.environ.get("DBG_SUB") == "1":
                        break
                    spl = sb(p2, "spl", [128, NB, 8, 3], BF16); r_spl = Res()
                    res1 = sb(p2, "res1", [128, 512]); r_res1 = Res()
                    clv = cl[:].rearrange("p (h b) -> p b h", h=8)
                    r1v = res1[:].rearrange("p (h b) -> p b h", h=8)
                    K.op("dve", lambda e: e.tensor_copy(out=spl[:, :, :, 0], in_=clv), reads=[r_cl], writes=[r_spl])
                    K.op("dve", lambda e: e.tensor_tensor(out=r1v, in0=clv, in1=spl[:, :, :, 0], op=ALU.subtract), reads=[r_cl, r_spl], writes=[r_res1])
                    K.op("dve", lambda e: e.tensor_copy(out=spl[:, :, :, 1], in_=r1v), reads=[r_res1], writes=[r_spl])
                    K.op("dve", lambda e: e.tensor_tensor(out=r1v, in0=r1v, in1=spl[:, :, :, 1], op=ALU.subtract), reads=[r_res1, r_spl], writes=[r_res1])
                    K.op("dve", lambda e: e.tensor_copy(out=spl[:, :, :, 2], in_=r1v), reads=[r_res1], writes=[r_spl])
                    if os.environ.get("DBG_SUB") == "2":
                        break
                    augT = sb(p2, "augT", [24, NT], BF16); r_augT = Res()
                    qaug = sb(p2, "qaug", [24, NOWN], BF16); r_qaug = Res()
                    pa = [ps(p2, "pa%d" % i, [128, 512]) for i in range(2)]; r_pa = [Res() for _ in range(2)]
                    for g4 in range(16):
                        pi = g4 % 2
                        for bb in range(4):
                            blk = 4 * g4 + bb
                            K.op("pe", lambda e, blk=blk, bb=bb, pi=pi: e.matmul(pa[pi][0:24, bb * 128:(bb + 1) * 128], lhsT=spl[:, blk, :, :].rearrange("p h s -> p (h s)"), rhs=identb[:], start=True, stop=True),
                                 reads=[r_spl, r_const], writes=[r_pa[pi]], signal=(bb == 3))
                        if os.environ.get("DBG_VAR") != "B":
                            evac(augT[:, g4 * 512:(g4 + 1) * 512], pa[pi][0:24, :], [r_pa[pi]], [r_augT], eng={"C": "act", "D": "dve"}.get(os.environ.get("DBG_VAR"), None))
                        if os.environ.get("DBG_VAR") == "A":
                            continue
                        K.op("dve", lambda e, g4=g4, pi=pi: e.tensor_scalar(out=qaug[:, g4 * 128:(g4 + 1) * 128], in0=pa[pi][0:24, 384:512], scalar1=-1.0, scalar2=None, op0=ALU.mult),
                             reads=[r_pa[pi]], writes=[r_qaug])
                    if os.environ.get("DBG_SUB") == "3":
                        break
                    for h in range(8):
                        K.dma("sp", kT[h, 64:67, :], augT[3 * h:3 * h + 3, :], reads=[r_augT], dres=r_augT)
                        K.dma("sp", qT[h, 67:70, :], qaug[3 * h:3 * h + 3, :], reads=[r_qaug], dres=r_qaug)
                    K.barrier()

            if stage <= 2:
                raise _Stop()
            with contextlib.ExitStack() as p3:
                T16 = 16
                r_pc_ = Res("parc"); r_pcc = Res("parcc")
                def lam_bar(st, P, n, lr, li, ldt, tag, pw, st_tmp=None, r_in=None):
                    lbr = sb(st, tag + "lbr", [P, n]); lbi = sb(st, tag + "lbi", [P, n])
                    al = sb(st, tag + "al", [P, n]); tf = sb(st, tag + "tf", [P, n])
                    st2 = st if st_tmp is None else st_tmp
                    dt_ = sb(st2, tag + "dt", [P, n]); th = sb(st2, tag + "th", [P, n])
                    ti = sb(st2, tag + "ti", [P, n], I32); fa = sb(st2, tag + "fa", [P, n])
                    mg = sb(st2, tag + "mg", [P, n]); sn = sb(st2, tag + "sn", [P, n]); cs = sb(st2, tag + "cs", [P, n])
                    rr = Res(tag)
                    r_in = r_pc_ if r_in is None else r_in
                    K.op("act", lambda e: e.activation(out=dt_[:], in_=ldt[:], func=AF.Exp), reads=[r_in], writes=[rr])
                    K.op("dve", lambda e: e.tensor_tensor(out=al[:], in0=lr[:], in1=dt_[:], op=ALU.mult), reads=[rr, r_in], writes=[rr])
                    K.op("dve", lambda e: e.tensor_tensor(out=th[:], in0=li[:], in1=dt_[:], op=ALU.mult), reads=[rr, r_in], writes=[rr])
                    K.op("act", lambda e: e.activation(out=mg[:], in_=al[:], func=AF.Exp, scale=float(pw)), reads=[rr], writes=[rr])
                    K.op("dve", lambda e: e.tensor_scalar(out=tf[:], in0=th[:], scalar1=float(pw) / TWO_PI, scalar2=None, op0=ALU.mult), reads=[rr], writes=[rr])
                    K.op("dve", lambda e: e.tensor_copy(out=ti[:], in_=tf[:]), reads=[rr], writes=[rr])
                    K.op("dve", lambda e: e.tensor_copy(out=fa[:], in_=ti[:]), reads=[rr], writes=[rr])
                    K.op("dve", lambda e: e.tensor_tensor(out=tf[:], in0=tf[:], in1=fa[:], op=ALU.subtract), reads=[rr], writes=[rr])
                    K.op("act", lambda e: e.activation(out=sn[:], in_=tf[:], func=AF.Sin, scale=TWO_PI), reads=[rr], writes=[rr])
                    K.op("dve", lambda e: e.tensor_scalar(out=fa[:], in0=tf[:], scalar1=-1.0, scalar2=None, op0=ALU.mult), reads=[rr], writes=[rr])
                    K.op("dve", lambda e: e.tensor_tensor(out=fa[:], in0=fa[:], in1=tf[:], op=ALU.max), reads=[rr], writes=[rr])
                    K.op("act", lambda e: e.activation(out=cs[:], in_=fa[:], func=AF.Sin, scale=-TWO_PI, bias=halfpi[0:P, :]), reads=[rr], writes=[rr])
                    K.op("dve", lambda e: e.tensor_tensor(out=lbr[:], in0=mg[:], in1=cs[:], op=ALU.mult), reads=[rr], writes=[rr])
                    K.op("dve", lambda e: e.tensor_tensor(out=lbi[:], in0=mg[:], in1=sn[:], op=ALU.mult), reads=[rr], writes=[rr])
                    return lbr, lbi, al, tf, rr

                Wa = sb(p3, "Wa", [96, T16, 2, 768], BF16)
                lrr = sb(p3, "lrr", [128, 16]); lir = sb(p3, "lir", [128, 16]); dtr = sb(p3, "dtr", [128, 16])
                Crb = sb(p3, "Crb", [128, 512], BF16); Cib = sb(p3, "Cib", [128, 512], BF16)
                Dmb = sb(p3, "Dmb", [96, 192], BF16)
                for t_, s_ in ((lrr, lamr_r), (lir, lami_r), (dtr, ldt_r)):
                    K.dma("sp", t_[:], s_[:, :], writes=[r_pc_], dres=r_pc_)
                Vtab = sb(p3, "Vtab", [128, 16, T16, 2, 32], BF16)
                Kd = sb(p3, "Kd", [96, 6, T16, 32], BF16)
                K.op("pool", lambda e: e.memset(Kd[:].rearrange("p s d c -> p (s d c)"), 0.0), writes=[Res()])
                cidx = sb(p3, "cidx", [128, 512])
                K.dma("sp", cidx[:], c_idx[:, :], writes=[r_pc_], dres=r_pc_)
                r_tab = Res("tab")
                lbr_r, lbi_r, alr, f1r, r_lr1 = lam_bar(p3, 128, 16, lrr, lir, dtr, "r1", 1)
                _, _, _, f16r, r_lr16 = lam_bar(p3, 128, 16, lrr, lir, dtr, "r16", 16)
                rho16 = sb(p3, "rho16", [128, 16])
                K.op("act", lambda e: e.activation(out=rho16[:], in_=alr[:], func=AF.Exp, scale=16.0), reads=[r_lr1], writes=[r_tab])
                tmp = {e_: (sb(p3, "ta_" + e_, [128, 512]), sb(p3, "tb_" + e_, [128, 512]), Res()) for e_ in ("dve", "pool")}
                ptc = contextlib.ExitStack()
                Bbr = sb(ptc, "Bbr", [128, 512], BF16); Bbi = sb(ptc, "Bbi", [128, 512], BF16)
                Lr = sb(ptc, "Lr", [128, 16, T16 + 1]); Li = sb(ptc, "Li", [128, 16, T16 + 1])
                Brf = sb(ptc, "Brf", [128, 512]); Bif = sb(ptc, "Bif", [128, 512]); Cif = sb(ptc, "Cif", [128, 512])
                Crf = sb(ptc, "Crf", [128, 512]); Dmf = sb(ptc, "Dmf", [96, 192]); r_cd = Res("cd")
                cfr = sb(ptc, "cfr", [128, 16]); cfi = sb(ptc, "cfi", [128, 16]); s1 = sb(ptc, "s1", [128, 16]); s2 = sb(ptc, "s2", [128, 16])
                dn = sb(ptc, "dn", [128, 16]); nr2 = sb(ptc, "nr2", [128, 16]); r_cf = Res("cf")
                K.dma("sp", Cif[:], Ci_r[:, :], writes=[r_cd], dres=r_cd)
                K.dma("sp", Brf[:], Brow_r[:, :], writes=[r_cd], dres=r_cd)
                K.dma("sp", Bif[:], Brow_i[:, :], writes=[r_cd], dres=r_cd)
                K.dma("sp", Crf[:], Cr_r[:, :], writes=[r_cd], dres=r_cd)
                K.dma("sp", Dmf[:], Dm[:, :], writes=[r_cd], dres=r_cd)
                r_cdb = Res("cdb")
                K.op("dve", lambda e: e.tensor_copy(out=Crb[:], in_=Crf[:]), reads=[r_cd], writes=[r_cdb])
                K.op("dve", lambda e: e.tensor_copy(out=Dmb[:], in_=Dmf[:]), reads=[r_cd], writes=[r_cdb])
                ptb = contextlib.ExitStack()
                lrc = sb(ptb, "lrc", [96, 768]); lic = sb(ptb, "lic", [96, 768]); dtc = sb(ptb, "dtc", [96, 768])
                Brc = sb(ptb, "Brc", [96, 768]); Bic = sb(ptb, "Bic", [96, 768])
                for t_, s_ in ((lrc, lamr_c), (lic, lami_c), (dtc, ldt_c), (Brc, Br_c), (Bic, Bi_c)):
                    K.dma("sp", t_[:], s_[:, :], writes=[r_pcc], dres=r_pcc)
                K.op("dve", lambda e: e.tensor_scalar(out=Cib[:], in0=Cif[:], scalar1=-1.0, scalar2=None, op0=ALU.mult), reads=[r_cd], writes=[r_tab])

                with contextlib.ExitStack() as ptb2:
                    lbr, lbi, alc, _, r_lc = lam_bar(ptb, 96, 768, lrc, lic, dtc, "c", 1, st_tmp=ptb2, r_in=r_pcc)
                    K.barrier()
                Pr = sb(ptb, "Pr", [96, 768]); Pi = sb(ptb, "Pi", [96, 768]); t1 = sb(ptb, "t1", [96, 768]); t2 = sb(ptb, "t2", [96, 768])
                den = sb(ptb, "den", [96, 768]); nr = sb(ptb, "nr", [96, 768])
                r_P = Res("P")
                V = "dve"
                K.op(V, lambda e: e.tensor_scalar(out=nr[:], in0=lbr[:], scalar1=-1.0, scalar2=None, op0=ALU.add), reads=[r_lc], writes=[r_P])
                K.op(V, lambda e: e.tensor_tensor(out=den[:], in0=lrc[:], in1=lrc[:], op=ALU.mult), reads=[r_pcc], writes=[r_P])
                K.op(V, lambda e: e.tensor_tensor(out=t1[:], in0=lic[:], in1=lic[:], op=ALU.mult), reads=[r_pcc, r_P], writes=[r_P])
                K.op(V, lambda e: e.tensor_tensor(out=den[:], in0=den[:], in1=t1[:], op=ALU.add), reads=[r_P], writes=[r_P])
                K.op(V, lambda e: e.reciprocal(out=den[:], in_=den[:]), reads=[r_P], writes=[r_P])
                K.op(V, lambda e: e.tensor_tensor(out=t1[:], in0=nr[:], in1=lrc[:], op=ALU.mult), reads=[r_P, r_pcc], writes=[r_P])
                K.op(V, lambda e: e.tensor_tensor(out=t2[:], in0=lbi[:], in1=lic[:], op=ALU.mult), reads=[r_P, r_pcc, r_lc], writes=[r_P])
                K.op(V, lambda e: e.tensor_tensor(out=t1[:], in0=t1[:], in1=t2[:], op=ALU.add), reads=[r_P], writes=[r_P])
                K.op(V, lambda e: e.tensor_tensor(out=Pr[:], in0=t1[:], in1=den[:], op=ALU.mult), reads=[r_P], writes=[r_P])
                K.op(V, lambda e: e.tensor_tensor(out=t1[:], in0=lbi[:], in1=lrc[:], op=ALU.mult), reads=[r_P, r_pcc, r_lc], writes=[r_P])
                K.op(V, lambda e: e.tensor_tensor(out=t2[:], in0=nr[:], in1=lic[:], op=ALU.mult), reads=[r_P, r_pcc], writes=[r_P])
                K.op(V, lambda e: e.tensor_tensor(out=t1[:], in0=t1[:], in1=t2[:], op=ALU.subtract), reads=[r_P], writes=[r_P])
                K.op(V, lambda e: e.tensor_tensor(out=Pi[:], in0=t1[:], in1=den[:], op=ALU.mult), reads=[r_P], writes=[r_P])
                Pn = sb(ptb, "Pn", [96, 768])
                for kk in range(T16):
                    tau = T16 - 1 - kk
                    K.op(V, lambda e: e.tensor_tensor(out=t1[:], in0=Pr[:], in1=Brc[:], op=ALU.mult), reads=[r_P, r_pcc], writes=[r_P])
                    K.op(V, lambda e: e.tensor_tensor(out=t2[:], in0=Pi[:], in1=Bic[:], op=ALU.mult), reads=[r_P, r_pcc], writes=[r_P])
                    K.op(V, lambda e, tau=tau: e.tensor_tensor(out=Wa[:, tau, 0, :], in0=t1[:], in1=t2[:], op=ALU.subtract), reads=[r_P], writes=[r_tab])
                    K.op(V, lambda e: e.tensor_tensor(out=t1[:], in0=Pr[:], in1=Bic[:], op=ALU.mult), reads=[r_P, r_pcc], writes=[r_P])
                    K.op(V, lambda e: e.tensor_tensor(out=t2[:], in0=Pi[:], in1=Brc[:], op=ALU.mult), reads=[r_P, r_pcc], writes=[r_P])
                    K.op(V, lambda e, tau=tau: e.tensor_tensor(out=Wa[:, tau, 1, :], in0=t1[:], in1=t2[:], op=ALU.add), reads=[r_P], writes=[r_tab])
                    if kk < T16 - 1:
                        K.op(V, lambda e: e.tensor_tensor(out=t1[:], in0=Pr[:], in1=lbr[:], op=ALU.mult), reads=[r_P, r_lc], writes=[r_P])
                        K.op(V, lambda e: e.tensor_tensor(out=t2[:], in0=Pi[:], in1=lbi[:], op=ALU.mult), reads=[r_P, r_lc], writes=[r_P])
                        K.op(V, lambda e: e.tensor_tensor(out=Pn[:], in0=t1[:], in1=t2[:], op=ALU.subtract), reads=[r_P], writes=[r_P])
                        K.op(V, lambda e: e.tensor_tensor(out=t1[:], in0=Pr[:], in1=lbi[:], op=ALU.mult), reads=[r_P, r_lc], writes=[r_P])
                        K.op(V, lambda e: e.tensor_tensor(out=t2[:], in0=Pi[:], in1=lbr[:], op=ALU.mult), reads=[r_P, r_lc], writes=[r_P])
                        K.op(V, lambda e: e.tensor_tensor(out=Pi[:], in0=t1[:], in1=t2[:], op=ALU.add), reads=[r_P], writes=[r_P])
                        K.op(V, lambda e: e.tensor_copy(out=Pr[:], in_=Pn[:]), reads=[r_P], writes=[r_P])
                K.barrier()
                ptb.close()

                def cmul(eng, outr, outi, ar, ai, br, bi, conj_b, reads, writes, shp):
                    ta, tb, r_tt = tmp[eng]
                    o1 = ALU.subtract if not conj_b else ALU.add
                    o2 = ALU.add if not conj_b else ALU.subtract
                    tav, tbv = shp(ta), shp(tb)
                    K.op(eng, lambda e: e.tensor_tensor(out=tav, in0=ar, in1=br, op=ALU.mult), reads=reads, writes=[r_tt])
                    K.op(eng, lambda e: e.tensor_tensor(out=tbv, in0=ai, in1=bi, op=ALU.mult), reads=reads + [r_tt], writes=[r_tt])
                    K.op(eng, lambda e: e.tensor_tensor(out=outr, in0=tav, in1=tbv, op=o1), reads=[r_tt], writes=writes)
                    K.op(eng, lambda e: e.tensor_tensor(out=tav, in0=ai, in1=br, op=ALU.mult), reads=reads + [r_tt], writes=[r_tt])
                    K.op(eng, lambda e: e.tensor_tensor(out=tbv, in0=ar, in1=bi, op=ALU.mult), reads=reads + [r_tt], writes=[r_tt])
                    K.op(eng, lambda e: e.tensor_tensor(out=outi, in0=tav, in1=tbv, op=o2), reads=[r_tt], writes=writes)

                flat = lambda n: (lambda t: t[:, 0:n])
                pq32 = lambda t: t[:].rearrange("p (q c) -> p q c", c=32)

                V = "dve"
                K.op(V, lambda e: e.tensor_scalar(out=nr2[:], in0=lbr_r[:], scalar1=-1.0, scalar2=None, op0=ALU.add), reads=[r_lr1], writes=[r_cf])
                K.op(V, lambda e: e.tensor_tensor(out=dn[:], in0=lrr[:], in1=lrr[:], op=ALU.mult), reads=[r_pc_], writes=[r_cf])
                K.op(V, lambda e: e.tensor_tensor(out=s1[:], in0=lir[:], in1=lir[:], op=ALU.mult), reads=[r_pc_, r_cf], writes=[r_cf])
                K.op(V, lambda e: e.tensor_tensor(out=dn[:], in0=dn[:], in1=s1[:], op=ALU.add), reads=[r_cf], writes=[r_cf])
                K.op(V, lambda e: e.reciprocal(out=dn[:], in_=dn[:]), reads=[r_cf], writes=[r_cf])
                K.op(V, lambda e: e.tensor_tensor(out=s1[:], in0=nr2[:], in1=lrr[:], op=ALU.mult), reads=[r_cf, r_pc_], writes=[r_cf])
                K.op(V, lambda e: e.tensor_tensor(out=s2[:], in0=lbi_r[:], in1=lir[:], op=ALU.mult), reads=[r_cf, r_pc_, r_lr1], writes=[r_cf])
                K.op(V, lambda e: e.tensor_tensor(out=s1[:], in0=s1[:], in1=s2[:], op=ALU.add), reads=[r_cf], writes=[r_cf])
                K.op(V, lambda e: e.tensor_tensor(out=cfr[:], in0=s1[:], in1=dn[:], op=ALU.mult), reads=[r_cf], writes=[r_cf])
                K.op(V, lambda e: e.tensor_tensor(out=s1[:], in0=lbi_r[:], in1=lrr[:], op=ALU.mult), reads=[r_cf, r_pc_, r_lr1], writes=[r_cf])
                K.op(V, lambda e: e.tensor_tensor(out=s2[:], in0=nr2[:], in1=lir[:], op=ALU.mult), reads=[r_cf, r_pc_], writes=[r_cf])
                K.op(V, lambda e: e.tensor_tensor(out=s1[:], in0=s1[:], in1=s2[:], op=ALU.subtract), reads=[r_cf], writes=[r_cf])
                K.op(V, lambda e: e.tensor_tensor(out=cfi[:], in0=s1[:], in1=dn[:], op=ALU.mult), reads=[r_cf], writes=[r_cf])
                r_Bb = Res("Bb")
                bc = lambda t: t[:].unsqueeze(2).to_broadcast([128, 16, 32])
                cmul("dve", pq32(Bbr), pq32(Bbi), pq32(Brf), pq32(Bif), bc(cfr), bc(cfi), False, [r_cd, r_cf], [r_Bb], pq32)
                r_L = Res("L")
                K.op("pool", lambda e: e.memset(Lr[:, :, 0:1], 1.0), writes=[r_L])
                K.op("pool", lambda e: e.memset(Li[:, :, 0:1], 0.0), writes=[r_L])
                tp_ = tmp["pool"]
                for k in range(T16):
                    ta, tb, r_tt = tp_
                    K.op("pool", lambda e, k=k: e.tensor_tensor(out=ta[:, 0:16], in0=Lr[:, :, k], in1=lbr_r[:], op=ALU.mult), reads=[r_L, r_lr1], writes=[r_tt])
                    K.op("pool", lambda e, k=k: e.tensor_tensor(out=tb[:, 0:16], in0=Li[:, :, k], in1=lbi_r[:], op=ALU.mult), reads=[r_L, r_lr1, r_tt], writes=[r_tt])
                    K.op("pool", lambda e, k=k: e.tensor_tensor(out=Lr[:, :, k + 1], in0=ta[:, 0:16], in1=tb[:, 0:16], op=ALU.subtract), reads=[r_tt], writes=[r_L])
                    K.op("pool", lambda e, k=k: e.tensor_tensor(out=ta[:, 0:16], in0=Lr[:, :, k], in1=lbi_r[:], op=ALU.mult), reads=[r_L, r_lr1, r_tt], writes=[r_tt])
                    K.op("pool", lambda e, k=k: e.tensor_tensor(out=tb[:, 0:16], in0=Li[:, :, k], in1=lbr_r[:], op=ALU.mult), reads=[r_L, r_lr1, r_tt], writes=[r_tt])
                    K.op("pool", lambda e, k=k: e.tensor_tensor(out=Li[:, :, k + 1], in0=ta[:, 0:16], in1=tb[:, 0:16], op=ALU.add), reads=[r_tt], writes=[r_L])
                r_V = Res("V")
                for tau in range(T16):
                    eng = "dve" if tau % 2 == 0 else "pool"
                    ta, tb, r_tt = tmp[eng]
                    lr_b = Lr[:, :, tau + 1].unsqueeze(2).to_broadcast([128, 16, 32])
                    li_b = Li[:, :, tau + 1].unsqueeze(2).to_broadcast([128, 16, 32])
                    K.op(eng, lambda e, ta=ta, lr_b=lr_b: e.tensor_tensor(out=pq32(ta), in0=pq32(Crf), in1=lr_b, op=ALU.mult), reads=[r_cd, r_L], writes=[r_tt])
                    K.op(eng, lambda e, tb=tb, li_b=li_b: e.tensor_tensor(out=pq32(tb), in0=pq32(Cif), in1=li_b, op=ALU.mult), reads=[r_cd, r_L, r_tt], writes=[r_tt])
                    K.op(eng, lambda e, ta=ta, tb=tb, tau=tau: e.tensor_tensor(out=Vtab[:, :, tau, 0, :], in0=pq32(ta), in1=pq32(tb), op=ALU.subtract), reads=[r_tt], writes=[r_V])
                    K.op(eng, lambda e, ta=ta, li_b=li_b: e.tensor_tensor(out=pq32(ta), in0=pq32(Crf), in1=li_b, op=ALU.mult), reads=[r_cd, r_L, r_tt], writes=[r_tt])
                    K.op(eng, lambda e, tb=tb, lr_b=lr_b: e.tensor_tensor(out=pq32(tb), in0=pq32(Cif), in1=lr_b, op=ALU.mult), reads=[r_cd, r_L, r_tt], writes=[r_tt])
                    K.op(eng, lambda e, ta=ta, tb=tb: e.tensor_tensor(out=pq32(ta), in0=pq32(ta), in1=pq32(tb), op=ALU.add), reads=[r_tt], writes=[r_tt])
                    K.op(eng, lambda e, ta=ta, tau=tau: e.tensor_scalar(out=Vtab[:, :, tau, 1, :], in0=pq32(ta), scalar1=-1.0, scalar2=None, op0=ALU.mult), reads=[r_tt], writes=[r_V])
                r_Kd = Res("Kd")
                with contextlib.ExitStack() as pk_:
                    pskd = [ps(pk_, "pskd%d" % i_, [96, 512]) for i_ in range(2)]; r_pskd = [Res() for _ in range(2)]
                    for sl in range(6):
                        pi_ = sl % 2
                        npair = min(3, 16 - 3 * sl)
                        for kk in range(npair):
                            q = 3 * sl + kk
                            for dl in range(T16):
                                rr_ = Crb[:, q * 32:(q + 1) * 32] if dl == 0 else Vtab[:, q, dl - 1, 0, :]
                                ri_ = Cib[:, q * 32:(q + 1) * 32] if dl == 0 else Vtab[:, q, dl - 1, 1, :]
                                lastmm = (kk == npair - 1 and dl == T16 - 1)
                                K.op("pe", lambda e, rr_=rr_, q=q, kk=kk, dl=dl, pi_=pi_: e.matmul(pskd[pi_][32 * kk:32 * kk + 32, dl * 32:(dl + 1) * 32], lhsT=Bbr[:, q * 32:(q + 1) * 32], rhs=rr_, start=True, stop=False),
                                     reads=[r_Bb, r_V, r_cdb, r_tab], writes=[r_pskd[pi_]], signal=False)
                                K.op("pe", lambda e, ri_=ri_, q=q, kk=kk, dl=dl, pi_=pi_: e.matmul(pskd[pi_][32 * kk:32 * kk + 32, dl * 32:(dl + 1) * 32], lhsT=Bbi[:, q * 32:(q + 1) * 32], rhs=ri_, start=False, stop=True),
                                     reads=[r_Bb, r_V, r_cdb, r_tab], writes=[r_pskd[pi_]], signal=lastmm)
                        np_ = 32 * npair
                        evac(Kd[0:np_, sl, :, :].rearrange("p d c -> p (d c)"), pskd[pi_][0:np_, :], [r_pskd[pi_]], [r_Kd], eng="act")
                    K.op("dve", lambda e: e.tensor_tensor(out=Kd[:, :, 0, :], in0=Kd[:, :, 0, :], in1=Dmf[:].rearrange("p (s c) -> p s c", c=32), op=ALU.add), reads=[r_Kd, r_cd], writes=[r_Kd])
                    K.barrier()
                ptc.close()

                usp = [sb(p3, "usp%d" % i, [96, 16, 512], BF16) for i in range(2)]; r_usp = [Res() for _ in range(2)]
                uso = [sb(p3, "uso%d" % i, [96, 16, 128], BF16) for i in range(2)]; r_uso = [Res() for _ in range(2)]
                pz = [ps(p3, "pz%d" % i, [128, 512]) for i in range(2)]; r_pz = [Res() for _ in range(2)]
                pY = [[ps(p3, "pY%d_%d" % (b_, j_), [32, 512]) for j_ in range(1)] for b_ in range(4)]
                r_pY = [Res() for _ in range(4)]
                Z = sb(p3, "Z", [128, 2, 512]); r_Z = Res()
                Zd = sb(p3, "Zd", [128, 2, 512]); r_Zd = Res()
                Wc = sb(p3, "Wc", [128, 2, 512]); r_Wc = Res()
                ang, angf, r_ang = tmp["dve"]; angi = sb(p3, "angi", [128, 512], I32)
                Ec = sb(p3, "Ec", [128, 2, 512]); r_Ec = Res()
                rhoT = sb(p3, "rhoT", [128, 512]); r_rhoT = Res()
                Sown = [sb(p3, "Sown%d" % i, [128, 2, 128], BF16) for i in range(2)]; r_Sown = [Res() for _ in range(2)]
                ysb1 = sb(p3, "ysb", [32, NOWN], BF16); ysb = [ysb1, ysb1]; r_ysb1 = Res(); r_ysb = [r_ysb1, r_ysb1]

                def load_pair(q):
                    i = q % 2
                    kb = 32 * (q % 3)
                    K.dma("sp", usp[i][kb:kb + 32, :, :], usT[q * 32:(q + 1) * 32, :, :], writes=[r_usp[i]], dres=r_usp[i])
                    K.op("act", lambda e: e.activation(out=uso[i][kb:kb + 32, :, :].rearrange("p s (m k) -> p s m k", k=8),
                                                       in_=usp[i][kb:kb + 32, :, :].rearrange("p s (m r k) -> p s m r k", r=4, k=8)[:, :, :, 3, :], func=AF.Copy),
                         reads=[r_usp[i]], writes=[r_uso[i]])

                def twiddle(dst, r_dst, idx_ap, f_col, n):
                    K.op("dve", lambda e: e.tensor_scalar(out=ang[:, 0:n], in0=idx_ap, scalar1=f_col, scalar2=None, op0=ALU.mult), reads=[r_pc_, r_lr1, r_lr16], writes=[r_ang])
                    K.op("dve", lambda e: e.tensor_copy(out=angi[:, 0:n], in_=ang[:, 0:n]), reads=[r_ang], writes=[r_ang])
                    K.op("dve", lambda e: e.tensor_copy(out=angf[:, 0:n], in_=angi[:, 0:n]), reads=[r_ang], writes=[r_ang])
                    K.op("dve", lambda e: e.tensor_tensor(out=ang[:, 0:n], in0=ang[:, 0:n], in1=angf[:, 0:n], op=ALU.subtract), reads=[r_ang], writes=[r_ang])
                    K.op("act", lambda e: e.activation(out=dst[:, 1, 0:n], in_=ang[:, 0:n], func=AF.Sin, scale=-TWO_PI), reads=[r_ang], writes=[r_dst])
                    K.op("dve", lambda e: e.tensor_scalar(out=angf[:, 0:n], in0=ang[:, 0:n], scalar1=-1.0, scalar2=None, op0=ALU.mult), reads=[r_ang], writes=[r_ang])
                    K.op("dve", lambda e: e.tensor_tensor(out=angf[:, 0:n], in0=angf[:, 0:n], in1=ang[:, 0:n], op=ALU.max), reads=[r_ang], writes=[r_ang])
                    K.op("act", lambda e: e.activation(out=dst[:, 0, 0:n], in_=angf[:, 0:n], func=AF.Sin, scale=-TWO_PI, bias=halfpi[:]), reads=[r_ang], writes=[r_dst])

                def stage_P(q):
                    i = q % 2
                    kb = 32 * (q % 3)
                    sl = q // 3
                    cs_ = slice(sl * 128, (sl + 1) * 128)
                    for ri in range(2):
                        for tau in range(T16):
                            K.op("pe", lambda e, ri=ri, tau=tau: e.matmul(pz[ri][:], lhsT=Wa[kb:kb + 32, tau, ri, cs_], rhs=usp[i][kb:kb + 32, tau, :], start=(tau == 0), stop=(tau == T16 - 1)),
                                 reads=[r_usp[i], r_tab], writes=[r_pz[ri]], signal=(tau == T16 - 1))
                        evac(Z[:, ri, :], pz[ri][:], [r_pz[ri]], [r_Z], eng="act")
                    twiddle(Ec, r_Ec, cidx[:], f16r[:, q:q + 1], 512)
                    cmul("dve", Zd[:, 0, :], Zd[:, 1, :], Z[:, 0, :], Z[:, 1, :], Ec[:, 0, :], Ec[:, 1, :], False, [r_Z, r_Ec], [r_Zd], flat(512))
                    K.op("dve", lambda e: e.tensor_copy(out=rhoT[:], in_=rho16[:, q:q + 1].to_broadcast([128, 512])), reads=[r_tab], writes=[r_rhoT])
                    for ri in range(2):
                        K.op("dve", lambda e, ri=ri: e.tensor_tensor_scan(out=Wc[:, ri, :], data0=rhoT[:], data1=Zd[:, ri, :], initial=0.0, op0=ALU.mult, op1=ALU.add),
                             reads=[r_rhoT, r_Zd], writes=[r_Wc])
                    wv = Wc[:].rearrange("p r (m c) -> p r m c", c=32)
                    ev = Ec[:].rearrange("p r (m c) -> p r m c", c=32)
                    so = Sown[i][:].rearrange("p r (m k) -> p r m k", k=8)
                    mk = lambda t: t[:, 0:128].rearrange("p (m k) -> p m k", k=8)
                    cmul("dve", so[:, 0], so[:, 1], wv[:, 0, :, 23:31], wv[:, 1, :, 23:31], ev[:, 0, :, 23:31], ev[:, 1, :, 23:31], True, [r_Wc, r_Ec], [r_Sown[i]], mk)

                def stage_F(q):
                    i = q % 2
                    kb = 32 * (q % 3)
                    sl = q // 3
                    for tau in range(T16):
                        j_ = tau // 4
                        oap = pY[j_][0][:, (tau % 4) * 128:(tau % 4 + 1) * 128]
                        K.op("pe", lambda e, tau=tau, oap=oap: e.matmul(oap, lhsT=Vtab[:, q, tau, 0, :], rhs=Sown[i][:, 0, :], start=True, stop=False),
                             reads=[r_V, r_Sown[i]], writes=[r_pY[j_]], signal=False)
                        K.op("pe", lambda e, tau=tau, oap=oap: e.matmul(oap, lhsT=Vtab[:, q, tau, 1, :], rhs=Sown[i][:, 1, :], start=False, stop=False),
                             reads=[r_V, r_Sown[i]], writes=[r_pY[j_]], signal=False)
                        for sg_ in range(tau + 1):
                            K.op("pe", lambda e, tau=tau, sg_=sg_, oap=oap: e.matmul(oap, lhsT=Kd[kb:kb + 32, sl, tau - sg_, :], rhs=uso[i][kb:kb + 32, sg_, :], start=False, stop=(sg_ == tau)),
                                 reads=[r_Kd, r_uso[i]], writes=[r_pY[j_]], signal=(sg_ == tau and tau % 4 == 3))
                        if tau % 4 == 3:
                            yv = ysb[i][:].rearrange("p (c t) -> p c t", t=T16)
                            evac(yv[:, :, 4 * j_:4 * j_ + 4], pY[j_][0][:].rearrange("p (t c) -> p c t", t=4), [r_pY[j_]], [r_ysb[i]], eng="act")
                    K.dma("sp", yfm[32 * (q % 4):32 * (q % 4) + 32, q // 4, :], ysb[i][:], reads=[r_ysb[i]], dres=r_ysb[i])

                conv = []
                if stage >= 9 and not debug:
                    for name, src, rows, cols, c0 in (("Wglu", w_glu, 512, 2048, 0), ("Wfo", w_fox_o, 512, D, 0), ("Wg", w_in, D, 2048, 2056),
                                                      ("Wmx", w_mix, D, D, 0), ("Wmq", w_mem_q, D, 512, 0), ("Wmo", w_mem_o, 512, D, 0),
                                                      ("Wfi", w_ffn_in, D, 5632, 0), ("Wfo2", w_ffn_out, 2816, D, 0)):
                        wb[name] = dscr("wb_" + name, [rows, cols])
                        for k in range(rows // 128):
                            cc = 0
                            while cc < cols:
                                n = min(1024, cols - cc)
                                conv.append((src[k * 128:(k + 1) * 128, c0 + cc:c0 + cc + n], wb[name][k * 128:(k + 1) * 128, cc:cc + n], n))
                                cc += n
                cvb = [sb(p3, "cvb%d" % i_, [128, 1024], BF16) for i_ in range(2)]; r_cvb = [Res() for _ in range(2)]
                cstate = {"i": 0, "pend": []}

                def conv_step():
                    i_ = cstate["i"]
                    if i_ < len(conv):
                        src_ap, dst_ap, n = conv[i_]
                        si = wst_cnt[0] % NWST
                        wst_cnt[0] += 1
                        bi = i_ % 2
                        K.dma("sp", wst[si][:, 0:n], src_ap, writes=[r_wst[si]], dres=r_wst[si])
                        K.op("pool", lambda e: e.tensor_copy(out=cvb[bi][:, 0:n], in_=wst[si][:, 0:n]), reads=[r_wst[si]], writes=[r_cvb[bi]])
                        cstate["pend"].append((dst_ap, bi, n))
                        cstate["i"] += 1
                    while cstate["pend"] and (len(cstate["pend"]) > 1 or cstate["i"] >= len(conv)):
                        dst_ap, bi, n = cstate["pend"].pop(0)
                        K.dma("sp", dst_ap, cvb[bi][:, 0:n], reads=[r_cvb[bi]], dres=r_cvb[bi])

                load_pair(0)
                load_pair(1)
                stage_P(0)
                for q in range(16):
                    if q + 1 < 16:
                        stage_P(q + 1)
                    for _ in range(4):
                        conv_step()
                    stage_F(q)
                    if q + 2 < 16:
                        load_pair(q + 2)
                    for _ in range(4):
                        conv_step()
                while conv and (cstate["i"] < len(conv) or cstate["pend"]):
                    conv_step()
                K.barrier()
                for t in range(4):
                    K.op("act", lambda e, t=t: e.activation(out=yfm[:, t, :], in_=yfm[:, t, :], func=AF.Gelu_apprx_tanh), reads=[r_yfm], writes=[r_yfm])
                K.barrier()

            if debug:
                dbg["yfm"] = nc.dram_tensor("dbg_yfm", [128, 4, NOWN], BF16, kind="ExternalOutput").ap()
                K.dma("sp", dbg["yfm"][:, :, :], yfm[:], dres=Res())
            if stage <= 3:
                raise _Stop()
            attT = sb(es, "attT", [128, 4, NOWN], BF16)
            with contextlib.ExitStack() as p4:
                kth = [sb(p4, "kth%d" % i, [71, NT], BF16) for i in range(2)]; r_kth = [Res() for _ in range(2)]
                vah = [sb(p4, "vah%d" % i, [128, NB, 128], BF16) for i in range(2)]; r_vah = [Res() for _ in range(2)]
                qth = [sb(p4, "qth%d" % i, [71, NOWN], BF16) for i in range(2)]; r_qth = [Res() for _ in range(2)]
                pT = [sb(p4, "pT%d" % i, [128, 512], BF16) for i in range(4)]; r_pT = [Res() for _ in range(4)]
                osb = sb(p4, "osb", [128, 512]); r_osb = Res()
                rcp = sb(p4, "rcp", [128, 512]); r_rcp = Res()
                pss = [ps(p4, "pss%d" % i, [128, 512]) for i in range(4)]; r_pss = [Res() for _ in range(4)]
                pso = [ps(p4, "pso%d" % i, [128, 512]) for i in range(2)]; r_pso = [Res() for _ in range(2)]
                psb = ps(p4, "psb", [128, 512]); r_psb = Res()

                def load_head(h):
                    i = h % 2
                    K.dma("sp", kth[i][:], kT[h, :, :], writes=[r_kth[i]], dres=r_kth[i])
                    K.dma("sp", qth[i][:], qT[h, :, :], writes=[r_qth[i]], dres=r_qth[i])
                    K.dma("sp", vah[i][:], vA[h, :, :].rearrange("(b t) c -> t b c", t=128), writes=[r_vah[i]], dres=r_vah[i])

                NSB = 4
                LA = 3
                load_head(0)
                tiles = []
                for h in range(8):
                    for M in range(4):
                        written = [False] * 4
                        nkb = 16 * M + 16
                        for kb in range(nkb):
                            rel = kb - 16 * M - 3
                            i0 = 0 if rel <= 0 else (rel + 3) // 4
                            diag = (rel >= 0 and rel % 4 == 0)
                            st_flag = not written[i0]
                            assert all(written[x] == written[i0] for x in range(i0, 4))
                            for x in range(i0, 4):
                                written[x] = True
                            tiles.append(dict(h=h, M=M, kb=kb, c0=128 * i0, diag=diag, st=st_flag, last=(kb == nkb - 1), first=(kb == 0), g=h * 4 + M))

                def emit_qk(t, T):
                    h, M, kb, c0, diag = T["h"], T["M"], T["kb"], T["c0"], T["diag"]
                    i = h % 2
                    si = t % NSB
                    if not diag:
                        K.op("pe", lambda e: e.matmul(pss[si][:, c0:512], lhsT=kth[i][:, kb * 128:(kb + 1) * 128], rhs=qth[i][:, M * 512 + c0:(M + 1) * 512], start=True, stop=True),
                             reads=[r_kth[i], r_qth[i]], writes=[r_pss[si]], signal=True)
                    else:
                        if c0 + 128 < 512:
                            K.op("pe", lambda e: e.matmul(pss[si][:, c0 + 128:512], lhsT=kth[i][:, kb * 128:(kb + 1) * 128], rhs=qth[i][:, M * 512 + c0 + 128:(M + 1) * 512], start=True, stop=True),
                                 reads=[r_kth[i], r_qth[i]], writes=[r_pss[si]], signal=False)
                        K.op("pe", lambda e: e.matmul(pss[si][:, c0:c0 + 128], lhsT=kth[i][:, kb * 128:(kb + 1) * 128], rhs=qth[i][:, M * 512 + c0:M * 512 + c0 + 128], start=True, stop=False),
                             reads=[r_kth[i], r_qth[i]], writes=[r_pss[si]], signal=False)
                        K.op("pe", lambda e: e.matmul(pss[si][:, c0:c0 + 128], lhsT=identb[:], rhs=tmaskb[:], start=False, stop=True),
                             reads=[r_const], writes=[r_pss[si]], signal=True)
                    K.op("act", lambda e: e.activation(out=pT[si][:, c0:512], in_=pss[si][:, c0:512], func=AF.Exp), reads=[r_pss[si]], writes=[r_pT[si]])

                def emit_pv(t, T):
                    h, M, kb, c0 = T["h"], T["M"], T["kb"], T["c0"]
                    i = h % 2
                    si = t % NSB
                    oi = T["g"] % 2
                    odd = h % 2
                    mcols = 128 if odd else 65
                    st_flag, last = T["st"], T["last"]
                    if T["first"] and M == 0 and h + 1 < 8:
                        load_head(h + 1)
                    K.op("pe", lambda e: e.matmul(pso[oi][0:mcols, c0:512], lhsT=vah[i][:, kb, 0:mcols], rhs=pT[si][:, c0:512], start=st_flag, stop=last),
                         reads=[r_vah[i], r_pT[si]], writes=[r_pso[oi]], signal=last)
                    if last:
                        K.op("act", lambda e: e.activation(out=osb[0:mcols, :], in_=pso[oi][0:mcols, :], func=AF.Copy), reads=[r_pso[oi]], writes=[r_osb])
                        if odd:
                            K.op("pe", lambda e: e.matmul(psb[:], lhsT=self_[0:1, :], rhs=osb[0:1, :], start=True, stop=True), reads=[r_osb, r_const], writes=[r_psb])
                        else:
                            K.op("pe", lambda e: e.matmul(psb[0:64, :], lhsT=self_[64:65, 0:64], rhs=osb[64:65, :], start=True, stop=True), reads=[r_osb, r_const], writes=[r_psb])
                        lo = 64 if odd else 0
                        K.op("dve", lambda e: e.reciprocal(out=rcp[lo:lo + 64, :], in_=psb[lo:lo + 64, :]), reads=[r_psb], writes=[r_rcp])
                        K.op("dve", lambda e: e.tensor_tensor(out=attT[lo:lo + 64, h // 2, M * 512:(M + 1) * 512], in0=osb[lo:lo + 64, :], in1=rcp[lo:lo + 64, :], op=ALU.mult),
                             reads=[r_osb, r_rcp], writes=[r_att])

                nt = len(tiles)
                for t in range(nt + LA):
                    if t < nt:
                        emit_qk(t, tiles[t])
                    if t - LA >= 0:
                        emit_pv(t - LA, tiles[t - LA])
                K.barrier()

            if debug:
                dbg["att"] = nc.dram_tensor("dbg_att", [128, 4, NOWN], BF16, kind="ExternalOutput").ap()
                K.dma("sp", dbg["att"][:, :, :], attT[:], dres=Res())
            if stage <= 4:
                raise _Stop()
            kmT = sb(es, "kmT", [128, 4, 256], BF16)
            vmem = sb(es, "vmem", [128, 2, 512], BF16)
            r_km = Res("km")
            with contextlib.ExitStack() as pm:
                gkv = sb(pm, "gkv", [128, D]); r_gkv = Res()
                K.dma("sp", gkv[:], g_memkv[0:1, :].partition_broadcast(128), writes=[r_gkv], dres=r_gkv)
                Wkv = sb(pm, "Wkv", [128, 8, D], BF16); r_Wkv = Res()
                for k in range(8):
                    cast_load(lambda cc, n, k=k: Wkv[:, k, cc:cc + n], r_Wkv, w_mem_kv, k * 128, D, 0)
                mt = sb(pm, "mt", [128, 2, D]); r_mt = Res()
                K.dma("sp", mt[:], mem.rearrange("(b t) d -> t b d", t=128), writes=[r_mt], dres=r_mt)
                junk = sb(pm, "junkm", [128, D], BF16); r_junk = Res()
                ssm_ = sb(pm, "ssm_", [128, 1]); r_ssm = Res(); rsm = sb(pm, "rsm", [128, 1]); r_rsm = Res()
                mb = sb(pm, "mb", [128, D], BF16); r_mb = Res()
                mT = sb(pm, "mT", [128, 8, 256], BF16); r_mT = Res()
                ptm = ps(pm, "ptm", [128, 1024], BF16); r_ptm = Res()
                pkm = [ps(pm, "pkm%d" % i, [128, 512]) for i in range(2)]; r_pkm = [Res() for _ in range(2)]
                for b2 in range(2):
                    rmsnorm_tok("pm", mt[:, b2, :], r_mt, gkv[:], r_gkv, mb[:], r_mb, junk[:], r_junk, ssm_[:], r_ssm, rsm[:], r_rsm)
                    for k in range(8):
                        K.op("pe", lambda e, k=k: e.transpose(out=ptm[:, k * 128:(k + 1) * 128], in_=mb[:, k * 128:(k + 1) * 128], identity=identb[:]), reads=[r_mb, r_const], writes=[r_ptm], signal=(k == 7))
                    evac(mT[:, :, b2 * 128:(b2 + 1) * 128], ptm[:].rearrange("p (k t) -> p k t", k=8), [r_ptm], [r_mT])
                for hh in range(4):
                    pi = hh % 2
                    for k in range(8):
                        K.op("pe", lambda e, k=k, hh=hh, pi=pi: e.matmul(pkm[pi][:, 0:256], lhsT=Wkv[:, k, hh * 128:(hh + 1) * 128], rhs=mT[:, k, :], start=(k == 0), stop=(k == 7)),
                             reads=[r_Wkv, r_mT], writes=[r_pkm[pi]], signal=(k == 7))
                    evac(kmT[:, hh, :], pkm[pi][:, 0:256], [r_pkm[pi]], [r_km])
                for b2 in range(2):
                    pi = b2 % 2
                    for k in range(8):
                        K.op("pe", lambda e, k=k, b2=b2, pi=pi: e.matmul(pkm[pi][:], lhsT=mT[:, k, b2 * 128:(b2 + 1) * 128], rhs=Wkv[:, k, 512:1024], start=(k == 0), stop=(k == 7)),
                             reads=[r_Wkv, r_mT], writes=[r_pkm[pi]], signal=(k == 7))
                    evac(vmem[:, b2, :], pkm[pi][:], [r_pkm[pi]], [r_km])
                K.barrier()

            wcache = {}

            def load_w(st, name, src, rows, cols, c0=0):
                nk = rows // 128
                t = sb(st, name, [128, nk, cols], BF16)
                r = Res(name)
                if name in wb:
                    K.dma("sp", t[:], wb[name].rearrange("(k p) c -> p k c", p=128), writes=[r], dres=r)
                    return t, r
                if name in wcache:
                    K.dma("sp", t[:], wcache[name].rearrange("(k p) c -> p k c", p=128), writes=[r], dres=r)
                    return t, r
                for k in range(nk):
                    cast_load(lambda cc, n, k=k: t[:, k, cc:cc + n], r, src, k * 128, cols, c0)
                if not debug:
                    wcache[name] = dscr("wc_" + name, [rows, cols])
                    K.dma("act", wcache[name].rearrange("(k p) c -> p k c", p=128), t[:], reads=[r], dres=Res())
                return t, r

            NH = 1024
            for half in range(2):
                tok0 = half * NH
                with contextlib.ExitStack() as ph:
                    mixed = sb(ph, "mixed", [128, 8, NH], BF16); r_mixed = Res()
                    with contextlib.ExitStack() as pa_:
                        Wglu, r_Wglu = load_w(pa_, "Wglu", w_glu, 512, 2048)
                        uTh = sb(pa_, "uTh", [128, 8, NH], BF16); r_uTh = Res()
                        for k in range(8):
                            K.dma("sp", uTh[:, k, :], uTo[k * 128:(k + 1) * 128, tok0:tok0 + NH], writes=[r_uTh] if k == 0 else [], dres=r_uTh)
                        r_uTh.w = (r_uTh.dsem, K.cnt[r_uTh.dsem])
                        Wg = sb(pa_, "Wg", [128, 8, 2048], BF16); r_Wga = Res("Wga"); r_Wgb = Res("Wgb")
                        wgv = wb["Wg"].rearrange("(k p) c -> p k c", p=128)
                        K.dma("sp", Wg[:, :, 0:D], wgv[:, :, 0:D], writes=[r_Wga], dres=r_Wga)
                        Wfo, r_Wfo = load_w(pa_, "Wfo", w_fox_o, 512, D)
                        K.dma("sp", Wg[:, :, D:2048], wgv[:, :, D:2048], writes=[r_Wgb], dres=r_Wgb)
                        pA = [ps(pa_, "pA%d" % i, [128, 512]) for i in range(2)]; r_pA = [Res() for _ in range(2)]
                        pB = [ps(pa_, "pB%d" % i, [128, 512]) for i in range(2)]; r_pB = [Res() for _ in range(2)]
                        pG = [ps(pa_, "pG%d" % i, [128, 512]) for i in range(2)]; r_pG = [Res() for _ in range(2)]
                        sg = [sb(pa_, "sg%d" % i, [128, 512]) for i in range(2)]; r_sg = [Res() for _ in range(2)]
                        oa = [sb(pa_, "oa%d" % i, [128, 512]) for i in range(2)]; r_oa = [Res() for _ in range(2)]
                        ob = [sb(pa_, "ob%d" % i, [128, 512]) for i in range(2)]; r_ob = [Res() for _ in range(2)]
                        cnt = 0
                        for ft in range(8):
                            for ch in range(2):
                                x = cnt % 2
                                cnt += 1
                                tsl = slice(tok0 + ch * 512, tok0 + (ch + 1) * 512)
                                lsl = slice(ch * 512, (ch + 1) * 512)
                                for k in range(4):
                                    K.op("pe", lambda e, k=k, x=x, ft=ft, tsl=tsl: e.matmul(pA[x][:], lhsT=Wglu[:, k, ft * 128:(ft + 1) * 128], rhs=yfm[:, k, tsl], start=(k == 0), stop=(k == 3)),
                                         reads=[r_Wglu, r_yfm], writes=[r_pA[x]], signal=(k == 3))
                                for k in range(4):
                                    K.op("pe", lambda e, k=k, x=x, ft=ft, tsl=tsl: e.matmul(pB[x][:], lhsT=Wglu[:, k, D + ft * 128:D + (ft + 1) * 128], rhs=yfm[:, k, tsl], start=(k == 0), stop=(k == 3)),
                                         reads=[r_Wglu, r_yfm], writes=[r_pB[x]], signal=(k == 3))
                                K.op("act", lambda e, x=x: e.activation(out=sg[x][:], in_=pB[x][:], func=AF.Sigmoid), reads=[r_pB[x]], writes=[r_sg[x]])
                                K.op("dve", lambda e, x=x: e.tensor_tensor(out=oa[x][:], in0=pA[x][:], in1=sg[x][:], op=ALU.mult), reads=[r_pA[x], r_sg[x]], writes=[r_oa[x]])
                                for k in range(8):
                                    K.op("pe", lambda e, k=k, x=x, ft=ft, lsl=lsl: e.matmul(pG[x][:], lhsT=Wg[:, k, ft * 128:(ft + 1) * 128], rhs=uTh[:, k, lsl], start=(k == 0), stop=(k == 7)),
                                         reads=[r_Wga, r_uTh], writes=[r_pG[x]], signal=(k == 7))
                                K.op("act", lambda e, x=x: e.activation(out=sg[x][:], in_=pG[x][:], func=AF.Sigmoid), reads=[r_pG[x], r_oa[x]], writes=[r_sg[x]])
                                K.op("pool", lambda e, x=x: e.tensor_tensor(out=oa[x][:], in0=oa[x][:], in1=sg[x][:], op=ALU.mult), reads=[r_oa[x], r_sg[x]], writes=[r_oa[x]])
                                for k in range(4):
                                    K.op("pe", lambda e, k=k, x=x, ft=ft, tsl=tsl: e.matmul(pA[x][:], lhsT=Wfo[:, k, ft * 128:(ft + 1) * 128], rhs=attT[:, k, tsl], start=(k == 0), stop=(k == 3)),
                                         reads=[r_Wfo, r_att], writes=[r_pA[x]], signal=(k == 3))
                                for k in range(8):
                                    K.op("pe", lambda e, k=k, x=x, ft=ft, lsl=lsl: e.matmul(pG[x][:], lhsT=Wg[:, k, D + ft * 128:D + (ft + 1) * 128], rhs=uTh[:, k, lsl], start=(k == 0), stop=(k == 7)),
                                         reads=[r_Wgb, r_uTh], writes=[r_pG[x]], signal=(k == 7))
                                K.op("act", lambda e, x=x: e.activation(out=sg[x][:], in_=pG[x][:], func=AF.Sigmoid), reads=[r_pG[x], r_oa[x]], writes=[r_sg[x]])
                                K.op("dve", lambda e, x=x: e.tensor_tensor(out=ob[x][:], in0=pA[x][:], in1=sg[x][:], op=ALU.mult), reads=[r_pA[x], r_sg[x]], writes=[r_ob[x]])
                                K.op("pool", lambda e, x=x, ft=ft, lsl=lsl: e.tensor_tensor(out=mixed[:, ft, lsl], in0=oa[x][:], in1=ob[x][:], op=ALU.add), reads=[r_oa[x], r_ob[x]], writes=[r_mixed])
                        K.barrier()

                    hres = sb(ph, "hres", [128, 8, D]); r_h = [Res() for _ in range(8)]
                    for bb in range(8):
                        gb = 4 * (8 * half + bb) + 3
                        K.dma("sp", hres[:, bb, :], xp[gb * 128:(gb + 1) * 128, :], writes=[r_h[bb]], dres=r_h[bb])

                    def norm_T(st, tag, gain_src, nT, r_nT):
                        gt = sb(st, tag + "g", [128, D]); r_gt = Res()
                        K.dma("sp", gt[:], gain_src[0:1, :].partition_broadcast(128), writes=[r_gt], dres=r_gt)
                        junk = sb(st, tag + "junk", [128, D], BF16); r_junk = Res()
                        ss8 = sb(st, tag + "ss8", [128, 8]); r_ss8 = Res()
                        rs8 = sb(st, tag + "rs8", [128, 8]); r_rs8 = Res()
                        nb = [sb(st, tag + "nb%d" % i, [128, D], BF16) for i in range(2)]; r_nb = [Res() for _ in range(2)]
                        ptn = [ps(st, tag + "pt%d" % i, [128, 1024], BF16) for i in range(2)]; r_ptn = [Res() for _ in range(2)]
                        for bb in range(8):
                            K.op("act", lambda e, bb=bb: e.activation(out=junk[:], in_=hres[:, bb, :], func=AF.Square, accum_out=ss8[:, bb:bb + 1]), reads=[r_h[bb]], writes=[r_junk, r_ss8])
                        K.op("dve", lambda e: e.tensor_scalar(out=rs8[:], in0=ss8[:], scalar1=1.0 / D, scalar2=EPS, op0=ALU.mult, op1=ALU.add), reads=[r_ss8], writes=[r_rs8])
                        K.op("act", lambda e: e.activation(out=rs8[:], in_=rs8[:], func=AF.Sqrt), reads=[r_rs8], writes=[r_rs8])
                        K.op("dve", lambda e: e.reciprocal(out=rs8[:], in_=rs8[:]), reads=[r_rs8], writes=[r_rs8])
                        for bb in range(8):
                            x = bb % 2
                            K.op("dve", lambda e, bb=bb, x=x: e.scalar_tensor_tensor(out=nb[x][:], in0=hres[:, bb, :], scalar=rs8[:, bb:bb + 1], in1=gt[:], op0=ALU.mult, op1=ALU.mult),
                                 reads=[r_h[bb], r_rs8, r_gt], writes=[r_nb[x]])
                            for k in range(8):
                                K.op("pe", lambda e, k=k, x=x: e.transpose(out=ptn[x][:, k * 128:(k + 1) * 128], in_=nb[x][:, k * 128:(k + 1) * 128], identity=identb[:]),
                                     reads=[r_nb[x], r_const], writes=[r_ptn[x]], signal=(k == 7))
                            evac(nT[:, :, bb * 128:(bb + 1) * 128], ptn[x][:].rearrange("p (k t) -> p k t", k=8), [r_ptn[x]], [r_nT])

                    def proj_add(st, tag, actT, r_actT, nk, W, r_W):
                        pr = [ps(st, tag + "pr%d" % i, [128, 512]) for i in range(2)]; r_pr = [Res() for _ in range(2)]
                        c = 0
                        for bb in range(8):
                            for cc in range(2):
                                x = c % 2
                                c += 1
                                for k in range(nk):
                                    K.op("pe", lambda e, k=k, x=x, bb=bb, cc=cc: e.matmul(pr[x][:], lhsT=actT[:, k, bb * 128:(bb + 1) * 128], rhs=W[:, k, cc * 512:(cc + 1) * 512], start=(k == 0), stop=(k == nk - 1)),
                                         reads=[r_actT, r_W], writes=[r_pr[x]], signal=(k == nk - 1))
                                K.op("dve", lambda e, x=x, bb=bb, cc=cc: e.tensor_tensor(out=hres[:, bb, cc * 512:(cc + 1) * 512], in0=pr[x][:], in1=hres[:, bb, cc * 512:(cc + 1) * 512], op=ALU.add),
                                     reads=[r_pr[x], r_h[bb]], writes=[r_h[bb]])

                    with contextlib.ExitStack() as pb_:
                        Wmx, r_Wmx = load_w(pb_, "Wmx", w_mix, D, D)
                        proj_add(pb_, "mx", mixed, r_mixed, 8, Wmx, r_Wmx)
                        K.barrier()
                    with contextlib.ExitStack() as pc_:
                        nT = sb(pc_, "nT", [128, 8, NH], BF16); r_nT = Res()
                        Wmq, r_Wmq = load_w(pc_, "Wmq", w_mem_q, D, 512)
                        Wmo, r_Wmo = load_w(pc_, "Wmo", w_mem_o, 512, D)
                        with contextlib.ExitStack() as pn_:
                            norm_T(pn_, "n5", g_memq, nT, r_nT)
                            K.barrier()
                        qm = sb(pc_, "qm", [128, 4, NH], BF16); r_qm = Res()
                        om = sb(pc_, "om", [128, 4, NH], BF16); r_om = Res()
                        with contextlib.ExitStack() as pq_:
                            pq = [ps(pq_, "pq%d" % i, [128, 512]) for i in range(2)]; r_pq = [Res() for _ in range(2)]
                            pS = [ps(pq_, "pS%d" % i, [128, 512]) for i in range(2)]; r_pS = [Res() for _ in range(2)]
                            pO = [ps(pq_, "pO%d" % i, [128, 512]) for i in range(2)]; r_pO = [Res() for _ in range(2)]
                            pD = [ps(pq_, "pD%d" % i, [128, 512]) for i in range(2)]; r_pD = [Res() for _ in range(2)]
                            pe_ = [[sb(pq_, "pe%d_%d" % (i, j_), [128, 512], BF16) for j_ in range(2)] for i in range(2)]
                            r_pe = [[Res() for _ in range(2)] for _ in range(2)]
                            rc = [sb(pq_, "rc%d" % i, [128, 512]) for i in range(2)]; r_rc = [Res() for _ in range(2)]
                            r_qmi = [Res() for _ in range(8)]
                            its = [(hh, ch) for hh in range(4) for ch in range(2)]

                            def st_Q(it):
                                hh, ch = its[it]
                                x = it % 2
                                lsl = slice(ch * 512, (ch + 1) * 512)
                                for k in range(8):
                                    K.op("pe", lambda e, k=k: e.matmul(pq[x][:], lhsT=Wmq[:, k, hh * 128:(hh + 1) * 128], rhs=nT[:, k, lsl], start=(k == 0), stop=(k == 7)),
                                         reads=[r_Wmq, r_nT], writes=[r_pq[x]], signal=(k == 7))
                                evac(qm[:, hh, lsl], pq[x][:], [r_pq[x]], [r_qmi[it]], scale=1.0 / math.sqrt(128.0))

                            def st_S(it):
                                hh, ch = its[it]
                                x = it % 2
                                lsl = slice(ch * 512, (ch + 1) * 512)
                                for mt_ in range(2):
                                    K.op("pe", lambda e, mt_=mt_: e.matmul(pS[mt_][:], lhsT=kmT[:, hh, mt_ * 128:(mt_ + 1) * 128], rhs=qm[:, hh, lsl], start=True, stop=True),
                                         reads=[r_km, r_qmi[it]], writes=[r_pS[mt_]])
                                    K.op("act", lambda e, mt_=mt_: e.activation(out=pe_[x][mt_][:], in_=pS[mt_][:], func=AF.Exp), reads=[r_pS[mt_]], writes=[r_pe[x][mt_]])

                            def st_O(it):
                                hh, ch = its[it]
                                x = it % 2
                                lsl = slice(ch * 512, (ch + 1) * 512)
                                for mt_ in range(2):
                                    K.op("pe", lambda e, mt_=mt_: e.matmul(pO[x][:], lhsT=vmem[:, mt_, hh * 128:(hh + 1) * 128], rhs=pe_[x][mt_][:], start=(mt_ == 0), stop=(mt_ == 1)),
                                         reads=[r_km, r_pe[x][mt_]], writes=[r_pO[x]], signal=(mt_ == 1))
                                for mt_ in range(2):
                                    K.op("pe", lambda e, mt_=mt_: e.matmul(pD[x][:], lhsT=onesb[:], rhs=pe_[x][mt_][:], start=(mt_ == 0), stop=(mt_ == 1)),
                                         reads=[r_pe[x][mt_]], writes=[r_pD[x]], signal=(mt_ == 1))
                                K.op("dve", lambda e: e.reciprocal(out=rc[x][:], in_=pD[x][:]), reads=[r_pD[x]], writes=[r_rc[x]])
                                K.op("dve", lambda e: e.tensor_tensor(out=om[:, hh, lsl], in0=pO[x][:], in1=rc[x][:], op=ALU.mult), reads=[r_pO[x], r_rc[x]], writes=[r_om])

                            st_Q(0)
                            st_Q(1)
                            for it in range(8):
                                st_S(it)
                                if it + 2 < 8:
                                    st_Q(it + 2)
                                st_O(it)
                            K.barrier()
                        with contextlib.ExitStack() as po_:
                            proj_add(po_, "mo", om, r_om, 4, Wmo, r_Wmo)
                            K.barrier()
                    with contextlib.ExitStack() as pf_:
                        hid = sb(pf_, "hid", [128, 22, NH], BF16); r_hid = Res()
                        with contextlib.ExitStack() as pg_:
                            n2T = sb(pg_, "n2T", [128, 8, NH], BF16); r_n2T = Res()
                            with contextlib.ExitStack() as pn_:
                                norm_T(pn_, "n6", g_ffn, n2T, r_n2T)
                                K.barrier()
                            wfa = [sb(pg_, "wfa%d" % i, [128, 8, 2, 512], BF16) for i in range(2)]; r_wfa = [Res() for _ in range(2)]
                            pfa = [ps(pg_, "pfa%d" % i, [128, 512]) for i in range(2)]; r_pfa = [Res() for _ in range(2)]
                            pfb = [ps(pg_, "pfb%d" % i, [128, 512]) for i in range(2)]; r_pfb = [Res() for _ in range(2)]
                            sl_ = [sb(pg_, "sl%d" % i, [128, 512]) for i in range(2)]; r_sl = [Res() for _ in range(2)]
                            NG = 6

                            def load_ff(g):
                                w = g % 2
                                nt_ = min(4, 22 - 4 * g)
                                key = "ffin%d" % g
                                if "Wfi" in wb:
                                    wv_ = wb["Wfi"].rearrange("(k p) c -> p k c", p=128)
                                    for ab in range(2):
                                        K.dma("sp", wfa[w][:, :, ab, 0:nt_ * 128], wv_[:, :, ab * 2816 + g * 512:ab * 2816 + g * 512 + nt_ * 128], writes=[r_wfa[w]], dres=r_wfa[w])
                                    return
                                if key in wcache:
                                    K.dma("sp", wfa[w][:], wcache[key][:, :, :, :], writes=[r_wfa[w]], dres=r_wfa[w])
                                    return
                                for k in range(8):
                                    for ab in range(2):
                                        cast_load(lambda cc, n, k=k, ab=ab, w=w: wfa[w][:, k, ab, cc:cc + n], r_wfa[w], w_ffn_in, k * 128, nt_ * 128, ab * 2816 + g * 512)
                                if not debug:
                                    wcache[key] = dscr("wc_" + key, [128, 8, 2, 512])
                                    K.dma("act", wcache[key][:, :, :, :], wfa[w][:], reads=[r_wfa[w]], dres=Res())

                            load_ff(0)
                            c = 0
                            for g in range(NG):
                                w = g % 2
                                if g + 1 < NG:
                                    load_ff(g + 1)
                                for hi in range(min(4, 22 - 4 * g)):
                                    ht = 4 * g + hi
                                    for ch in range(2):
                                        x = c % 2
                                        c += 1
                                        lsl = slice(ch * 512, (ch + 1) * 512)
                                        for k in range(8):
                                            K.op("pe", lambda e, k=k, x=x, w=w, lsl=lsl, hi=hi: e.matmul(pfa[x][:], lhsT=wfa[w][:, k, 0, hi * 128:(hi + 1) * 128], rhs=n2T[:, k, lsl], start=(k == 0), stop=(k == 7)),
                                                 reads=[r_wfa[w], r_n2T], writes=[r_pfa[x]], signal=(k == 7))
                                        for k in range(8):
                                            K.op("pe", lambda e, k=k, x=x, w=w, lsl=lsl, hi=hi: e.matmul(pfb[x][:], lhsT=wfa[w][:, k, 1, hi * 128:(hi + 1) * 128], rhs=n2T[:, k, lsl], start=(k == 0), stop=(k == 7)),
                                                 reads=[r_wfa[w], r_n2T], writes=[r_pfb[x]], signal=(k == 7))
                                        K.op("act", lambda e, x=x: e.activation(out=sl_[x][:], in_=pfa[x][:], func=AF.Silu), reads=[r_pfa[x]], writes=[r_sl[x]])
                                        K.op("dve", lambda e, x=x, ht=ht, lsl=lsl: e.tensor_tensor(out=hid[:, ht, lsl], in0=pfb[x][:], in1=sl_[x][:], op=ALU.mult), reads=[r_pfb[x], r_sl[x]], writes=[r_hid])
                            K.barrier()
                        with contextlib.ExitStack() as po_:
                            Wfo2, r_Wfo2 = load_w(po_, "Wfo2", w_ffn_out, 2816, D)
                            proj_add(po_, "fo", hid, r_hid, 22, Wfo2, r_Wfo2)
                            K.barrier()
                    with contextlib.ExitStack() as pz_:
                        gt = sb(pz_, "gfin", [128, D]); r_gt = Res()
                        K.dma("sp", gt[:], g_fin[0:1, :].partition_broadcast(128), writes=[r_gt], dres=r_gt)
                        junk = sb(pz_, "junkz", [128, D], BF16); r_junk = Res()
                        ssz = [sb(pz_, "ssz%d" % i, [128, 1]) for i in range(2)]; r_ssz = [Res() for _ in range(2)]
                        rsz = [sb(pz_, "rsz%d" % i, [128, 1]) for i in range(2)]; r_rsz = [Res() for _ in range(2)]
                        ot = [sb(pz_, "ot%d" % i, [128, D]) for i in range(2)]; r_ot = [Res() for _ in range(2)]
                        r_out = Res("out")
                        for bb in range(8):
                            x = bb % 2
                            rmsnorm_tok("fz", hres[:, bb, :], r_h[bb], gt[:], r_gt, ot[x][:], r_ot[x], junk[:], r_junk, ssz[x][:], r_ssz[x], rsz[x][:], r_rsz[x])
                            m = 8 * half + bb
                            K.dma("sp", out[m * 128:(m + 1) * 128, :], ot[x][:], reads=[r_ot[x]], dres=r_ot[x])
                        K.barrier()
        except _Stop:
            pass
        K.barrier()
    return nc, K.rec


_CACHE = {}


def _consts():
    c = {}
    c["c_idx"] = np.tile(np.arange(512, dtype=np.float32)[None, :], (128, 1))
    c["c_tau"] = np.tile(np.arange(1, 129, dtype=np.float32)[None, :], (128, 1))
    rs = np.ones((128, 512), np.float32)
    rs[:, 0::128] = 0.0
    c["c_reset"] = rs
    c["c_ident"] = np.eye(128, dtype=np.float32)
    s = np.arange(128)
    c["c_triu"] = (s[:, None] <= s[None, :]).astype(np.float32)
    e = np.zeros((128, 128), np.float32)
    e[127, :] = 1.0
    c["c_e127"] = e
    c["c_tmask"] = np.where(s[:, None] <= s[None, :], 0.0, NEG).astype(np.float32)
    sel = np.zeros((128, 128), np.float32)
    sel[0, :] = 1.0
    sel[64, 0:64] = 1.0
    c["c_sel"] = sel
    return c


def _prep(x, mem, norm_mix, w_in, b_forget, lam_re, lam_im, log_dt, b_re, b_im, c_re, c_im,
          d_skip, w_glu, w_fox_o, w_mix_out, norm_mem_q, norm_mem_kv, w_mem_q, w_mem_kv,
          w_mem_o, norm_ffn, w_ffn_in, w_ffn_out, norm_final):
    f = lambda a: np.ascontiguousarray(np.asarray(a, dtype=np.float32))
    x = f(x); mem = f(mem)
    lam_re = f(lam_re)[0]; lam_im = f(lam_im)[0]; log_dt = f(log_dt)[0]
    b_re = f(b_re)[0]; b_im = f(b_im)[0]; c_re = f(c_re)[0]; c_im = f(c_im)[0]; d_skip = f(d_skip)[0]
    lamr_c = np.full((96, 6, 128), -1.0, np.float32); lami_c = np.zeros((96, 6, 128), np.float32)
    ldt_c = np.zeros((96, 6, 128), np.float32)
    Br_c = np.zeros((96, 6, 128), np.float32); Bi_c = np.zeros((96, 6, 128), np.float32)
    Dm = np.zeros((96, 6, 32), np.float32)
    lamr_r = np.zeros((128, 16), np.float32); lami_r = np.zeros((128, 16), np.float32); ldt_r = np.zeros((128, 16), np.float32)
    Cr_r = np.zeros((128, 16, 32), np.float32); Ci_r = np.zeros((128, 16, 32), np.float32)
    Brow_r = np.zeros((128, 16, 32), np.float32); Brow_i = np.zeros((128, 16, 32), np.float32)
    for q in range(16):
        sl, kk = q // 3, q % 3
        for g2 in range(2):
            g = 2 * q + g2
            lamr_c[32 * kk:32 * kk + 32, sl, 64 * g2:64 * g2 + 64] = lam_re[g][None, :]
            lami_c[32 * kk:32 * kk + 32, sl, 64 * g2:64 * g2 + 64] = lam_im[g][None, :]
            ldt_c[32 * kk:32 * kk + 32, sl, 64 * g2:64 * g2 + 64] = log_dt[g]
            Br_c[32 * kk + 16 * g2:32 * kk + 16 * g2 + 16, sl, 64 * g2:64 * g2 + 64] = b_re[g].T
            Bi_c[32 * kk + 16 * g2:32 * kk + 16 * g2 + 16, sl, 64 * g2:64 * g2 + 64] = b_im[g].T
            for m in range(16):
                Dm[32 * kk + 16 * g2 + m, sl, 16 * g2 + m] = d_skip[16 * g + m]
            lamr_r[64 * g2:64 * g2 + 64, q] = lam_re[g]
            lami_r[64 * g2:64 * g2 + 64, q] = lam_im[g]
            ldt_r[64 * g2:64 * g2 + 64, q] = log_dt[g]
            Cr_r[64 * g2:64 * g2 + 64, q, 16 * g2:16 * g2 + 16] = c_re[g].T
            Ci_r[64 * g2:64 * g2 + 64, q, 16 * g2:16 * g2 + 16] = c_im[g].T
            Brow_r[64 * g2:64 * g2 + 64, q, 16 * g2:16 * g2 + 16] = b_re[g]
            Brow_i[64 * g2:64 * g2 + 64, q, 16 * g2:16 * g2 + 16] = b_im[g]
    shared = dict(
        w_in=f(w_in)[0], g_mix=f(norm_mix), g_memq=f(norm_mem_q), g_memkv=f(norm_mem_kv), g_ffn=f(norm_ffn),
        g_fin=f(norm_final).reshape(1, D), b_forget=f(b_forget),
        lamr_c=lamr_c.reshape(96, 768), lami_c=lami_c.reshape(96, 768), ldt_c=ldt_c.reshape(96, 768),
        Br_c=Br_c.reshape(96, 768), Bi_c=Bi_c.reshape(96, 768),
        lamr_r=lamr_r, lami_r=lami_r, ldt_r=ldt_r, Cr_r=Cr_r.reshape(128, 512), Ci_r=Ci_r.reshape(128, 512),
        Dm=Dm.reshape(96, 192), Brow_r=Brow_r.reshape(128, 512), Brow_i=Brow_i.reshape(128, 512),
        w_glu=f(w_glu)[0], w_fox_o=f(w_fox_o)[0], w_mix=f(w_mix_out)[0], w_mem_q=f(w_mem_q)[0], w_mem_kv=f(w_mem_kv)[0],
        w_mem_o=f(w_mem_o)[0], w_ffn_in=f(w_ffn_in)[0], w_ffn_out=f(w_ffn_out)[0],
    )
    shared.update(_consts())
    in_maps = []
    for c in range(8):
        b, j = c // 4, c % 4
        npad = (3 - j) * 128
        xpad = np.zeros((NT, D), np.float32)
        xpad[npad:] = x[b, :NT - npad]
        pr = np.zeros((1, NT), np.float32)
        pr[0, :npad] = NEG
        m = dict(shared)
        m["xp"] = xpad
        m["padrow"] = pr
        m["mem"] = mem[b]
        in_maps.append(m)
    return in_maps


def kernel(**inputs):
    in_maps = _prep(**inputs)
    if "nc" not in _CACHE:
        _CACHE["nc"] = build()
    res = run_bass_kernel_spmd(_CACHE["nc"], in_maps, core_ids=list(range(8)))
    outp = np.zeros((2, 8192, D), np.float32)
    for c in range(8):
        b, j = c // 4, c % 4
        o = np.asarray(res.results[c]["out"]).reshape(16, 128, D)
        for m in range(16):
            gblk = 4 * m + j
            outp[b, gblk * 128:(gblk + 1) * 128, :] = o[m]
    return outp
```

```python
import contextlib
import os
import math
import numpy as np
import concourse.bass as bass
import concourse.mybir as mybir
from concourse.bass_utils import run_bass_kernel_spmd

F32 = mybir.dt.float32
BF16 = mybir.dt.bfloat16
I32 = mybir.dt.int32
AF = mybir.ActivationFunctionType
ALU = mybir.AluOpType

D = 1024
NB = 64
NT = NB * 128
NOWN = 2048
EPS = 1e-6
NEG = -30000.0
TWO_PI = 2.0 * math.pi


class _Stop(Exception):
    pass


class Res:
    __slots__ = ("w", "r", "dsem", "dcnt", "name")

    def __init__(self, name=""):
        self.w = None
        self.r = {}
        self.dsem = None
        self.dcnt = 0
        self.name = name


class Ker:
    def __init__(self, nc, es, needed=None):
        self.nc = nc
        self.es = es
        self.needed = needed
        self.rec = set()
        self.pcnt = {}
        self.pmap = {}
        self.eng = {"pe": nc.tensor, "act": nc.scalar, "dve": nc.vector, "pool": nc.gpsimd, "sp": nc.sync}
        self.sem = {}
        self.cnt = {}
        for e in ("pe", "act", "dve", "pool"):
            self.sem[e] = es.enter_context(nc.semaphore("s_" + e))
            self.cnt[e] = 0
            self.pcnt[e] = 0
        self.waited = {e: {} for e in self.eng}
        self.free_d = []
        self.phase_res = []
        self.ndsem = 0

    def new_dsem(self):
        if self.free_d:
            return self.free_d.pop()
        s = self.es.enter_context(self.nc.semaphore("d%d" % self.ndsem))
        key = "d%d" % self.ndsem
        self.ndsem += 1
        self.sem[key] = s
        self.cnt[key] = 0
        return key

    def _need(self, e, tok, needs):
        if tok is None:
            return
        k, v = tok
        if k == "pe" and e == "pe":
            return
        if needs.get(k, 0) < v:
            needs[k] = v

    def _waits(self, e, reads, writes, skip_key=None):
        needs = {}
        for r in reads:
            self._need(e, r.w, needs)
            for k, v in r.r.items():
                if k != e:
                    self._need(e, (k, v), needs)
        for r in writes:
            self._need(e, r.w, needs)
            for k, v in r.r.items():
                self._need(e, (k, v), needs)
        wd = self.waited[e]
        for k, v in needs.items():
            if k == skip_key:
                continue
            if wd.get(k, 0) < v:
                self._emit_wait(e, k, v)
                wd[k] = v

    def _emit_wait(self, e, k, v):
        if k in self.pcnt:
            self.rec.add((k, v))
            pv = v if self.needed is None else self.pmap[(k, v)]
        else:
            pv = v
        self.eng[e].wait_ge(self.sem[k], pv)

    def op(self, e, fn, reads=(), writes=(), signal=True):
        self._waits(e, reads, writes)
        ins = fn(self.eng[e])
        if signal:
            self.cnt[e] += 1
            if self.needed is None or (e, self.cnt[e]) in self.needed:
                self.pcnt[e] += 1
                self.pmap[(e, self.cnt[e])] = self.pcnt[e]
                ins.then_inc(self.sem[e], 1)
            tok = (e, self.cnt[e])
        else:
            tok = (e, self.cnt[e] + 1)
        for r in writes:
            r.w = tok
            r.r = {}
        for r in reads:
            if r.r.get(tok[0], 0) < tok[1]:
                r.r[tok[0]] = tok[1]
        return tok

    def dma(self, q, out, in_, reads=(), writes=(), dres=None):
        if dres.dsem is None:
            dres.dsem = self.new_dsem()
            self.phase_res.append(dres)
        k = dres.dsem
        self._waits(q, reads, writes, skip_key=k)
        self.eng[q].dma_start(out=out, in_=in_).then_inc(self.sem[k], 16)
        self.cnt[k] += 16
        tok = (k, self.cnt[k])
        for r in writes:
            r.w = tok
            r.r = {}
        for r in reads:
            if r.r.get(tok[0], 0) < tok[1]:
                r.r[tok[0]] = tok[1]
        return tok

    def barrier(self):
        for e in self.eng:
            wd = self.waited[e]
            for k, v in self.cnt.items():
                if v > 0 and wd.get(k, 0) < v:
                    self._emit_wait(e, k, v)
                    wd[k] = v
        for r in self.phase_res:
            self.free_d.append(r.dsem)
            r.dsem = None
        self.phase_res = []


def build(stage=9, debug=False):
    _, rec = _build(stage, debug, None)
    nc, _ = _build(stage, debug, rec)
    return nc


def _build(stage, debug, needed):
    nc = bass.Bass("TRN2", target_bir_lowering=False)

    def din(name, shape, dt=F32):
        return nc.dram_tensor(name, list(shape), dt, kind="ExternalInput").ap()

    xp = din("xp", [NT, D])
    padrow = din("padrow", [1, NT])
    mem = din("mem", [256, D])
    w_in = din("w_in", [D, 4104])
    g_mix = din("g_mix", [1, D]); g_memq = din("g_memq", [1, D]); g_memkv = din("g_memkv", [1, D])
    g_ffn = din("g_ffn", [1, D]); g_fin = din("g_fin", [1, D])
    b_forget = din("b_forget", [1, 8])
    lamr_c = din("lamr_c", [96, 768]); lami_c = din("lami_c", [96, 768]); ldt_c = din("ldt_c", [96, 768])
    Br_c = din("Br_c", [96, 768]); Bi_c = din("Bi_c", [96, 768])
    lamr_r = din("lamr_r", [128, 16]); lami_r = din("lami_r", [128, 16]); ldt_r = din("ldt_r", [128, 16])
    Cr_r = din("Cr_r", [128, 512]); Ci_r = din("Ci_r", [128, 512])
    Dm = din("Dm", [96, 192])
    Brow_r = din("Brow_r", [128, 512]); Brow_i = din("Brow_i", [128, 512])
    w_glu = din("w_glu", [512, 2048]); w_fox_o = din("w_fox_o", [512, D]); w_mix = din("w_mix", [D, D])
    w_mem_q = din("w_mem_q", [D, 512]); w_mem_kv = din("w_mem_kv", [D, D]); w_mem_o = din("w_mem_o", [512, D])
    w_ffn_in = din("w_ffn_in", [D, 5632]); w_ffn_out = din("w_ffn_out", [2816, D])
    c_idx = din("c_idx", [128, 512]); c_tau = din("c_tau", [128, 128]); c_reset = din("c_reset", [128, 512])
    c_ident = din("c_ident", [128, 128]); c_triu = din("c_triu", [128, 128]); c_e127 = din("c_e127", [128, 128])
    c_tmask = din("c_tmask", [128, 128]); c_sel = din("c_sel", [128, 128])

    out = nc.dram_tensor("out", [NOWN, D], F32, kind="ExternalOutput").ap()

    def dscr(name, shape, dt=BF16):
        return nc.dram_tensor(name, list(shape), dt, kind="ExternalOutput" if debug else "Internal").ap()

    kT = dscr("kT", [8, 71, NT])
    qT = dscr("qT", [8, 71, NOWN])
    vA = dscr("vA", [8, NT, 128])
    usT = dscr("usT", [512, 16, 512])
    uTo = dscr("uTo", [D, NOWN])

    with contextlib.ExitStack() as es:
        K = Ker(nc, es, needed)
        dbg = {}
        try:

            uid = [0]

            def sb(st, name, shape, dt=F32):
                uid[0] += 1
                return st.enter_context(nc.sbuf_tensor("%s_%d" % (name, uid[0]), list(shape), dt))

            def ps(st, name, shape, dt=F32):
                uid[0] += 1
                return st.enter_context(nc.psum_tensor("%s_%d" % (name, uid[0]), list(shape), dt))

            identf = sb(es, "identf", [128, 128]); identb = sb(es, "identb", [128, 128], BF16)
            triu = sb(es, "triu", [128, 128]); e127 = sb(es, "e127", [128, 128])
            tmaskb = sb(es, "tmaskb", [128, 128], BF16)
            self_ = sb(es, "self_", [128, 128])
            onesb = sb(es, "onesb", [128, 128], BF16)
            halfpi = sb(es, "halfpi", [128, 1])
            epsc = sb(es, "epsc", [128, 1])
            yfm = sb(es, "yfm", [128, 4, NOWN], BF16)
            r_const = Res("const")
            r_yfm = Res("yfm"); r_att = Res("att")
            r_scr = {n: Res(n) for n in ("kT", "qT", "vA", "usT", "uTo")}

            ld = Res("ldc")
            for t_, src in ((identf, c_ident), (triu, c_triu), (e127, c_e127), (self_, c_sel)):
                K.dma("sp", t_[:], src[:, :], writes=[r_const], dres=ld)
            tmaskf = sb(es, "tmaskf", [128, 128])
            K.dma("sp", tmaskf[:], c_tmask[:, :], writes=[r_const], dres=ld)
            K.op("dve", lambda e: e.tensor_copy(out=identb[:], in_=identf[:]), reads=[r_const], writes=[Res()])
            K.op("dve", lambda e: e.tensor_copy(out=tmaskb[:], in_=tmaskf[:]), reads=[r_const], writes=[Res()])
            K.op("dve", lambda e: e.memset(onesb[:], 1.0), writes=[Res()])
            K.op("dve", lambda e: e.memset(halfpi[:], math.pi / 2), writes=[Res()])
            K.op("dve", lambda e: e.memset(epsc[:], EPS), writes=[Res()])
            K.barrier()

            NWST = 4
            wst = [sb(es, "wst%d" % i, [128, 1024]) for i in range(NWST)]
            r_wst = [Res("wst%d" % i) for i in range(NWST)]
            wst_cnt = [0]

            def cast_load(dst_fn, r_dst, src, rows0, ncols, c0, cast_eng="rr"):
                cc = 0
                while cc < ncols:
                    n = min(1024, ncols - cc)
                    si = wst_cnt[0] % NWST
                    wst_cnt[0] += 1
                    K.dma("sp", wst[si][:, 0:n], src[rows0:rows0 + 128, c0 + cc:c0 + cc + n], writes=[r_wst[si]], dres=r_wst[si])
                    dst = dst_fn(cc, n)
                    ce = ("pool", "act", "dve")[wst_cnt[0] % 3] if cast_eng == "rr" else cast_eng
                    if ce == "act":
                        K.op("act", lambda e, dst=dst, si=si, n=n: e.activation(out=dst, in_=wst[si][:, 0:n], func=AF.Copy), reads=[r_wst[si]], writes=[r_dst])
                    else:
                        K.op(ce, lambda e, dst=dst, si=si, n=n: e.tensor_copy(out=dst, in_=wst[si][:, 0:n]), reads=[r_wst[si]], writes=[r_dst])
                    cc += n

            evac_flip = [0]

            def evac(out_ap, in_ap, reads, writes, scale=None, eng=None):
                if eng is None:
                    eng = "act" if evac_flip[0] % 2 == 0 else "dve"
                    evac_flip[0] += 1
                if eng == "act":
                    if scale is None:
                        return K.op("act", lambda e: e.activation(out=out_ap, in_=in_ap, func=AF.Copy), reads=reads, writes=writes)
                    return K.op("act", lambda e: e.activation(out=out_ap, in_=in_ap, func=AF.Copy, scale=scale), reads=reads, writes=writes)
                if scale is None:
                    return K.op("dve", lambda e: e.tensor_copy(out=out_ap, in_=in_ap), reads=reads, writes=writes)
                return K.op("dve", lambda e: e.tensor_scalar(out=out_ap, in0=in_ap, scalar1=scale, scalar2=None, op0=ALU.mult), reads=reads, writes=writes)

            def rmsnorm_tok(st_tag, x_ap, r_x, gain_t, r_gain, out_bf, r_out, junk, r_junk, ss, r_ss, rstd, r_rstd):
                K.op("act", lambda e: e.activation(out=junk, in_=x_ap, func=AF.Square, accum_out=ss), reads=[r_x], writes=[r_junk, r_ss])
                K.op("dve", lambda e: e.tensor_scalar(out=rstd, in0=ss, scalar1=1.0 / D, scalar2=EPS, op0=ALU.mult, op1=ALU.add), reads=[r_ss], writes=[r_rstd])
                K.op("act", lambda e: e.activation(out=rstd, in_=rstd, func=AF.Sqrt), reads=[r_rstd], writes=[r_rstd])
                K.op("dve", lambda e: e.reciprocal(out=rstd, in_=rstd), reads=[r_rstd], writes=[r_rstd])
                K.op("dve", lambda e: e.scalar_tensor_tensor(out=out_bf, in0=x_ap, scalar=rstd, in1=gain_t, op0=ALU.mult, op1=ALU.mult),
                     reads=[r_x, r_rstd, r_gain], writes=[r_out])

            F_all = sb(es, "F_all", [128, 8, NB])
            wb = {}
            r_F = Res("F")
            with contextlib.ExitStack() as p1:
                W1 = sb(p1, "W1", [128, 8, 2056], BF16)
                r_W1 = Res("W1")
                for k in range(8):
                    cast_load(lambda cc, n, k=k: W1[:, k, cc:cc + n], r_W1, w_in, k * 128, 2056, 0)
                gmix = sb(p1, "gmix", [128, D]); r_g = Res("g")
                K.dma("sp", gmix[:], g_mix[0:1, :].partition_broadcast(128), writes=[r_g], dres=r_g)
                bfg = sb(p1, "bfg", [128, 8])
                K.dma("sp", bfg[:], b_forget[0:1, :].partition_broadcast(128), writes=[r_g], dres=r_g)
                onesrow = sb(p1, "onesrow", [3, NOWN], BF16); r_or = Res("or")
                K.op("dve", lambda e: e.memset(onesrow[:], 1.0), writes=[r_or])
                zrow = sb(p1, "zrow", [1, NOWN], BF16)
                K.op("dve", lambda e: e.memset(zrow[:], 0.0), writes=[r_or])
                padb = sb(p1, "padb", [1, 512], BF16); r_pb = Res("pb")
                padf = sb(p1, "padf", [1, 512]); r_pf_ = Res("pf_")
                K.dma("sp", padf[:], padrow[0:1, 0:512], writes=[r_pf_], dres=r_pf_)
                K.op("dve", lambda e: e.tensor_copy(out=padb[:], in_=padf[:]), reads=[r_pf_], writes=[r_pb])
                st_aug = Res("st_aug")
                for h in range(8):
                    for c4 in range(4):
                        K.dma("sp", kT[h, 67:70, c4 * NOWN:(c4 + 1) * NOWN], onesrow[:], reads=[r_or], dres=st_aug)
                    K.dma("sp", kT[h, 70:71, 0:512], padb[:], reads=[r_pb], dres=st_aug)
                    for c0_, c1_ in ((512, 2560), (2560, 4608), (4608, 6656), (6656, 8192)):
                        K.dma("sp", kT[h, 70:71, c0_:c1_], zrow[:, 0:c1_ - c0_], reads=[r_or], dres=st_aug)
                    K.dma("sp", qT[h, 64:67, :], onesrow[:], reads=[r_or], dres=st_aug)
                    K.dma("sp", qT[h, 70:71, :], onesrow[0:1, :], reads=[r_or], dres=st_aug)

                NXB = 6
                xt = [sb(p1, "xt%d" % i, [128, D]) for i in range(NXB)]
                r_xt = [Res("xt%d" % i) for i in range(NXB)]
                junk = sb(p1, "junk", [128, D], BF16); r_junk = Res("junk")
                ss = [sb(p1, "ss%d" % i, [128, 1]) for i in range(4)]; r_ss = [Res() for _ in range(4)]
                rstd = [sb(p1, "rstd%d" % i, [128, 1]) for i in range(4)]; r_rstd = [Res() for _ in range(4)]
                ub = [sb(p1, "ub%d" % i, [128, D], BF16) for i in range(4)]; r_ub = [Res() for _ in range(4)]
                uT = [sb(p1, "uT%d" % i, [128, 8, 512], BF16) for i in range(2)]; r_uT = [Res() for _ in range(2)]
                kst = [sb(p1, "kst%d" % i, [128, 512], BF16) for i in range(4)]; r_kst = [Res() for _ in range(4)]
                vst = [sb(p1, "vst%d" % i, [128, 8, 128], BF16) for i in range(3)]; r_vst = [Res() for _ in range(3)]
                qst = [sb(p1, "qst%d" % i, [128, 128], BF16) for i in range(2)]; r_qst = [Res() for _ in range(2)]
                usd = [sb(p1, "usd%d" % i, [128, 16, 128], BF16) for i in range(4)]; r_usd = [Res() for _ in range(4)]
                ptr = [ps(p1, "ptr%d" % i, [128, 1024], BF16) for i in range(2)]; r_ptr = [Res() for _ in range(2)]
                pk = [ps(p1, "pk%d" % i, [128, 512]) for i in range(3)]; r_pk = [Res() for _ in range(3)]
                pv = [ps(p1, "pv%d" % i, [128, 512]) for i in range(2)]; r_pv = [Res() for _ in range(2)]
                pf = ps(p1, "pf", [128, 8]); r_pf = Res()
                for i in range(3):
                    v_ = vst[i]
                    K.op("pool", lambda e, v_=v_: e.memset(v_[:], 0.0), writes=[r_vst[i]])
                    K.op("pool", lambda e, v_=v_: e.memset(v_[:, 0:8:2, 64:65], 1.0), writes=[r_vst[i]])
                    K.op("pool", lambda e, v_=v_: e.memset(v_[:, 1:8:2, 0:1], 1.0), writes=[r_vst[i]])

                def load_x(gb):
                    i = gb % NXB
                    K.dma("sp", xt[i][:], xp[gb * 128:(gb + 1) * 128, :], writes=[r_xt[i]], dres=r_xt[i])

                for gb in range(NXB):
                    load_x(gb)
                kcount = [0]
                vcount = [0]

                def stage_N(s):
                    for blk in range(4):
                        gb = 4 * s + blk
                        xi = gb % NXB
                        bi = gb % 4
                        rmsnorm_tok("p1", xt[xi][:], r_xt[xi], gmix[:], r_g, ub[bi][:], r_ub[bi], junk[:], r_junk,
                                    ss[bi][:], r_ss[bi], rstd[bi][:], r_rstd[bi])
                        if gb + NXB < NB:
                            load_x(gb + NXB)

                def stage_T(s):
                    ui = s % 2
                    for blk in range(4):
                        gb = 4 * s + blk
                        bi = gb % 4
                        pi_ = gb % 2
                        pt = ptr[pi_]
                        for k in range(8):
                            K.op("pe", lambda e, k=k, pt=pt, bi=bi: e.transpose(out=pt[:, k * 128:(k + 1) * 128], in_=ub[bi][:, k * 128:(k + 1) * 128], identity=identb[:]),
                                 reads=[r_ub[bi], r_const], writes=[r_ptr[pi_]], signal=(k == 7))
                        evac(uT[ui][:, :, blk * 128:(blk + 1) * 128], pt[:].rearrange("p (k t) -> p k t", k=8), [r_ptr[pi_]], [r_uT[ui]])

                def stage_M(s):
                    ui = s % 2
                    for blk in range(4):
                        gb = 4 * s + blk
                        pvi = vcount[0] % 2
                        for k in range(8):
                            K.op("pe", lambda e, k=k, pvi=pvi, blk=blk: e.matmul(pv[pvi][:], lhsT=uT[ui][:, k, blk * 128:(blk + 1) * 128], rhs=W1[:, k, 1536:2048], start=(k == 0), stop=(k == 7)),
                                 reads=[r_uT[ui], r_W1], writes=[r_pv[pvi]], signal=(k == 7))
                        vi = vcount[0] % 3
                        vcount[0] += 1
                        pvv = pv[pvi][:].rearrange("p (hp e d) -> p hp e d", hp=4, e=2)
                        vsv = vst[vi][:].rearrange("p (hp e) c -> p hp e c", e=2)
                        evac(vsv[:, :, 0, 0:64], pvv[:, :, 0, :], [r_pv[pvi]], [r_vst[vi]], eng="act")
                        evac(vsv[:, :, 1, 64:128], pvv[:, :, 1, :], [r_pv[pvi]], [r_vst[vi]], eng="dve")
                        K.dma("sp", vA[:, gb * 128:(gb + 1) * 128, :].rearrange("h t c -> t h c"), vst[vi][:], reads=[r_vst[vi]], dres=r_vst[vi])
                        for k in range(8):
                            K.op("pe", lambda e, k=k, blk=blk: e.matmul(pf[:], lhsT=uT[ui][:, k, blk * 128:(blk + 1) * 128], rhs=W1[:, k, 2048:2056], start=(k == 0), stop=(k == 7)),
                                 reads=[r_uT[ui], r_W1], writes=[r_pf], signal=(k == 7))
                        K.op("dve", lambda e, gb=gb: e.tensor_tensor(out=F_all[:, :, gb], in0=pf[:], in1=bfg[:], op=ALU.add), reads=[r_pf, r_g], writes=[r_F])
                    for grp, c0, dst in (("k", 1024, "kT"), ("u", 0, "usT")):
                        for t in range(4):
                            pi = kcount[0] % 3
                            si = kcount[0] % 4
                            kcount[0] += 1
                            for k in range(8):
                                K.op("pe", lambda e, k=k, pi=pi, c0=c0, t=t: e.matmul(pk[pi][:], lhsT=W1[:, k, c0 + t * 128:c0 + (t + 1) * 128], rhs=uT[ui][:, k, :], start=(k == 0), stop=(k == 7)),
                                     reads=[r_uT[ui], r_W1], writes=[r_pk[pi]], signal=(k == 7))
                            if grp == "k":
                                evac(kst[si][:], pk[pi][:], [r_pk[pi]], [r_kst[si]])
                                for e_ in range(2):
                                    K.dma("sp", kT[2 * t + e_, 0:64, s * 512:(s + 1) * 512], kst[si][64 * e_:64 * e_ + 64, :], reads=[r_kst[si]], dres=r_kst[si])
                            else:
                                s4 = s % 4
                                evac(usd[t][:, :, s4 * 32:(s4 + 1) * 32], pk[pi][:].rearrange("p (c s) -> p s c", s=16), [r_pk[pi]], [r_usd[t]])
                                if s4 == 3:
                                    g4_ = s // 4
                                    K.dma("sp", usT[t * 128:(t + 1) * 128, :, g4_ * 128:(g4_ + 1) * 128], usd[t][:], reads=[r_usd[t]], dres=r_usd[t])
                    for t in range(4):
                        pi = kcount[0] % 3
                        qi = kcount[0] % 2
                        kcount[0] += 1
                        for k in range(8):
                            K.op("pe", lambda e, k=k, pi=pi, t=t: e.matmul(pk[pi][:, 0:128], lhsT=W1[:, k, 512 + t * 128:512 + (t + 1) * 128], rhs=uT[ui][:, k, 384:512], start=(k == 0), stop=(k == 7)),
                                 reads=[r_uT[ui], r_W1], writes=[r_pk[pi]], signal=(k == 7))
                        evac(qst[qi][:], pk[pi][:, 0:128], [r_pk[pi]], [r_qst[qi]], scale=0.125)
                        for e_ in range(2):
                            K.dma("sp", qT[2 * t + e_, 0:64, s * 128:(s + 1) * 128], qst[qi][64 * e_:64 * e_ + 64, :], reads=[r_qst[qi]], dres=r_qst[qi])
                    K.dma("sp", uTo[:, s * 128:(s + 1) * 128].rearrange("(k p) t -> p k t", p=128), uT[ui][:, :, 384:512], reads=[r_uT[ui]], dres=r_uT[ui])

                stage_N(0)
                stage_T(0)
                stage_N(1)
                for s in range(16):
                    if s + 1 < 16:
                        stage_T(s + 1)
                    if s + 2 < 16:
                        stage_N(s + 2)
                    stage_M(s)
                K.barrier()

            if stage <= 1:
                raise _Stop()
            with contextlib.ExitStack() as p2:
                for _once in (0,):
                    L = sb(p2, "L", [128, 512]); r_L = Res()
                    Fv = F_all[:].rearrange("p h b -> p (h b)")
                    K.op("act", lambda e: e.activation(out=L[:], in_=Fv, func=AF.Exp, scale=-1.0), reads=[r_F], writes=[r_L])
                    one1 = sb(p2, "one1", [128, 1]); r_one1 = Res()
                    K.op("dve", lambda e: e.memset(one1[:], 1.0), writes=[r_one1])
                    K.op("act", lambda e: e.activation(out=L[:], in_=L[:], func=AF.Ln, bias=one1[:], scale=1.0), reads=[r_L, r_one1], writes=[r_L])
                    pc = ps(p2, "pc", [128, 512]); r_pc = Res()
                    pc2 = ps(p2, "pc2", [128, 512]); r_pc2 = Res()
                    K.op("pe", lambda e: e.matmul(pc[:], lhsT=triu[:], rhs=L[:], start=True, stop=True), reads=[r_L, r_const], writes=[r_pc])
                    incl = sb(p2, "incl", [128, 512]); r_incl = Res()
                    evac(incl[:], pc[:], [r_pc], [r_incl], eng="dve")
                    K.op("pe", lambda e: e.matmul(pc2[:], lhsT=e127[:], rhs=incl[:], start=True, stop=True), reads=[r_incl, r_const], writes=[r_pc2])
                    tot = sb(p2, "tot", [128, 512]); r_tot = Res()
                    evac(tot[:], pc2[:], [r_pc2], [r_tot], eng="dve")
                    rmul = sb(p2, "rmul", [128, 8, NB]); r_rmul = Res()
                    K.op("dve", lambda e: e.memset(rmul[:], 1.0), writes=[r_rmul])
                    K.op("dve", lambda e: e.memset(rmul[:, :, 0:1], 0.0), writes=[r_rmul])
                    offs = sb(p2, "offs", [128, 512]); r_offs = Res()
                    K.op("dve", lambda e: e.tensor_tensor_scan(out=offs[:], data0=rmul[:].rearrange("p h b -> p (h b)"), data1=tot[:], initial=0.0, op0=ALU.mult, op1=ALU.add),
                         reads=[r_rmul, r_tot], writes=[r_offs])
                    cl = sb(p2, "cl", [128, 512]); r_cl = Res()
                    K.op("dve", lambda e: e.tensor_tensor(out=cl[:], in0=offs[:], in1=tot[:], op=ALU.subtract), reads=[r_offs, r_tot], writes=[r_cl])
                    K.op("dve", lambda e: e.tensor_tensor(out=cl[:], in0=cl[:], in1=incl[:], op=ALU.add), reads=[r_cl, r_incl], writes=[r_cl])
                    if os.environ.get("DBG_SUB") == "1":
                        break
                    spl = sb(p2, "spl", [128, NB, 8, 3], BF16); r_spl = Res()
                    res1 = sb(p2, "res1", [128, 512]); r_res1 = Res()
                    clv = cl[:].rearrange("p (h b) -> p b h", h=8)
                    r1v = res1[:].rearrange("p (h b) -> p b h", h=8)
                    K.op("dve", lambda e: e.tensor_copy(out=spl[:, :, :, 0], in_=clv), reads=[r_cl], writes=[r_spl])
                    K.op("dve", lambda e: e.tensor_tensor(out=r1v, in0=clv, in1=spl[:, :, :, 0], op=ALU.subtract), reads=[r_cl, r_spl], writes=[r_res1])
                    K.op("dve", lambda e: e.tensor_copy(out=spl[:, :, :, 1], in_=r1v), reads=[r_res1], writes=[r_spl])
                    K.op("dve", lambda e: e.tensor_tensor(out=r1v, in0=r1v, in1=spl[:, :, :, 1], op=ALU.subtract), reads=[r_res1, r_spl], writes=[r_res1])
                    K.op("dve", lambda e: e.tensor_copy(out=spl[:, :, :, 2], in_=r1v), reads=[r_res1], writes=[r_spl])
                    if os.environ.get("DBG_SUB") == "2":
                        break
                    augT = sb(p2, "augT", [24, NT], BF16); r_augT = Res()
                    qaug = sb(p2, "qaug", [24, NOWN], BF16); r_qaug = Res()
                    pa = [ps(p2, "pa%d" % i, [128, 512]) for i in range(2)]; r_pa = [Res() for _ in range(2)]
                    for g4 in range(16):
                        pi = g4 % 2
                        for bb in range(4):
                            blk = 4 * g4 + bb
                            K.op("pe", lambda e, blk=blk, bb=bb, pi=pi: e.matmul(pa[pi][0:24, bb * 128:(bb + 1) * 128], lhsT=spl[:, blk, :, :].rearrange("p h s -> p (h s)"), rhs=identb[:], start=True, stop=True),
                                 reads=[r_spl, r_const], writes=[r_pa[pi]], signal=(bb == 3))
                        if os.environ.get("DBG_VAR") != "B":
                            evac(augT[:, g4 * 512:(g4 + 1) * 512], pa[pi][0:24, :], [r_pa[pi]], [r_augT], eng={"C": "act", "D": "dve"}.get(os.environ.get("DBG_VAR"), None))
                        if os.environ.get("DBG_VAR") == "A":
                            continue
                        K.op("dve", lambda e, g4=g4, pi=pi: e.tensor_scalar(out=qaug[:, g4 * 128:(g4 + 1) * 128], in0=pa[pi][0:24, 384:512], scalar1=-1.0, scalar2=None, op0=ALU.mult),
                             reads=[r_pa[pi]], writes=[r_qaug])
                    if os.environ.get("DBG_SUB") == "3":
                        break
                    for h in range(8):
                        K.dma("sp", kT[h, 64:67, :], augT[3 * h:3 * h + 3, :], reads=[r_augT], dres=r_augT)
                        K.dma("sp", qT[h, 67:70, :], qaug[3 * h:3 * h + 3, :], reads=[r_qaug], dres=r_qaug)
                    K.barrier()

            if stage <= 2:
                raise _Stop()
            with contextlib.ExitStack() as p3:
                T16 = 16
                r_pc_ = Res("parc"); r_pcc = Res("parcc")
                def lam_bar(st, P, n, lr, li, ldt, tag, pw, st_tmp=None, r_in=None):
                    lbr = sb(st, tag + "lbr", [P, n]); lbi = sb(st, tag + "lbi", [P, n])
                    al = sb(st, tag + "al", [P, n]); tf = sb(st, tag + "tf", [P, n])
                    st2 = st if st_tmp is None else st_tmp
                    dt_ = sb(st2, tag + "dt", [P, n]); th = sb(st2, tag + "th", [P, n])
                    ti = sb(st2, tag + "ti", [P, n], I32); fa = sb(st2, tag + "fa", [P, n])
                    mg = sb(st2, tag + "mg", [P, n]); sn = sb(st2, tag + "sn", [P, n]); cs = sb(st2, tag + "cs", [P, n])
                    rr = Res(tag)
                    r_in = r_pc_ if r_in is None else r_in
                    K.op("act", lambda e: e.activation(out=dt_[:], in_=ldt[:], func=AF.Exp), reads=[r_in], writes=[rr])
                    K.op("dve", lambda e: e.tensor_tensor(out=al[:], in0=lr[:], in1=dt_[:], op=ALU.mult), reads=[rr, r_in], writes=[rr])
                    K.op("dve", lambda e: e.tensor_tensor(out=th[:], in0=li[:], in1=dt_[:], op=ALU.mult), reads=[rr, r_in], writes=[rr])
                    K.op("act", lambda e: e.activation(out=mg[:], in_=al[:], func=AF.Exp, scale=float(pw)), reads=[rr], writes=[rr])
                    K.op("dve", lambda e: e.tensor_scalar(out=tf[:], in0=th[:], scalar1=float(pw) / TWO_PI, scalar2=None, op0=ALU.mult), reads=[rr], writes=[rr])
                    K.op("dve", lambda e: e.tensor_copy(out=ti[:], in_=tf[:]), reads=[rr], writes=[rr])
                    K.op("dve", lambda e: e.tensor_copy(out=fa[:], in_=ti[:]), reads=[rr], writes=[rr])
                    K.op("dve", lambda e: e.tensor_tensor(out=tf[:], in0=tf[:], in1=fa[:], op=ALU.subtract), reads=[rr], writes=[rr])
                    K.op("act", lambda e: e.activation(out=sn[:], in_=tf[:], func=AF.Sin, scale=TWO_PI), reads=[rr], writes=[rr])
                    K.op("dve", lambda e: e.tensor_scalar(out=fa[:], in0=tf[:], scalar1=-1.0, scalar2=None, op0=ALU.mult), reads=[rr], writes=[rr])
                    K.op("dve", lambda e: e.tensor_tensor(out=fa[:], in0=fa[:], in1=tf[:], op=ALU.max), reads=[rr], writes=[rr])
                    K.op("act", lambda e: e.activation(out=cs[:], in_=fa[:], func=AF.Sin, scale=-TWO_PI, bias=halfpi[0:P, :]), reads=[rr], writes=[rr])
                    K.op("dve", lambda e: e.tensor_tensor(out=lbr[:], in0=mg[:], in1=cs[:], op=ALU.mult), reads=[rr], writes=[rr])
                    K.op("dve", lambda e: e.tensor_tensor(out=lbi[:], in0=mg[:], in1=sn[:], op=ALU.mult), reads=[rr], writes=[rr])
                    return lbr, lbi, al, tf, rr

                Wa = sb(p3, "Wa", [96, T16, 2, 768], BF16)
                lrr = sb(p3, "lrr", [128, 16]); lir = sb(p3, "lir", [128, 16]); dtr = sb(p3, "dtr", [128, 16])
                Crb = sb(p3, "Crb", [128, 512], BF16); Cib = sb(p3, "Cib", [128, 512], BF16)
                Dmb = sb(p3, "Dmb", [96, 192], BF16)
                for t_, s_ in ((lrr, lamr_r), (lir, lami_r), (dtr, ldt_r)):
                    K.dma("sp", t_[:], s_[:, :], writes=[r_pc_], dres=r_pc_)
                Vtab = sb(p3, "Vtab", [128, 16, T16, 2, 32], BF16)
                Kd = sb(p3, "Kd", [96, 6, T16, 32], BF16)
                K.op("pool", lambda e: e.memset(Kd[:].rearrange("p s d c -> p (s d c)"), 0.0), writes=[Res()])
                cidx = sb(p3, "cidx", [128, 512])
                K.dma("sp", cidx[:], c_idx[:, :], writes=[r_pc_], dres=r_pc_)
                r_tab = Res("tab")
                lbr_r, lbi_r, alr, f1r, r_lr1 = lam_bar(p3, 128, 16, lrr, lir, dtr, "r1", 1)
                _, _, _, f16r, r_lr16 = lam_bar(p3, 128, 16, lrr, lir, dtr, "r16", 16)
                rho16 = sb(p3, "rho16", [128, 16])
                K.op("act", lambda e: e.activation(out=rho16[:], in_=alr[:], func=AF.Exp, scale=16.0), reads=[r_lr1], writes=[r_tab])
                tmp = {e_: (sb(p3, "ta_" + e_, [128, 512]), sb(p3, "tb_" + e_, [128, 512]), Res()) for e_ in ("dve", "pool")}
                ptc = contextlib.ExitStack()
                Bbr = sb(ptc, "Bbr", [128, 512], BF16); Bbi = sb(ptc, "Bbi", [128, 512], BF16)
                Lr = sb(ptc, "Lr", [128, 16, T16 + 1]); Li = sb(ptc, "Li", [128, 16, T16 + 1])
                Brf = sb(ptc, "Brf", [128, 512]); Bif = sb(ptc, "Bif", [128, 512]); Cif = sb(ptc, "Cif", [128, 512])
                Crf = sb(ptc, "Crf", [128, 512]); Dmf = sb(ptc, "Dmf", [96, 192]); r_cd = Res("cd")
                cfr = sb(ptc, "cfr", [128, 16]); cfi = sb(ptc, "cfi", [128, 16]); s1 = sb(ptc, "s1", [128, 16]); s2 = sb(ptc, "s2", [128, 16])
                dn = sb(ptc, "dn", [128, 16]); nr2 = sb(ptc, "nr2", [128, 16]); r_cf = Res("cf")
                K.dma("sp", Cif[:], Ci_r[:, :], writes=[r_cd], dres=r_cd)
                K.dma("sp", Brf[:], Brow_r[:, :], writes=[r_cd], dres=r_cd)
                K.dma("sp", Bif[:], Brow_i[:, :], writes=[r_cd], dres=r_cd)
                K.dma("sp", Crf[:], Cr_r[:, :], writes=[r_cd], dres=r_cd)
                K.dma("sp", Dmf[:], Dm[:, :], writes=[r_cd], dres=r_cd)
                r_cdb = Res("cdb")
                K.op("dve", lambda e: e.tensor_copy(out=Crb[:], in_=Crf[:]), reads=[r_cd], writes=[r_cdb])
                K.op("dve", lambda e: e.tensor_copy(out=Dmb[:], in_=Dmf[:]), reads=[r_cd], writes=[r_cdb])
                ptb = contextlib.ExitStack()
                lrc = sb(ptb, "lrc", [96, 768]); lic = sb(ptb, "lic", [96, 768]); dtc = sb(ptb, "dtc", [96, 768])
                Brc = sb(ptb, "Brc", [96, 768]); Bic = sb(ptb, "Bic", [96, 768])
                for t_, s_ in ((lrc, lamr_c), (lic, lami_c), (dtc, ldt_c), (Brc, Br_c), (Bic, Bi_c)):
                    K.dma("sp", t_[:], s_[:, :], writes=[r_pcc], dres=r_pcc)
                K.op("dve", lambda e: e.tensor_scalar(out=Cib[:], in0=Cif[:], scalar1=-1.0, scalar2=None, op0=ALU.mult), reads=[r_cd], writes=[r_tab])

                with contextlib.ExitStack() as ptb2:
                    lbr, lbi, alc, _, r_lc = lam_bar(ptb, 96, 768, lrc, lic, dtc, "c", 1, st_tmp=ptb2, r_in=r_pcc)
                    K.barrier()
                Pr = sb(ptb, "Pr", [96, 768]); Pi = sb(ptb, "Pi", [96, 768]); t1 = sb(ptb, "t1", [96, 768]); t2 = sb(ptb, "t2", [96, 768])
                den = sb(ptb, "den", [96, 768]); nr = sb(ptb, "nr", [96, 768])
                r_P = Res("P")
                V = "dve"
                K.op(V, lambda e: e.tensor_scalar(out=nr[:], in0=lbr[:], scalar1=-1.0, scalar2=None, op0=ALU.add), reads=[r_lc], writes=[r_P])
                K.op(V, lambda e: e.tensor_tensor(out=den[:], in0=lrc[:], in1=lrc[:], op=ALU.mult), reads=[r_pcc], writes=[r_P])
                K.op(V, lambda e: e.tensor_tensor(out=t1[:], in0=lic[:], in1=lic[:], op=ALU.mult), reads=[r_pcc, r_P], writes=[r_P])
                K.op(V, lambda e: e.tensor_tensor(out=den[:], in0=den[:], in1=t1[:], op=ALU.add), reads=[r_P], writes=[r_P])
                K.op(V, lambda e: e.reciprocal(out=den[:], in_=den[:]), reads=[r_P], writes=[r_P])
                K.op(V, lambda e: e.tensor_tensor(out=t1[:], in0=nr[:], in1=lrc[:], op=ALU.mult), reads=[r_P, r_pcc], writes=[r_P])
                K.op(V, lambda e: e.tensor_tensor(out=t2[:], in0=lbi[:], in1=lic[:], op=ALU.mult), reads=[r_P, r_pcc, r_lc], writes=[r_P])
                K.op(V, lambda e: e.tensor_tensor(out=t1[:], in0=t1[:], in1=t2[:], op=ALU.add), reads=[r_P], writes=[r_P])
                K.op(V, lambda e: e.tensor_tensor(out=Pr[:], in0=t1[:], in1=den[:], op=ALU.mult), reads=[r_P], writes=[r_P])
                K.op(V, lambda e: e.tensor_tensor(out=t1[:], in0=lbi[:], in1=lrc[:], op=ALU.mult), reads=[r_P, r_pcc, r_lc], writes=[r_P])
                K.op(V, lambda e: e.tensor_tensor(out=t2[:], in0=nr[:], in1=lic[:], op=ALU.mult), reads=[r_P, r_pcc], writes=[r_P])
                K.op(V, lambda e: e.tensor_tensor(out=t1[:], in0=t1[:], in1=t2[:], op=ALU.subtract), reads=[r_P], writes=[r_P])
                K.op(V, lambda e: e.tensor_tensor(out=Pi[:], in0=t1[:], in1=den[:], op=ALU.mult), reads=[r_P], writes=[r_P])
                Pn = sb(ptb, "Pn", [96, 768])
                for kk in range(T16):
                    tau = T16 - 1 - kk
                    K.op(V, lambda e: e.tensor_tensor(out=t1[:], in0=Pr[:], in1=Brc[:], op=ALU.mult), reads=[r_P, r_pcc], writes=[r_P])
                    K.op(V, lambda e: e.tensor_tensor(out=t2[:], in0=Pi[:], in1=Bic[:], op=ALU.mult), reads=[r_P, r_pcc], writes=[r_P])
                    K.op(V, lambda e, tau=tau: e.tensor_tensor(out=Wa[:, tau, 0, :], in0=t1[:], in1=t2[:], op=ALU.subtract), reads=[r_P], writes=[r_tab])
                    K.op(V, lambda e: e.tensor_tensor(out=t1[:], in0=Pr[:], in1=Bic[:], op=ALU.mult), reads=[r_P, r_pcc], writes=[r_P])
                    K.op(V, lambda e: e.tensor_tensor(out=t2[:], in0=Pi[:], in1=Brc[:], op=ALU.mult), reads=[r_P, r_pcc], writes=[r_P])
                    K.op(V, lambda e, tau=tau: e.tensor_tensor(out=Wa[:, tau, 1, :], in0=t1[:], in1=t2[:], op=ALU.add), reads=[r_P], writes=[r_tab])
                    if kk < T16 - 1:
                        K.op(V, lambda e: e.tensor_tensor(out=t1[:], in0=Pr[:], in1=lbr[:], op=ALU.mult), reads=[r_P, r_lc], writes=[r_P])
                        K.op(V, lambda e: e.tensor_tensor(out=t2[:], in0=Pi[:], in1=lbi[:], op=ALU.mult), reads=[r_P, r_lc], writes=[r_P])
                        K.op(V, lambda e: e.tensor_tensor(out=Pn[:], in0=t1[:], in1=t2[:], op=ALU.subtract), reads=[r_P], writes=[r_P])
                        K.op(V, lambda e: e.tensor_tensor(out=t1[:], in0=Pr[:], in1=lbi[:], op=ALU.mult), reads=[r_P, r_lc], writes=[r_P])
                        K.op(V, lambda e: e.tensor_tensor(out=t2[:], in0=Pi[:], in1=lbr[:], op=ALU.mult), reads=[r_P, r_lc], writes=[r_P])
                        K.op(V, lambda e: e.tensor_tensor(out=Pi[:], in0=t1[:], in1=t2[:], op=ALU.add), reads=[r_P], writes=[r_P])
                        K.op(V, lambda e: e.tensor_copy(out=Pr[:], in_=Pn[:]), reads=[r_P], writes=[r_P])
                K.barrier()
                ptb.close()

                def cmul(eng, outr, outi, ar, ai, br, bi, conj_b, reads, writes, shp):
                    ta, tb, r_tt = tmp[eng]
                    o1 = ALU.subtract if not conj_b else ALU.add
                    o2 = ALU.add if not conj_b else ALU.subtract
                    tav, tbv = shp(ta), shp(tb)
                    K.op(eng, lambda e: e.tensor_tensor(out=tav, in0=ar, in1=br, op=ALU.mult), reads=reads, writes=[r_tt])
                    K.op(eng, lambda e: e.tensor_tensor(out=tbv, in0=ai, in1=bi, op=ALU.mult), reads=reads + [r_tt], writes=[r_tt])
                    K.op(eng, lambda e: e.tensor_tensor(out=outr, in0=tav, in1=tbv, op=o1), reads=[r_tt], writes=writes)
                    K.op(eng, lambda e: e.tensor_tensor(out=tav, in0=ai, in1=br, op=ALU.mult), reads=reads + [r_tt], writes=[r_tt])
                    K.op(eng, lambda e: e.tensor_tensor(out=tbv, in0=ar, in1=bi, op=ALU.mult), reads=reads + [r_tt], writes=[r_tt])
                    K.op(eng, lambda e: e.tensor_tensor(out=outi, in0=tav, in1=tbv, op=o2), reads=[r_tt], writes=writes)

                flat = lambda n: (lambda t: t[:, 0:n])
                pq32 = lambda t: t[:].rearrange("p (q c) -> p q c", c=32)

                V = "dve"
                K.op(V, lambda e: e.tensor_scalar(out=nr2[:], in0=lbr_r[:], scalar1=-1.0, scalar2=None, op0=ALU.add), reads=[r_lr1], writes=[r_cf])
                K.op(V, lambda e: e.tensor_tensor(out=dn[:], in0=lrr[:], in1=lrr[:], op=ALU.mult), reads=[r_pc_], writes=[r_cf])
                K.op(V, lambda e: e.tensor_tensor(out=s1[:], in0=lir[:], in1=lir[:], op=ALU.mult), reads=[r_pc_, r_cf], writes=[r_cf])
                K.op(V, lambda e: e.tensor_tensor(out=dn[:], in0=dn[:], in1=s1[:], op=ALU.add), reads=[r_cf], writes=[r_cf])
                K.op(V, lambda e: e.reciprocal(out=dn[:], in_=dn[:]), reads=[r_cf], writes=[r_cf])
                K.op(V, lambda e: e.tensor_tensor(out=s1[:], in0=nr2[:], in1=lrr[:], op=ALU.mult), reads=[r_cf, r_pc_], writes=[r_cf])
                K.op(V, lambda e: e.tensor_tensor(out=s2[:], in0=lbi_r[:], in1=lir[:], op=ALU.mult), reads=[r_cf, r_pc_, r_lr1], writes=[r_cf])
                K.op(V, lambda e: e.tensor_tensor(out=s1[:], in0=s1[:], in1=s2[:], op=ALU.add), reads=[r_cf], writes=[r_cf])
                K.op(V, lambda e: e.tensor_tensor(out=cfr[:], in0=s1[:], in1=dn[:], op=ALU.mult), reads=[r_cf], writes=[r_cf])
                K.op(V, lambda e: e.tensor_tensor(out=s1[:], in0=lbi_r[:], in1=lrr[:], op=ALU.mult), reads=[r_cf, r_pc_, r_lr1], writes=[r_cf])
                K.op(V, lambda e: e.tensor_tensor(out=s2[:], in0=nr2[:], in1=lir[:], op=ALU.mult), reads=[r_cf, r_pc_], writes=[r_cf])
                K.op(V, lambda e: e.tensor_tensor(out=s1[:], in0=s1[:], in1=s2[:], op=ALU.subtract), reads=[r_cf], writes=[r_cf])
                K.op(V, lambda e: e.tensor_tensor(out=cfi[:], in0=s1[:], in1=dn[:], op=ALU.mult), reads=[r_cf], writes=[r_cf])
                r_Bb = Res("Bb")
                bc = lambda t: t[:].unsqueeze(2).to_broadcast([128, 16, 32])
                cmul("dve", pq32(Bbr), pq32(Bbi), pq32(Brf), pq32(Bif), bc(cfr), bc(cfi), False, [r_cd, r_cf], [r_Bb], pq32)
                r_L = Res("L")
                K.op("pool", lambda e: e.memset(Lr[:, :, 0:1], 1.0), writes=[r_L])
                K.op("pool", lambda e: e.memset(Li[:, :, 0:1], 0.0), writes=[r_L])
                tp_ = tmp["pool"]
                for k in range(T16):
                    ta, tb, r_tt = tp_
                    K.op("pool", lambda e, k=k: e.tensor_tensor(out=ta[:, 0:16], in0=Lr[:, :, k], in1=lbr_r[:], op=ALU.mult), reads=[r_L, r_lr1], writes=[r_tt])
                    K.op("pool", lambda e, k=k: e.tensor_tensor(out=tb[:, 0:16], in0=Li[:, :, k], in1=lbi_r[:], op=ALU.mult), reads=[r_L, r_lr1, r_tt], writes=[r_tt])
                    K.op("pool", lambda e, k=k: e.tensor_tensor(out=Lr[:, :, k + 1], in0=ta[:, 0:16], in1=tb[:, 0:16], op=ALU.subtract), reads=[r_tt], writes=[r_L])
                    K.op("pool", lambda e, k=k: e.tensor_tensor(out=ta[:, 0:16], in0=Lr[:, :, k], in1=lbi_r[:], op=ALU.mult), reads=[r_L, r_lr1, r_tt], writes=[r_tt])
                    K.op("pool", lambda e, k=k: e.tensor_tensor(out=tb[:, 0:16], in0=Li[:, :, k], in1=lbr_r[:], op=ALU.mult), reads=[r_L, r_lr1, r_tt], writes=[r_tt])
                    K.op("pool", lambda e, k=k: e.tensor_tensor(out=Li[:, :, k + 1], in0=ta[:, 0:16], in1=tb[:, 0:16], op=ALU.add), reads=[r_tt], writes=[r_L])
                r_V = Res("V")
                for tau in range(T16):
                    eng = "dve" if tau % 2 == 0 else "pool"
                    ta, tb, r_tt = tmp[eng]
                    lr_b = Lr[:, :, tau + 1].unsqueeze(2).to_broadcast([128, 16, 32])
                    li_b = Li[:, :, tau + 1].unsqueeze(2).to_broadcast([128, 16, 32])
                    K.op(eng, lambda e, ta=ta, lr_b=lr_b: e.tensor_tensor(out=pq32(ta), in0=pq32(Crf), in1=lr_b, op=ALU.mult), reads=[r_cd, r_L], writes=[r_tt])
                    K.op(eng, lambda e, tb=tb, li_b=li_b: e.tensor_tensor(out=pq32(tb), in0=pq32(Cif), in1=li_b, op=ALU.mult), reads=[r_cd, r_L, r_tt], writes=[r_tt])
                    K.op(eng, lambda e, ta=ta, tb=tb, tau=tau: e.tensor_tensor(out=Vtab[:, :, tau, 0, :], in0=pq32(ta), in1=pq32(tb), op=ALU.subtract), reads=[r_tt], writes=[r_V])
                    K.op(eng, lambda e, ta=ta, li_b=li_b: e.tensor_tensor(out=pq32(ta), in0=pq32(Crf), in1=li_b, op=ALU.mult), reads=[r_cd, r_L, r_tt], writes=[r_tt])
                    K.op(eng, lambda e, tb=tb, lr_b=lr_b: e.tensor_tensor(out=pq32(tb), in0=pq32(Cif), in1=lr_b, op=ALU.mult), reads=[r_cd, r_L, r_tt], writes=[r_tt])
                    K.op(eng, lambda e, ta=ta, tb=tb: e.tensor_tensor(out=pq32(ta), in0=pq32(ta), in1=pq32(tb), op=ALU.add), reads=[r_tt], writes=[r_tt])
                    K.op(eng, lambda e, ta=ta, tau=tau: e.tensor_scalar(out=Vtab[:, :, tau, 1, :], in0=pq32(ta), scalar1=-1.0, scalar2=None, op0=ALU.mult), reads=[r_tt], writes=[r_V])
                r_Kd = Res("Kd")
                with contextlib.ExitStack() as pk_:
                    pskd = [ps(pk_, "pskd%d" % i_, [96, 512]) for i_ in range(2)]; r_pskd = [Res() for _ in range(2)]
                    for sl in range(6):
                        pi_ = sl % 2
                        npair = min(3, 16 - 3 * sl)
                        for kk in range(npair):
                            q = 3 * sl + kk
                            for dl in range(T16):
                                rr_ = Crb[:, q * 32:(q + 1) * 32] if dl == 0 else Vtab[:, q, dl - 1, 0, :]
                                ri_ = Cib[:, q * 32:(q + 1) * 32] if dl == 0 else Vtab[:, q, dl - 1, 1, :]
                                lastmm = (kk == npair - 1 and dl == T16 - 1)
                                K.op("pe", lambda e, rr_=rr_, q=q, kk=kk, dl=dl, pi_=pi_: e.matmul(pskd[pi_][32 * kk:32 * kk + 32, dl * 32:(dl + 1) * 32], lhsT=Bbr[:, q * 32:(q + 1) * 32], rhs=rr_, start=True, stop=False),
                                     reads=[r_Bb, r_V, r_cdb, r_tab], writes=[r_pskd[pi_]], signal=False)
                                K.op("pe", lambda e, ri_=ri_, q=q, kk=kk, dl=dl, pi_=pi_: e.matmul(pskd[pi_][32 * kk:32 * kk + 32, dl * 32:(dl + 1) * 32], lhsT=Bbi[:, q * 32:(q + 1) * 32], rhs=ri_, start=False, stop=True),
                                     reads=[r_Bb, r_V, r_cdb, r_tab], writes=[r_pskd[pi_]], signal=lastmm)
                        np_ = 32 * npair
                        evac(Kd[0:np_, sl, :, :].rearrange("p d c -> p (d c)"), pskd[pi_][0:np_, :], [r_pskd[pi_]], [r_Kd], eng="act")
                    K.op("dve", lambda e: e.tensor_tensor(out=Kd[:, :, 0, :], in0=Kd[:, :, 0, :], in1=Dmf[:].rearrange("p (s c) -> p s c", c=32), op=ALU.add), reads=[r_Kd, r_cd], writes=[r_Kd])
                    K.barrier()
                ptc.close()

                usp = [sb(p3, "usp%d" % i, [96, 16, 512], BF16) for i in range(2)]; r_usp = [Res() for _ in range(2)]
                uso = [sb(p3, "uso%d" % i, [96, 16, 128], BF16) for i in range(2)]; r_uso = [Res() for _ in range(2)]
                pz = [ps(p3, "pz%d" % i, [128, 512]) for i in range(2)]; r_pz = [Res() for _ in range(2)]
                pY = [[ps(p3, "pY%d_%d" % (b_, j_), [32, 512]) for j_ in range(1)] for b_ in range(4)]
                r_pY = [Res() for _ in range(4)]
                Z = sb(p3, "Z", [128, 2, 512]); r_Z = Res()
                Zd = sb(p3, "Zd", [128, 2, 512]); r_Zd = Res()
                Wc = sb(p3, "Wc", [128, 2, 512]); r_Wc = Res()
                ang, angf, r_ang = tmp["dve"]; angi = sb(p3, "angi", [128, 512], I32)
                Ec = sb(p3, "Ec", [128, 2, 512]); r_Ec = Res()
                rhoT = sb(p3, "rhoT", [128, 512]); r_rhoT = Res()
                Sown = [sb(p3, "Sown%d" % i, [128, 2, 128], BF16) for i in range(2)]; r_Sown = [Res() for _ in range(2)]
                ysb1 = sb(p3, "ysb", [32, NOWN], BF16); ysb = [ysb1, ysb1]; r_ysb1 = Res(); r_ysb = [r_ysb1, r_ysb1]

                def load_pair(q):
                    i = q % 2
                    kb = 32 * (q % 3)
                    K.dma("sp", usp[i][kb:kb + 32, :, :], usT[q * 32:(q + 1) * 32, :, :], writes=[r_usp[i]], dres=r_usp[i])
                    K.op("act", lambda e: e.activation(out=uso[i][kb:kb + 32, :, :].rearrange("p s (m k) -> p s m k", k=8),
                                                       in_=usp[i][kb:kb + 32, :, :].rearrange("p s (m r k) -> p s m r k", r=4, k=8)[:, :, :, 3, :], func=AF.Copy),
                         reads=[r_usp[i]], writes=[r_uso[i]])

                def twiddle(dst, r_dst, idx_ap, f_col, n):
                    K.op("dve", lambda e: e.tensor_scalar(out=ang[:, 0:n], in0=idx_ap, scalar1=f_col, scalar2=None, op0=ALU.mult), reads=[r_pc_, r_lr1, r_lr16], writes=[r_ang])
                    K.op("dve", lambda e: e.tensor_copy(out=angi[:, 0:n], in_=ang[:, 0:n]), reads=[r_ang], writes=[r_ang])
                    K.op("dve", lambda e: e.tensor_copy(out=angf[:, 0:n], in_=angi[:, 0:n]), reads=[r_ang], writes=[r_ang])
                    K.op("dve", lambda e: e.tensor_tensor(out=ang[:, 0:n], in0=ang[:, 0:n], in1=angf[:, 0:n], op=ALU.subtract), reads=[r_ang], writes=[r_ang])
                    K.op("act", lambda e: e.activation(out=dst[:, 1, 0:n], in_=ang[:, 0:n], func=AF.Sin, scale=-TWO_PI), reads=[r_ang], writes=[r_dst])
                    K.op("dve", lambda e: e.tensor_scalar(out=angf[:, 0:n], in0=ang[:, 0:n], scalar1=-1.0, scalar2=None, op0=ALU.mult), reads=[r_ang], writes=[r_ang])
                    K.op("dve", lambda e: e.tensor_tensor(out=angf[:, 0:n], in0=angf[:, 0:n], in1=ang[:, 0:n], op=ALU.max), reads=[r_ang], writes=[r_ang])
                    K.op("act", lambda e: e.activation(out=dst[:, 0, 0:n], in_=angf[:, 0:n], func=AF.Sin, scale=-TWO_PI, bias=halfpi[:]), reads=[r_ang], writes=[r_dst])

                def stage_P(q):
                    i = q % 2
                    kb = 32 * (q % 3)
                    sl = q // 3
                    cs_ = slice(sl * 128, (sl + 1) * 128)
                    for ri in range(2):
                        for tau in range(T16):
                            K.op("pe", lambda e, ri=ri, tau=tau: e.matmul(pz[ri][:], lhsT=Wa[kb:kb + 32, tau, ri, cs_], rhs=usp[i][kb:kb + 32, tau, :], start=(tau == 0), stop=(tau == T16 - 1)),
                                 reads=[r_usp[i], r_tab], writes=[r_pz[ri]], signal=(tau == T16 - 1))
                        evac(Z[:, ri, :], pz[ri][:], [r_pz[ri]], [r_Z], eng="act")
                    twiddle(Ec, r_Ec, cidx[:], f16r[:, q:q + 1], 512)
                    cmul("dve", Zd[:, 0, :], Zd[:, 1, :], Z[:, 0, :], Z[:, 1, :], Ec[:, 0, :], Ec[:, 1, :], False, [r_Z, r_Ec], [r_Zd], flat(512))
                    K.op("dve", lambda e: e.tensor_copy(out=rhoT[:], in_=rho16[:, q:q + 1].to_broadcast([128, 512])), reads=[r_tab], writes=[r_rhoT])
                    for ri in range(2):
                        K.op("dve", lambda e, ri=ri: e.tensor_tensor_scan(out=Wc[:, ri, :], data0=rhoT[:], data1=Zd[:, ri, :], initial=0.0, op0=ALU.mult, op1=ALU.add),
                             reads=[r_rhoT, r_Zd], writes=[r_Wc])
                    wv = Wc[:].rearrange("p r (m c) -> p r m c", c=32)
                    ev = Ec[:].rearrange("p r (m c) -> p r m c", c=32)
                    so = Sown[i][:].rearrange("p r (m k) -> p r m k", k=8)
                    mk = lambda t: t[:, 0:128].rearrange("p (m k) -> p m k", k=8)
                    cmul("dve", so[:, 0], so[:, 1], wv[:, 0, :, 23:31], wv[:, 1, :, 23:31], ev[:, 0, :, 23:31], ev[:, 1, :, 23:31], True, [r_Wc, r_Ec], [r_Sown[i]], mk)

                def stage_F(q):
                    i = q % 2
                    kb = 32 * (q % 3)
                    sl = q // 3
                    for tau in range(T16):
                        j_ = tau // 4
                        oap = pY[j_][0][:, (tau % 4) * 128:(tau % 4 + 1) * 128]
                        K.op("pe", lambda e, tau=tau, oap=oap: e.matmul(oap, lhsT=Vtab[:, q, tau, 0, :], rhs=Sown[i][:, 0, :], start=True, stop=False),
                             reads=[r_V, r_Sown[i]], writes=[r_pY[j_]], signal=False)
                        K.op("pe", lambda e, tau=tau, oap=oap: e.matmul(oap, lhsT=Vtab[:, q, tau, 1, :], rhs=Sown[i][:, 1, :], start=False, stop=False),
                             reads=[r_V, r_Sown[i]], writes=[r_pY[j_]], signal=False)
                        for sg_ in range(tau + 1):
                            K.op("pe", lambda e, tau=tau, sg_=sg_, oap=oap: e.matmul(oap, lhsT=Kd[kb:kb + 32, sl, tau - sg_, :], rhs=uso[i][kb:kb + 32, sg_, :], start=False, stop=(sg_ == tau)),
                                 reads=[r_Kd, r_uso[i]], writes=[r_pY[j_]], signal=(sg_ == tau and tau % 4 == 3))
                        if tau % 4 == 3:
                            yv = ysb[i][:].rearrange("p (c t) -> p c t", t=T16)
                            evac(yv[:, :, 4 * j_:4 * j_ + 4], pY[j_][0][:].rearrange("p (t c) -> p c t", t=4), [r_pY[j_]], [r_ysb[i]], eng="act")
                    K.dma("sp", yfm[32 * (q % 4):32 * (q % 4) + 32, q // 4, :], ysb[i][:], reads=[r_ysb[i]], dres=r_ysb[i])

                conv = []
                if stage >= 9 and not debug:
                    for name, src, rows, cols, c0 in (("Wglu", w_glu, 512, 2048, 0), ("Wfo", w_fox_o, 512, D, 0), ("Wg", w_in, D, 2048, 2056),
                                                      ("Wmx", w_mix, D, D, 0), ("Wmq", w_mem_q, D, 512, 0), ("Wmo", w_mem_o, 512, D, 0),
                                                      ("Wfi", w_ffn_in, D, 5632, 0), ("Wfo2", w_ffn_out, 2816, D, 0)):
                        wb[name] = dscr("wb_" + name, [rows, cols])
                        for k in range(rows // 128):
                            cc = 0
                            while cc < cols:
                                n = min(1024, cols - cc)
                                conv.append((src[k * 128:(k + 1) * 128, c0 + cc:c0 + cc + n], wb[name][k * 128:(k + 1) * 128, cc:cc + n], n))
                                cc += n
                cvb = [sb(p3, "cvb%d" % i_, [128, 1024], BF16) for i_ in range(2)]; r_cvb = [Res() for _ in range(2)]
                cstate = {"i": 0, "pend": []}

                def conv_step():
                    i_ = cstate["i"]
                    if i_ < len(conv):
                        src_ap, dst_ap, n = conv[i_]
                        si = wst_cnt[0] % NWST
                        wst_cnt[0] += 1
                        bi = i_ % 2
                        K.dma("sp", wst[si][:, 0:n], src_ap, writes=[r_wst[si]], dres=r_wst[si])
                        K.op("pool", lambda e: e.tensor_copy(out=cvb[bi][:, 0:n], in_=wst[si][:, 0:n]), reads=[r_wst[si]], writes=[r_cvb[bi]])
                        cstate["pend"].append((dst_ap, bi, n))
                        cstate["i"] += 1
                    while cstate["pend"] and (len(cstate["pend"]) > 1 or cstate["i"] >= len(conv)):
                        dst_ap, bi, n = cstate["pend"].pop(0)
                        K.dma("sp", dst_ap, cvb[bi][:, 0:n], reads=[r_cvb[bi]], dres=r_cvb[bi])

                load_pair(0)
                load_pair(1)
                stage_P(0)
                for q in range(16):
                    if q + 1 < 16:
                        stage_P(q + 1)
                    for _ in range(4):
                        conv_step()
                    stage_F(q)
                    if q + 2 < 16:
                        load_pair(q + 2)
                    for _ in range(4):
                        conv_step()
                while conv and (cstate["i"] < len(conv) or cstate["pend"]):
                    conv_step()
                K.barrier()
                for t in range(4):
                    K.op("act", lambda e, t=t: e.activation(out=yfm[:, t, :], in_=yfm[:, t, :], func=AF.Gelu_apprx_tanh), reads=[r_yfm], writes=[r_yfm])
                K.barrier()

            if debug:
                dbg["yfm"] = nc.dram_tensor("dbg_yfm", [128, 4, NOWN], BF16, kind="ExternalOutput").ap()
                K.dma("sp", dbg["yfm"][:, :, :], yfm[:], dres=Res())
            if stage <= 3:
                raise _Stop()
            attT = sb(es, "attT", [128, 4, NOWN], BF16)
            with contextlib.ExitStack() as p4:
                kth = [sb(p4, "kth%d" % i, [71, NT], BF16) for i in range(2)]; r_kth = [Res() for _ in range(2)]
                vah = [sb(p4, "vah%d" % i, [128, NB, 128], BF16) for i in range(2)]; r_vah = [Res() for _ in range(2)]
                qth = [sb(p4, "qth%d" % i, [71, NOWN], BF16) for i in range(2)]; r_qth = [Res() for _ in range(2)]
                pT = [sb(p4, "pT%d" % i, [128, 512], BF16) for i in range(5)]; r_pT = [Res() for _ in range(5)]
                osb = sb(p4, "osb", [128, 512]); r_osb = Res()
                rcp = sb(p4, "rcp", [128, 512]); r_rcp = Res()
                pss = [ps(p4, "pss%d" % i, [128, 512]) for i in range(5)]; r_pss = [Res() for _ in range(5)]
                pso = [ps(p4, "pso%d" % i, [128, 512]) for i in range(2)]; r_pso = [Res() for _ in range(2)]
                psb = ps(p4, "psb", [128, 512]); r_psb = Res()

                def load_head(h):
                    i = h % 2
                    K.dma("sp", kth[i][:], kT[h, :, :], writes=[r_kth[i]], dres=r_kth[i])
                    K.dma("sp", qth[i][:], qT[h, :, :], writes=[r_qth[i]], dres=r_qth[i])
                    K.dma("sp", vah[i][:], vA[h, :, :].rearrange("(b t) c -> t b c", t=128), writes=[r_vah[i]], dres=r_vah[i])

                NSB = 5
                LA = 4
                load_head(0)
                tiles = []
                for h in range(8):
                    for M in range(4):
                        written = [False] * 4
                        nkb = 16 * M + 16
                        for kb in range(nkb):
                            rel = kb - 16 * M - 3
                            i0 = 0 if rel <= 0 else (rel + 3) // 4
                            diag = (rel >= 0 and rel % 4 == 0)
                            st_flag = not written[i0]
                            assert all(written[x] == written[i0] for x in range(i0, 4))
                            for x in range(i0, 4):
                                written[x] = True
                            tiles.append(dict(h=h, M=M, kb=kb, c0=128 * i0, diag=diag, st=st_flag, last=(kb == nkb - 1), first=(kb == 0), g=h * 4 + M))

                def emit_qk(t, T):
                    h, M, kb, c0, diag = T["h"], T["M"], T["kb"], T["c0"], T["diag"]
                    i = h % 2
                    si = t % NSB
                    if not diag:
                        K.op("pe", lambda e: e.matmul(pss[si][:, c0:512], lhsT=kth[i][:, kb * 128:(kb + 1) * 128], rhs=qth[i][:, M * 512 + c0:(M + 1) * 512], start=True, stop=True),
                             reads=[r_kth[i], r_qth[i]], writes=[r_pss[si]], signal=True)
                    else:
                        if c0 + 128 < 512:
                            K.op("pe", lambda e: e.matmul(pss[si][:, c0 + 128:512], lhsT=kth[i][:, kb * 128:(kb + 1) * 128], rhs=qth[i][:, M * 512 + c0 + 128:(M + 1) * 512], start=True, stop=True),
                                 reads=[r_kth[i], r_qth[i]], writes=[r_pss[si]], signal=False)
                        K.op("pe", lambda e: e.matmul(pss[si][:, c0:c0 + 128], lhsT=kth[i][:, kb * 128:(kb + 1) * 128], rhs=qth[i][:, M * 512 + c0:M * 512 + c0 + 128], start=True, stop=False),
                             reads=[r_kth[i], r_qth[i]], writes=[r_pss[si]], signal=False)
                        K.op("pe", lambda e: e.matmul(pss[si][:, c0:c0 + 128], lhsT=identb[:], rhs=tmaskb[:], start=False, stop=True),
                             reads=[r_const], writes=[r_pss[si]], signal=True)
                    K.op("act", lambda e: e.activation(out=pT[si][:, c0:512], in_=pss[si][:, c0:512], func=AF.Exp), reads=[r_pss[si]], writes=[r_pT[si]])

                def emit_pv(t, T):
                    h, M, kb, c0 = T["h"], T["M"], T["kb"], T["c0"]
                    i = h % 2
                    si = t % NSB
                    oi = T["g"] % 2
                    odd = h % 2
                    mcols = 128 if odd else 65
                    st_flag, last = T["st"], T["last"]
                    if T["first"] and M == 0 and h + 1 < 8:
                        load_head(h + 1)
                    K.op("pe", lambda e: e.matmul(pso[oi][0:mcols, c0:512], lhsT=vah[i][:, kb, 0:mcols], rhs=pT[si][:, c0:512], start=st_flag, stop=last),
                         reads=[r_vah[i], r_pT[si]], writes=[r_pso[oi]], signal=last)
                    if last:
                        K.op("act", lambda e: e.activation(out=osb[0:mcols, :], in_=pso[oi][0:mcols, :], func=AF.Copy), reads=[r_pso[oi]], writes=[r_osb])
                        if odd:
                            K.op("pe", lambda e: e.matmul(psb[:], lhsT=self_[0:1, :], rhs=osb[0:1, :], start=True, stop=True), reads=[r_osb, r_const], writes=[r_psb])
                        else:
                            K.op("pe", lambda e: e.matmul(psb[0:64, :], lhsT=self_[64:65, 0:64], rhs=osb[64:65, :], start=True, stop=True), reads=[r_osb, r_const], writes=[r_psb])
                        lo = 64 if odd else 0
                        K.op("dve", lambda e: e.reciprocal(out=rcp[lo:lo + 64, :], in_=psb[lo:lo + 64, :]), reads=[r_psb], writes=[r_rcp])
                        K.op("dve", lambda e: e.tensor_tensor(out=attT[lo:lo + 64, h // 2, M * 512:(M + 1) * 512], in0=osb[lo:lo + 64, :], in1=rcp[lo:lo + 64, :], op=ALU.mult),
                             reads=[r_osb, r_rcp], writes=[r_att])

                nt = len(tiles)
                for t in range(nt + LA):
                    if t < nt:
                        emit_qk(t, tiles[t])
                    if t - LA >= 0:
                        emit_pv(t - LA, tiles[t - LA])
                K.barrier()

            if debug:
                dbg["att"] = nc.dram_tensor("dbg_att", [128, 4, NOWN], BF16, kind="ExternalOutput").ap()
                K.dma("sp", dbg["att"][:, :, :], attT[:], dres=Res())
            if stage <= 4:
                raise _Stop()
            kmT = sb(es, "kmT", [128, 4, 256], BF16)
            vmem = sb(es, "vmem", [128, 2, 512], BF16)
            r_km = Res("km")
            with contextlib.ExitStack() as pm:
                gkv = sb(pm, "gkv", [128, D]); r_gkv = Res()
                K.dma("sp", gkv[:], g_memkv[0:1, :].partition_broadcast(128), writes=[r_gkv], dres=r_gkv)
                Wkv = sb(pm, "Wkv", [128, 8, D], BF16); r_Wkv = Res()
                for k in range(8):
                    cast_load(lambda cc, n, k=k: Wkv[:, k, cc:cc + n], r_Wkv, w_mem_kv, k * 128, D, 0)
                mt = sb(pm, "mt", [128, 2, D]); r_mt = Res()
                K.dma("sp", mt[:], mem.rearrange("(b t) d -> t b d", t=128), writes=[r_mt], dres=r_mt)
                junk = sb(pm, "junkm", [128, D], BF16); r_junk = Res()
                ssm_ = sb(pm, "ssm_", [128, 1]); r_ssm = Res(); rsm = sb(pm, "rsm", [128, 1]); r_rsm = Res()
                mb = sb(pm, "mb", [128, D], BF16); r_mb = Res()
                mT = sb(pm, "mT", [128, 8, 256], BF16); r_mT = Res()
                ptm = ps(pm, "ptm", [128, 1024], BF16); r_ptm = Res()
                pkm = [ps(pm, "pkm%d" % i, [128, 512]) for i in range(2)]; r_pkm = [Res() for _ in range(2)]
                for b2 in range(2):
                    rmsnorm_tok("pm", mt[:, b2, :], r_mt, gkv[:], r_gkv, mb[:], r_mb, junk[:], r_junk, ssm_[:], r_ssm, rsm[:], r_rsm)
                    for k in range(8):
                        K.op("pe", lambda e, k=k: e.transpose(out=ptm[:, k * 128:(k + 1) * 128], in_=mb[:, k * 128:(k + 1) * 128], identity=identb[:]), reads=[r_mb, r_const], writes=[r_ptm], signal=(k == 7))
                    evac(mT[:, :, b2 * 128:(b2 + 1) * 128], ptm[:].rearrange("p (k t) -> p k t", k=8), [r_ptm], [r_mT])
                for hh in range(4):
                    pi = hh % 2
                    for k in range(8):
                        K.op("pe", lambda e, k=k, hh=hh, pi=pi: e.matmul(pkm[pi][:, 0:256], lhsT=Wkv[:, k, hh * 128:(hh + 1) * 128], rhs=mT[:, k, :], start=(k == 0), stop=(k == 7)),
                             reads=[r_Wkv, r_mT], writes=[r_pkm[pi]], signal=(k == 7))
                    evac(kmT[:, hh, :], pkm[pi][:, 0:256], [r_pkm[pi]], [r_km])
                for b2 in range(2):
                    pi = b2 % 2
                    for k in range(8):
                        K.op("pe", lambda e, k=k, b2=b2, pi=pi: e.matmul(pkm[pi][:], lhsT=mT[:, k, b2 * 128:(b2 + 1) * 128], rhs=Wkv[:, k, 512:1024], start=(k == 0), stop=(k == 7)),
                             reads=[r_Wkv, r_mT], writes=[r_pkm[pi]], signal=(k == 7))
                    evac(vmem[:, b2, :], pkm[pi][:], [r_pkm[pi]], [r_km])
                K.barrier()

            wcache = {}

            def load_w(st, name, src, rows, cols, c0=0):
                nk = rows // 128
                t = sb(st, name, [128, nk, cols], BF16)
                r = Res(name)
                if name in wb:
                    K.dma("sp", t[:], wb[name].rearrange("(k p) c -> p k c", p=128), writes=[r], dres=r)
                    return t, r
                if name in wcache:
                    K.dma("sp", t[:], wcache[name].rearrange("(k p) c -> p k c", p=128), writes=[r], dres=r)
                    return t, r
                for k in range(nk):
                    cast_load(lambda cc, n, k=k: t[:, k, cc:cc + n], r, src, k * 128, cols, c0)
                if not debug:
                    wcache[name] = dscr("wc_" + name, [rows, cols])
                    K.dma("act", wcache[name].rearrange("(k p) c -> p k c", p=128), t[:], reads=[r], dres=Res())
                return t, r

            NH = 1024
            for half in range(2):
                tok0 = half * NH
                with contextlib.ExitStack() as ph:
                    mixed = sb(ph, "mixed", [128, 8, NH], BF16); r_mixed = Res()
                    with contextlib.ExitStack() as pa_:
                        uTh = sb(pa_, "uTh", [128, 8, NH], BF16); r_uTh = Res()
                        for k in range(8):
                            K.dma("sp", uTh[:, k, :], uTo[k * 128:(k + 1) * 128, tok0:tok0 + NH], writes=[r_uTh] if k == 0 else [], dres=r_uTh)
                        r_uTh.w = (r_uTh.dsem, K.cnt[r_uTh.dsem])
                        Wglu, r_Wglu = load_w(pa_, "Wglu", w_glu, 512, 2048)
                        Wfo, r_Wfo = load_w(pa_, "Wfo", w_fox_o, 512, D)
                        Wg, r_Wg = load_w(pa_, "Wg", w_in, D, 2048, c0=2056)
                        pA = [ps(pa_, "pA%d" % i, [128, 512]) for i in range(2)]; r_pA = [Res() for _ in range(2)]
                        pB = [ps(pa_, "pB%d" % i, [128, 512]) for i in range(2)]; r_pB = [Res() for _ in range(2)]
                        pG = [ps(pa_, "pG%d" % i, [128, 512]) for i in range(2)]; r_pG = [Res() for _ in range(2)]
                        sg = [sb(pa_, "sg%d" % i, [128, 512]) for i in range(2)]; r_sg = [Res() for _ in range(2)]
                        oa = [sb(pa_, "oa%d" % i, [128, 512]) for i in range(2)]; r_oa = [Res() for _ in range(2)]
                        ob = [sb(pa_, "ob%d" % i, [128, 512]) for i in range(2)]; r_ob = [Res() for _ in range(2)]
                        cnt = 0
                        for ft in range(8):
                            for ch in range(2):
                                x = cnt % 2
                                cnt += 1
                                tsl = slice(tok0 + ch * 512, tok0 + (ch + 1) * 512)
                                lsl = slice(ch * 512, (ch + 1) * 512)
                                for k in range(4):
                                    K.op("pe", lambda e, k=k, x=x, ft=ft, tsl=tsl: e.matmul(pA[x][:], lhsT=Wglu[:, k, ft * 128:(ft + 1) * 128], rhs=yfm[:, k, tsl], start=(k == 0), stop=(k == 3)),
                                         reads=[r_Wglu, r_yfm], writes=[r_pA[x]], signal=(k == 3))
                                for k in range(4):
                                    K.op("pe", lambda e, k=k, x=x, ft=ft, tsl=tsl: e.matmul(pB[x][:], lhsT=Wglu[:, k, D + ft * 128:D + (ft + 1) * 128], rhs=yfm[:, k, tsl], start=(k == 0), stop=(k == 3)),
                                         reads=[r_Wglu, r_yfm], writes=[r_pB[x]], signal=(k == 3))
                                K.op("act", lambda e, x=x: e.activation(out=sg[x][:], in_=pB[x][:], func=AF.Sigmoid), reads=[r_pB[x]], writes=[r_sg[x]])
                                K.op("dve", lambda e, x=x: e.tensor_tensor(out=oa[x][:], in0=pA[x][:], in1=sg[x][:], op=ALU.mult), reads=[r_pA[x], r_sg[x]], writes=[r_oa[x]])
                                for k in range(8):
                                    K.op("pe", lambda e, k=k, x=x, ft=ft, lsl=lsl: e.matmul(pG[x][:], lhsT=Wg[:, k, ft * 128:(ft + 1) * 128], rhs=uTh[:, k, lsl], start=(k == 0), stop=(k == 7)),
                                         reads=[r_Wg, r_uTh], writes=[r_pG[x]], signal=(k == 7))
                                K.op("act", lambda e, x=x: e.activation(out=sg[x][:], in_=pG[x][:], func=AF.Sigmoid), reads=[r_pG[x], r_oa[x]], writes=[r_sg[x]])
                                K.op("pool", lambda e, x=x: e.tensor_tensor(out=oa[x][:], in0=oa[x][:], in1=sg[x][:], op=ALU.mult), reads=[r_oa[x], r_sg[x]], writes=[r_oa[x]])
                                for k in range(4):
                                    K.op("pe", lambda e, k=k, x=x, ft=ft, tsl=tsl: e.matmul(pA[x][:], lhsT=Wfo[:, k, ft * 128:(ft + 1) * 128], rhs=attT[:, k, tsl], start=(k == 0), stop=(k == 3)),
                                         reads=[r_Wfo, r_att], writes=[r_pA[x]], signal=(k == 3))
                                for k in range(8):
                                    K.op("pe", lambda e, k=k, x=x, ft=ft, lsl=lsl: e.matmul(pG[x][:], lhsT=Wg[:, k, D + ft * 128:D + (ft + 1) * 128], rhs=uTh[:, k, lsl], start=(k == 0), stop=(k == 7)),
                                         reads=[r_Wg, r_uTh], writes=[r_pG[x]], signal=(k == 7))
                                K.op("act", lambda e, x=x: e.activation(out=sg[x][:], in_=pG[x][:], func=AF.Sigmoid), reads=[r_pG[x], r_oa[x]], writes=[r_sg[x]])
                                K.op("dve", lambda e, x=x: e.tensor_tensor(out=ob[x][:], in0=pA[x][:], in1=sg[x][:], op=ALU.mult), reads=[r_pA[x], r_sg[x]], writes=[r_ob[x]])
                                K.op("pool", lambda e, x=x, ft=ft, lsl=lsl: e.tensor_tensor(out=mixed[:, ft, lsl], in0=oa[x][:], in1=ob[x][:], op=ALU.add), reads=[r_oa[x], r_ob[x]], writes=[r_mixed])
                        K.barrier()

                    hres = sb(ph, "hres", [128, 8, D]); r_h = [Res() for _ in range(8)]
                    for bb in range(8):
                        gb = 4 * (8 * half + bb) + 3
                        K.dma("sp", hres[:, bb, :], xp[gb * 128:(gb + 1) * 128, :], writes=[r_h[bb]], dres=r_h[bb])

                    def norm_T(st, tag, gain_src, nT, r_nT):
                        gt = sb(st, tag + "g", [128, D]); r_gt = Res()
                        K.dma("sp", gt[:], gain_src[0:1, :].partition_broadcast(128), writes=[r_gt], dres=r_gt)
                        junk = sb(st, tag + "junk", [128, D], BF16); r_junk = Res()
                        ss8 = sb(st, tag + "ss8", [128, 8]); r_ss8 = Res()
                        rs8 = sb(st, tag + "rs8", [128, 8]); r_rs8 = Res()
                        nb = [sb(st, tag + "nb%d" % i, [128, D], BF16) for i in range(2)]; r_nb = [Res() for _ in range(2)]
                        ptn = [ps(st, tag + "pt%d" % i, [128, 1024], BF16) for i in range(2)]; r_ptn = [Res() for _ in range(2)]
                        for bb in range(8):
                            K.op("act", lambda e, bb=bb: e.activation(out=junk[:], in_=hres[:, bb, :], func=AF.Square, accum_out=ss8[:, bb:bb + 1]), reads=[r_h[bb]], writes=[r_junk, r_ss8])
                        K.op("dve", lambda e: e.tensor_scalar(out=rs8[:], in0=ss8[:], scalar1=1.0 / D, scalar2=EPS, op0=ALU.mult, op1=ALU.add), reads=[r_ss8], writes=[r_rs8])
                        K.op("act", lambda e: e.activation(out=rs8[:], in_=rs8[:], func=AF.Sqrt), reads=[r_rs8], writes=[r_rs8])
                        K.op("dve", lambda e: e.reciprocal(out=rs8[:], in_=rs8[:]), reads=[r_rs8], writes=[r_rs8])
                        for bb in range(8):
                            x = bb % 2
                            K.op("dve", lambda e, bb=bb, x=x: e.scalar_tensor_tensor(out=nb[x][:], in0=hres[:, bb, :], scalar=rs8[:, bb:bb + 1], in1=gt[:], op0=ALU.mult, op1=ALU.mult),
                                 reads=[r_h[bb], r_rs8, r_gt], writes=[r_nb[x]])
                            for k in range(8):
                                K.op("pe", lambda e, k=k, x=x: e.transpose(out=ptn[x][:, k * 128:(k + 1) * 128], in_=nb[x][:, k * 128:(k + 1) * 128], identity=identb[:]),
                                     reads=[r_nb[x], r_const], writes=[r_ptn[x]], signal=(k == 7))
                            evac(nT[:, :, bb * 128:(bb + 1) * 128], ptn[x][:].rearrange("p (k t) -> p k t", k=8), [r_ptn[x]], [r_nT])

                    def proj_add(st, tag, actT, r_actT, nk, W, r_W):
                        pr = [ps(st, tag + "pr%d" % i, [128, 512]) for i in range(2)]; r_pr = [Res() for _ in range(2)]
                        c = 0
                        for bb in range(8):
                            for cc in range(2):
                                x = c % 2
                                c += 1
                                for k in range(nk):
                                    K.op("pe", lambda e, k=k, x=x, bb=bb, cc=cc: e.matmul(pr[x][:], lhsT=actT[:, k, bb * 128:(bb + 1) * 128], rhs=W[:, k, cc * 512:(cc + 1) * 512], start=(k == 0), stop=(k == nk - 1)),
                                         reads=[r_actT, r_W], writes=[r_pr[x]], signal=(k == nk - 1))
                                K.op("dve", lambda e, x=x, bb=bb, cc=cc: e.tensor_tensor(out=hres[:, bb, cc * 512:(cc + 1) * 512], in0=pr[x][:], in1=hres[:, bb, cc * 512:(cc + 1) * 512], op=ALU.add),
                                     reads=[r_pr[x], r_h[bb]], writes=[r_h[bb]])

                    with contextlib.ExitStack() as pb_:
                        Wmx, r_Wmx = load_w(pb_, "Wmx", w_mix, D, D)
                        proj_add(pb_, "mx", mixed, r_mixed, 8, Wmx, r_Wmx)
                        K.barrier()
                    with contextlib.ExitStack() as pc_:
                        nT = sb(pc_, "nT", [128, 8, NH], BF16); r_nT = Res()
                        Wmq, r_Wmq = load_w(pc_, "Wmq", w_mem_q, D, 512)
                        Wmo, r_Wmo = load_w(pc_, "Wmo", w_mem_o, 512, D)
                        with contextlib.ExitStack() as pn_:
                            norm_T(pn_, "n5", g_memq, nT, r_nT)
                            K.barrier()
                        qm = sb(pc_, "qm", [128, 4, NH], BF16); r_qm = Res()
                        om = sb(pc_, "om", [128, 4, NH], BF16); r_om = Res()
                        with contextlib.ExitStack() as pq_:
                            pq = [ps(pq_, "pq%d" % i, [128, 512]) for i in range(2)]; r_pq = [Res() for _ in range(2)]
                            pS = [ps(pq_, "pS%d" % i, [128, 512]) for i in range(2)]; r_pS = [Res() for _ in range(2)]
                            pO = [ps(pq_, "pO%d" % i, [128, 512]) for i in range(2)]; r_pO = [Res() for _ in range(2)]
                            pD = [ps(pq_, "pD%d" % i, [128, 512]) for i in range(2)]; r_pD = [Res() for _ in range(2)]
                            pe_ = [[sb(pq_, "pe%d_%d" % (i, j_), [128, 512], BF16) for j_ in range(2)] for i in range(2)]
                            r_pe = [[Res() for _ in range(2)] for _ in range(2)]
                            rc = [sb(pq_, "rc%d" % i, [128, 512]) for i in range(2)]; r_rc = [Res() for _ in range(2)]
                            r_qmi = [Res() for _ in range(8)]
                            its = [(hh, ch) for hh in range(4) for ch in range(2)]

                            def st_Q(it):
                                hh, ch = its[it]
                                x = it % 2
                                lsl = slice(ch * 512, (ch + 1) * 512)
                                for k in range(8):
                                    K.op("pe", lambda e, k=k: e.matmul(pq[x][:], lhsT=Wmq[:, k, hh * 128:(hh + 1) * 128], rhs=nT[:, k, lsl], start=(k == 0), stop=(k == 7)),
                                         reads=[r_Wmq, r_nT], writes=[r_pq[x]], signal=(k == 7))
                                evac(qm[:, hh, lsl], pq[x][:], [r_pq[x]], [r_qmi[it]], scale=1.0 / math.sqrt(128.0))

                            def st_S(it):
                                hh, ch = its[it]
                                x = it % 2
                                lsl = slice(ch * 512, (ch + 1) * 512)
                                for mt_ in range(2):
                                    K.op("pe", lambda e, mt_=mt_: e.matmul(pS[mt_][:], lhsT=kmT[:, hh, mt_ * 128:(mt_ + 1) * 128], rhs=qm[:, hh, lsl], start=True, stop=True),
                                         reads=[r_km, r_qmi[it]], writes=[r_pS[mt_]])
                                    K.op("act", lambda e, mt_=mt_: e.activation(out=pe_[x][mt_][:], in_=pS[mt_][:], func=AF.Exp), reads=[r_pS[mt_]], writes=[r_pe[x][mt_]])

                            def st_O(it):
                                hh, ch = its[it]
                                x = it % 2
                                lsl = slice(ch * 512, (ch + 1) * 512)
                                for mt_ in range(2):
                                    K.op("pe", lambda e, mt_=mt_: e.matmul(pO[x][:], lhsT=vmem[:, mt_, hh * 128:(hh + 1) * 128], rhs=pe_[x][mt_][:], start=(mt_ == 0), stop=(mt_ == 1)),
                                         reads=[r_km, r_pe[x][mt_]], writes=[r_pO[x]], signal=(mt_ == 1))
                                for mt_ in range(2):
                                    K.op("pe", lambda e, mt_=mt_: e.matmul(pD[x][:], lhsT=onesb[:], rhs=pe_[x][mt_][:], start=(mt_ == 0), stop=(mt_ == 1)),
                                         reads=[r_pe[x][mt_]], writes=[r_pD[x]], signal=(mt_ == 1))
                                K.op("dve", lambda e: e.reciprocal(out=rc[x][:], in_=pD[x][:]), reads=[r_pD[x]], writes=[r_rc[x]])
                                K.op("dve", lambda e: e.tensor_tensor(out=om[:, hh, lsl], in0=pO[x][:], in1=rc[x][:], op=ALU.mult), reads=[r_pO[x], r_rc[x]], writes=[r_om])

                            st_Q(0)
                            st_Q(1)
                            for it in range(8):
                                st_S(it)
                                if it + 2 < 8:
                                    st_Q(it + 2)
                                st_O(it)
                            K.barrier()
                        with contextlib.ExitStack() as po_:
                            proj_add(po_, "mo", om, r_om, 4, Wmo, r_Wmo)
                            K.barrier()
                    with contextlib.ExitStack() as pf_:
                        hid = sb(pf_, "hid", [128, 22, NH], BF16); r_hid = Res()
                        with contextlib.ExitStack() as pg_:
                            n2T = sb(pg_, "n2T", [128, 8, NH], BF16); r_n2T = Res()
                            with contextlib.ExitStack() as pn_:
                                norm_T(pn_, "n6", g_ffn, n2T, r_n2T)
                                K.barrier()
                            wfa = [sb(pg_, "wfa%d" % i, [128, 8, 2, 512], BF16) for i in range(2)]; r_wfa = [Res() for _ in range(2)]
                            pfa = [ps(pg_, "pfa%d" % i, [128, 512]) for i in range(2)]; r_pfa = [Res() for _ in range(2)]
                            pfb = [ps(pg_, "pfb%d" % i, [128, 512]) for i in range(2)]; r_pfb = [Res() for _ in range(2)]
                            sl_ = [sb(pg_, "sl%d" % i, [128, 512]) for i in range(2)]; r_sl = [Res() for _ in range(2)]
                            NG = 6

                            def load_ff(g):
                                w = g % 2
                                nt_ = min(4, 22 - 4 * g)
                                key = "ffin%d" % g
                                if "Wfi" in wb:
                                    wv_ = wb["Wfi"].rearrange("(k p) c -> p k c", p=128)
                                    for ab in range(2):
                                        K.dma("sp", wfa[w][:, :, ab, 0:nt_ * 128], wv_[:, :, ab * 2816 + g * 512:ab * 2816 + g * 512 + nt_ * 128], writes=[r_wfa[w]], dres=r_wfa[w])
                                    return
                                if key in wcache:
                                    K.dma("sp", wfa[w][:], wcache[key][:, :, :, :], writes=[r_wfa[w]], dres=r_wfa[w])
                                    return
                                for k in range(8):
                                    for ab in range(2):
                                        cast_load(lambda cc, n, k=k, ab=ab, w=w: wfa[w][:, k, ab, cc:cc + n], r_wfa[w], w_ffn_in, k * 128, nt_ * 128, ab * 2816 + g * 512)
                                if not debug:
                                    wcache[key] = dscr("wc_" + key, [128, 8, 2, 512])
                                    K.dma("act", wcache[key][:, :, :, :], wfa[w][:], reads=[r_wfa[w]], dres=Res())

                            load_ff(0)
                            c = 0
                            for g in range(NG):
                                w = g % 2
                                if g + 1 < NG:
                                    load_ff(g + 1)
                                for hi in range(min(4, 22 - 4 * g)):
                                    ht = 4 * g + hi
                                    for ch in range(2):
                                        x = c % 2
                                        c += 1
                                        lsl = slice(ch * 512, (ch + 1) * 512)
                                        for k in range(8):
                                            K.op("pe", lambda e, k=k, x=x, w=w, lsl=lsl, hi=hi: e.matmul(pfa[x][:], lhsT=wfa[w][:, k, 0, hi * 128:(hi + 1) * 128], rhs=n2T[:, k, lsl], start=(k == 0), stop=(k == 7)),
                                                 reads=[r_wfa[w], r_n2T], writes=[r_pfa[x]], signal=(k == 7))
                                        for k in range(8):
                                            K.op("pe", lambda e, k=k, x=x, w=w, lsl=lsl, hi=hi: e.matmul(pfb[x][:], lhsT=wfa[w][:, k, 1, hi * 128:(hi + 1) * 128], rhs=n2T[:, k, lsl], start=(k == 0), stop=(k == 7)),
                                                 reads=[r_wfa[w], r_n2T], writes=[r_pfb[x]], signal=(k == 7))
                                        K.op("act", lambda e, x=x: e.activation(out=sl_[x][:], in_=pfa[x][:], func=AF.Silu), reads=[r_pfa[x]], writes=[r_sl[x]])
                                        K.op("dve", lambda e, x=x, ht=ht, lsl=lsl: e.tensor_tensor(out=hid[:, ht, lsl], in0=pfb[x][:], in1=sl_[x][:], op=ALU.mult), reads=[r_pfb[x], r_sl[x]], writes=[r_hid])
                            K.barrier()
                        with contextlib.ExitStack() as po_:
                            Wfo2, r_Wfo2 = load_w(po_, "Wfo2", w_ffn_out, 2816, D)
                            proj_add(po_, "fo", hid, r_hid, 22, Wfo2, r_Wfo2)
                            K.barrier()
                    with contextlib.ExitStack() as pz_:
                        gt = sb(pz_, "gfin", [128, D]); r_gt = Res()
                        K.dma("sp", gt[:], g_fin[0:1, :].partition_broadcast(128), writes=[r_gt], dres=r_gt)
                        junk = sb(pz_, "junkz", [128, D], BF16); r_junk = Res()
                        ssz = [sb(pz_, "ssz%d" % i, [128, 1]) for i in range(2)]; r_ssz = [Res() for _ in range(2)]
                        rsz = [sb(pz_, "rsz%d" % i, [128, 1]) for i in range(2)]; r_rsz = [Res() for _ in range(2)]
                        ot = [sb(pz_, "ot%d" % i, [128, D]) for i in range(2)]; r_ot = [Res() for _ in range(2)]
                        r_out = Res("out")
                        for bb in range(8):
                            x = bb % 2
                            rmsnorm_tok("fz", hres[:, bb, :], r_h[bb], gt[:], r_gt, ot[x][:], r_ot[x], junk[:], r_junk, ssz[x][:], r_ssz[x], rsz[x][:], r_rsz[x])
                            m = 8 * half + bb
                            K.dma("sp", out[m * 128:(m + 1) * 128, :], ot[x][:], reads=[r_ot[x]], dres=r_ot[x])
                        K.barrier()
        except _Stop:
            pass
        K.barrier()
    return nc, K.rec


_CACHE = {}


def _consts():
    c = {}
    c["c_idx"] = np.tile(np.arange(512, dtype=np.float32)[None, :], (128, 1))
    c["c_tau"] = np.tile(np.arange(1, 129, dtype=np.float32)[None, :], (128, 1))
    rs = np.ones((128, 512), np.float32)
    rs[:, 0::128] = 0.0
    c["c_reset"] = rs
    c["c_ident"] = np.eye(128, dtype=np.float32)
    s = np.arange(128)
    c["c_triu"] = (s[:, None] <= s[None, :]).astype(np.float32)
    e = np.zeros((128, 128), np.float32)
    e[127, :] = 1.0
    c["c_e127"] = e
    c["c_tmask"] = np.where(s[:, None] <= s[None, :], 0.0, NEG).astype(np.float32)
    sel = np.zeros((128, 128), np.float32)
    sel[0, :] = 1.0
    sel[64, 0:64] = 1.0
    c["c_sel"] = sel
    return c


def _prep(x, mem, norm_mix, w_in, b_forget, lam_re, lam_im, log_dt, b_re, b_im, c_re, c_im,
          d_skip, w_glu, w_fox_o, w_mix_out, norm_mem_q, norm_mem_kv, w_mem_q, w_mem_kv,
          w_mem_o, norm_ffn, w_ffn_in, w_ffn_out, norm_final):
    f = lambda a: np.ascontiguousarray(np.asarray(a, dtype=np.float32))
    x = f(x); mem = f(mem)
    lam_re = f(lam_re)[0]; lam_im = f(lam_im)[0]; log_dt = f(log_dt)[0]
    b_re = f(b_re)[0]; b_im = f(b_im)[0]; c_re = f(c_re)[0]; c_im = f(c_im)[0]; d_skip = f(d_skip)[0]
    lamr_c = np.full((96, 6, 128), -1.0, np.float32); lami_c = np.zeros((96, 6, 128), np.float32)
    ldt_c = np.zeros((96, 6, 128), np.float32)
    Br_c = np.zeros((96, 6, 128), np.float32); Bi_c = np.zeros((96, 6, 128), np.float32)
    Dm = np.zeros((96, 6, 32), np.float32)
    lamr_r = np.zeros((128, 16), np.float32); lami_r = np.zeros((128, 16), np.float32); ldt_r = np.zeros((128, 16), np.float32)
    Cr_r = np.zeros((128, 16, 32), np.float32); Ci_r = np.zeros((128, 16, 32), np.float32)
    Brow_r = np.zeros((128, 16, 32), np.float32); Brow_i = np.zeros((128, 16, 32), np.float32)
    for q in range(16):
        sl, kk = q // 3, q % 3
        for g2 in range(2):
            g = 2 * q + g2
            lamr_c[32 * kk:32 * kk + 32, sl, 64 * g2:64 * g2 + 64] = lam_re[g][None, :]
            lami_c[32 * kk:32 * kk + 32, sl, 64 * g2:64 * g2 + 64] = lam_im[g][None, :]
            ldt_c[32 * kk:32 * kk + 32, sl, 64 * g2:64 * g2 + 64] = log_dt[g]
            Br_c[32 * kk + 16 * g2:32 * kk + 16 * g2 + 16, sl, 64 * g2:64 * g2 + 64] = b_re[g].T
            Bi_c[32 * kk + 16 * g2:32 * kk + 16 * g2 + 16, sl, 64 * g2:64 * g2 + 64] = b_im[g].T
            for m in range(16):
                Dm[32 * kk + 16 * g2 + m, sl, 16 * g2 + m] = d_skip[16 * g + m]
            lamr_r[64 * g2:64 * g2 + 64, q] = lam_re[g]
            lami_r[64 * g2:64 * g2 + 64, q] = lam_im[g]
            ldt_r[64 * g2:64 * g2 + 64, q] = log_dt[g]
            Cr_r[64 * g2:64 * g2 + 64, q, 16 * g2:16 * g2 + 16] = c_re[g].T
            Ci_r[64 * g2:64 * g2 + 64, q, 16 * g2:16 * g2 + 16] = c_im[g].T
            Brow_r[64 * g2:64 * g2 + 64, q, 16 * g2:16 * g2 + 16] = b_re[g]
            Brow_i[64 * g2:64 * g2 + 64, q, 16 * g2:16 * g2 + 16] = b_im[g]
    shared = dict(
        w_in=f(w_in)[0], g_mix=f(norm_mix), g_memq=f(norm_mem_q), g_memkv=f(norm_mem_kv), g_ffn=f(norm_ffn),
        g_fin=f(norm_final).reshape(1, D), b_forget=f(b_forget),
        lamr_c=lamr_c.reshape(96, 768), lami_c=lami_c.reshape(96, 768), ldt_c=ldt_c.reshape(96, 768),
        Br_c=Br_c.reshape(96, 768), Bi_c=Bi_c.reshape(96, 768),
        lamr_r=lamr_r, lami_r=lami_r, ldt_r=ldt_r, Cr_r=Cr_r.reshape(128, 512), Ci_r=Ci_r.reshape(128, 512),
        Dm=Dm.reshape(96, 192), Brow_r=Brow_r.reshape(128, 512), Brow_i=Brow_i.reshape(128, 512),
        w_glu=f(w_glu)[0], w_fox_o=f(w_fox_o)[0], w_mix=f(w_mix_out)[0], w_mem_q=f(w_mem_q)[0], w_mem_kv=f(w_mem_kv)[0],
        w_mem_o=f(w_mem_o)[0], w_ffn_in=f(w_ffn_in)[0], w_ffn_out=f(w_ffn_out)[0],
    )
    shared.update(_consts())
    in_maps = []
    for c in range(8):
        b, j = c // 4, c % 4
        npad = (3 - j) * 128
        xpad = np.zeros((NT, D), np.float32)
        xpad[npad:] = x[b, :NT - npad]
        pr = np.zeros((1, NT), np.float32)
        pr[0, :npad] = NEG
        m = dict(shared)
        m["xp"] = xpad
        m["padrow"] = pr
        m["mem"] = mem[b]
        in_maps.append(m)
    return in_maps


def kernel(**inputs):
    in_maps = _prep(**inputs)
    if "nc" not in _CACHE:
        _CACHE["nc"] = build()
    res = run_bass_kernel_spmd(_CACHE["nc"], in_maps, core_ids=list(range(8)))
    outp = np.zeros((2, 8192, D), np.float32)
    for c in range(8):
        b, j = c // 4, c % 4
        o = np.asarray(res.results[c]["out"]).reshape(16, 128, D)
        for m in range(16):
            gblk = 4 * m + j
            outp[b, gblk * 128:(gblk + 1) * 128, :] = o[m]
    return outp
```
